# Optimizing a Trainium2 kernel written in Bass

```python
import math
import jax, jax.numpy as jnp
from jax import lax
import numpy as np


D_MODEL = 1024
BATCH = 2
SEQ = 8192
DEPTH = 1

PLE_DIM = 256
ATTN_HEADS = 8
HEAD_DIM = 64
ATTN_WIDTH = ATTN_HEADS * HEAD_DIM
SSM_WIDTH = D_MODEL - ATTN_WIDTH
SSM_GROUP_CH = 16
SSM_GROUPS = SSM_WIDTH // SSM_GROUP_CH
SSM_STATE = 64
MIX_WIDTH = ATTN_WIDTH + SSM_WIDTH
IN_COLS = 3 * ATTN_WIDTH + ATTN_HEADS + SSM_WIDTH
D_FF = 2816
Q_BLOCK = 128
EPS = 1e-6

kernel_name = 'hybrid_fox_s5_macaron_ple'


def rmsnorm(x, g):
    xf = x.astype(jnp.float32)
    y = xf * lax.rsqrt(jnp.mean(xf * xf, axis=-1, keepdims=True) + EPS)
    return (y * g.astype(jnp.float32)).astype(x.dtype)


def swiglu(x, w1, w3, w2):
    return (jax.nn.silu(x @ w1) * (x @ w3)) @ w2


def forgetting_attention(q, k, v, log_f):
    B, L, H, hd = q.shape
    scale = 1.0 / math.sqrt(hd)
    q = q.transpose(0, 2, 1, 3)
    k = k.transpose(0, 2, 1, 3)
    v = v.transpose(0, 2, 1, 3)
    c = jnp.cumsum(log_f, axis=1).transpose(0, 2, 1)
    kpos = jnp.arange(L)
    n_blocks = L // Q_BLOCK

    def block(i):
        s0 = i * Q_BLOCK
        qb = lax.dynamic_slice_in_dim(q, s0, Q_BLOCK, axis=2)
        cb = lax.dynamic_slice_in_dim(c, s0, Q_BLOCK, axis=2)
        logits = (jnp.einsum('bhqd,bhkd->bhqk', qb, k).astype(jnp.float32) * scale
                  + cb[..., :, None] - c[..., None, :])
        qpos = s0 + jnp.arange(Q_BLOCK)
        mask = kpos[None, :] <= qpos[:, None]
        w = jax.nn.softmax(jnp.where(mask, logits, -jnp.inf), axis=-1)
        return jnp.einsum('bhqk,bhkd->bhqd', w.astype(v.dtype), v)

    out = lax.map(block, jnp.arange(n_blocks))
    return out.transpose(1, 0, 3, 2, 4).reshape(B, L, H * hd)


def _ssm_combine(e1, e2):
    a1r, a1i, b1r, b1i = e1
    a2r, a2i, b2r, b2i = e2
    ar = a2r * a1r - a2i * a1i
    ai = a2r * a1i + a2i * a1r
    br = a2r * b1r - a2i * b1i + b2r
    bi = a2r * b1i + a2i * b1r + b2i
    return ar, ai, br, bi


def s5_mixer(s, a_re, a_im, log_dt, b_re, b_im, c_re, c_im, d_skip, w_glu, b_glu):
    B, L, _ = s.shape
    f32 = jnp.float32
    u = s.astype(f32).reshape(B, L, SSM_GROUPS, SSM_GROUP_CH)
    ar, ai = a_re.astype(f32), a_im.astype(f32)
    dt = jnp.exp(log_dt.astype(f32))[:, None]
    decay = jnp.exp(dt * ar)
    abar_r = decay * jnp.cos(dt * ai)
    abar_i = decay * jnp.sin(dt * ai)
    nr, ni = abar_r - 1.0, abar_i
    den = ar * ar + ai * ai
    fr = (nr * ar + ni * ai) / den
    fi = (ni * ar - nr * ai) / den
    br, bi = b_re.astype(f32), b_im.astype(f32)
    bbar_r = fr[..., None] * br - fi[..., None] * bi
    bbar_i = fr[..., None] * bi + fi[..., None] * br
    bu_r = jnp.einsum('blgh,gph->blgp', u, bbar_r)
    bu_i = jnp.einsum('blgh,gph->blgp', u, bbar_i)
    a_r_full = jnp.broadcast_to(abar_r, bu_r.shape)
    a_i_full = jnp.broadcast_to(abar_i, bu_i.shape)
    _, _, xr, xi = lax.associative_scan(_ssm_combine, (a_r_full, a_i_full, bu_r, bu_i), axis=1)
    y = (jnp.einsum('blgp,ghp->blgh', xr, c_re.astype(f32))
         - jnp.einsum('blgp,ghp->blgh', xi, c_im.astype(f32))
         + d_skip.astype(f32) * u)
    y = jax.nn.gelu(y.reshape(B, L, SSM_WIDTH)).astype(s.dtype)
    return y * jax.nn.sigmoid(y @ w_glu + b_glu)


def setup_inputs(seed: int = 0) -> dict:
    key = jax.random.key(seed)
    ks = iter(jax.random.split(key, 40))
    nrm = lambda shape, scale: jax.random.normal(next(ks), shape, jnp.float32) * scale
    gain = lambda shape: 1.0 + nrm(shape, 0.05)
    Dp, D, F = DEPTH, D_MODEL, D_FF
    G, P, Hc = SSM_GROUPS, SSM_STATE, SSM_GROUP_CH
    inp = {}
    inp['x'] = nrm((BATCH, SEQ, D), 1.0)
    inp['p'] = nrm((DEPTH, BATCH, SEQ, PLE_DIM), 1.0)
    inp['g_ffn1'] = gain((Dp, D))
    inp['w1_a'] = nrm((Dp, D, F), D ** -0.5)
    inp['w3_a'] = nrm((Dp, D, F), D ** -0.5)
    inp['w2_a'] = nrm((Dp, F, D), F ** -0.5)
    inp['g_mix'] = gain((Dp, D))
    inp['w_in'] = nrm((Dp, D, IN_COLS), D ** -0.5)
    inp['b_f'] = jnp.linspace(1.0, 5.0, ATTN_HEADS)[None, :] + nrm((Dp, ATTN_HEADS), 0.1)
    inp['a_re'] = -0.5 + nrm((Dp, G, P), 0.01)
    inp['a_im'] = jnp.pi * jnp.arange(P, dtype=jnp.float32)[None, None, :] + nrm((Dp, G, P), 0.01)
    inp['log_dt'] = jax.random.uniform(next(ks), (Dp, G), jnp.float32, math.log(1e-3), math.log(1e-1))
    inp['b_re'] = nrm((Dp, G, P, Hc), (2.0 * Hc) ** -0.5)
    inp['b_im'] = nrm((Dp, G, P, Hc), (2.0 * Hc) ** -0.5)
    inp['c_re'] = nrm((Dp, G, Hc, P), (2.0 * P) ** -0.5)
    inp['c_im'] = nrm((Dp, G, Hc, P), (2.0 * P) ** -0.5)
    inp['d_skip'] = nrm((Dp, G, Hc), 1.0)
    inp['w_glu'] = nrm((Dp, SSM_WIDTH, SSM_WIDTH), SSM_WIDTH ** -0.5)
    inp['b_glu'] = nrm((Dp, SSM_WIDTH), 0.02)
    inp['g_attn_out'] = gain((Dp, ATTN_WIDTH))
    inp['g_ssm_out'] = gain((Dp, SSM_WIDTH))
    inp['w_out'] = nrm((Dp, MIX_WIDTH, D), MIX_WIDTH ** -0.5)
    inp['g_ffn2'] = gain((Dp, D))
    inp['w1_b'] = nrm((Dp, D, F), D ** -0.5)
    inp['w3_b'] = nrm((Dp, D, F), D ** -0.5)
    inp['w2_b'] = nrm((Dp, F, D), F ** -0.5)
    inp['g_ple'] = gain((Dp, D))
    inp['w_ple_gate'] = nrm((Dp, D, D), D ** -0.5)
    inp['w_ple_proj'] = nrm((Dp, PLE_DIM, D), PLE_DIM ** -0.5)
    inp['g_final'] = gain((D,))
    return inp


def reference(x, p, g_ffn1, w1_a, w3_a, w2_a, g_mix, w_in, b_f, a_re, a_im, log_dt,
              b_re, b_im, c_re, c_im, d_skip, w_glu, b_glu, g_attn_out, g_ssm_out, w_out,
              g_ffn2, w1_b, w3_b, w2_b, g_ple, w_ple_gate, w_ple_proj, g_final):
    B, L, _ = x.shape
    h = x
    s_q, s_k, s_v, s_f = ATTN_WIDTH, 2 * ATTN_WIDTH, 3 * ATTN_WIDTH, 3 * ATTN_WIDTH + ATTN_HEADS
    for i in range(DEPTH):
        h = h + 0.5 * swiglu(rmsnorm(h, g_ffn1[i]), w1_a[i], w3_a[i], w2_a[i])
        u = rmsnorm(h, g_mix[i])
        z = u @ w_in[i]
        q = z[..., :s_q].reshape(B, L, ATTN_HEADS, HEAD_DIM)
        k = z[..., s_q:s_k].reshape(B, L, ATTN_HEADS, HEAD_DIM)
        v = z[..., s_k:s_v].reshape(B, L, ATTN_HEADS, HEAD_DIM)
        log_f = jax.nn.log_sigmoid(z[..., s_v:s_f].astype(jnp.float32) + b_f[i].astype(jnp.float32))
        s_in = z[..., s_f:]
        attn = forgetting_attention(q, k, v, log_f)
        ssm = s5_mixer(s_in, a_re[i], a_im[i], log_dt[i], b_re[i], b_im[i], c_re[i], c_im[i],
                       d_skip[i], w_glu[i], b_glu[i])
        mixed = jnp.concatenate([rmsnorm(attn, g_attn_out[i]), rmsnorm(ssm, g_ssm_out[i])], axis=-1)
        h = h + mixed @ w_out[i]
        h = h + 0.5 * swiglu(rmsnorm(h, g_ffn2[i]), w1_b[i], w3_b[i], w2_b[i])
        gate = jax.nn.sigmoid(rmsnorm(h, g_ple[i]) @ w_ple_gate[i])
        h = h + gate * (p[i] @ w_ple_proj[i])
    return rmsnorm(h, g_final)
```

```python
import numpy as np
import concourse.bass as bass
import concourse.mybir as mybir
from concourse.bass_utils import run_bass_kernel_spmd
from contextlib import ExitStack

F32 = mybir.dt.float32
BF16 = mybir.dt.bfloat16
AF = mybir.ActivationFunctionType
ALU = mybir.AluOpType
EPS = 1e-6
GROUPS = [[0, 1, 2, 3], [4, 5, 6, 7]]


class Tok:
    __slots__ = ("w", "r")

    def __init__(self):
        self.w = None
        self.r = []


class KB:
    ENG = ["pe", "act", "dve", "pool", "sp"]

    def __init__(self, nc, stack, n_dma_sems=16):
        self.nc = nc
        self.prog = {e: [] for e in self.ENG}
        self.sems = {e: stack.enter_context(nc.semaphore("s_" + e)) for e in self.ENG}
        self.cnt = {e: 0 for e in self.ENG}
        self.dsems = [stack.enter_context(nc.semaphore("dq%d" % i)) for i in range(n_dma_sems)]
        self.dcnt = [0] * n_dma_sems
        self.dnext = 0
        self.dnext_sw = 0
        self.csem = stack.enter_context(nc.semaphore("cc"))
        self.ccnt = 0
        self.waited = {e: {} for e in self.ENG}
        self.stack = stack
        self.nblk = 0

    def _sem(self, k):
        if k[0] == "e":
            return self.sems[k[1]]
        if k[0] == "c":
            return self.csem
        return self.dsems[k[1]]

    def _deps(self, reads, writes):
        deps = {}

        def add(tok):
            if tok is None:
                return
            k, v = tok
            if deps.get(k, 0) < v:
                deps[k] = v

        for b in reads:
            add(b.w)
        for b in writes:
            add(b.w)
            for t in b.r:
                add(t)
        return deps

    def _emit_waits(self, eng, deps, skip_self):
        for k, v in deps.items():
            if skip_self and k == ("e", eng):
                continue
            if self.waited[eng].get(k, 0) >= v:
                continue
            self.waited[eng][k] = v
            sem = self._sem(k)
            self.prog[eng].append(lambda e, sem=sem, v=v: e.wait_ge(sem, v))

    def _update(self, tok, reads, writes):
        for b in writes:
            b.w = tok
            b.r = []
        for b in reads:
            if b not in writes:
                b.r.append(tok)
                if len(b.r) > 64:
                    b.r = b.r[-64:]

    def op(self, eng, fn, reads=(), writes=()):
        deps = self._deps(reads, writes)
        self._emit_waits(eng, deps, skip_self=(eng == "pe"))
        self.cnt[eng] += 1
        tok = (("e", eng), self.cnt[eng])
        sem = self.sems[eng]
        self.prog[eng].append(lambda e, fn=fn, sem=sem: fn(e).then_inc(sem, 1))
        self._update(tok, reads, writes)
        return tok

    def dma(self, eng, fn, n, reads=(), writes=()):
        deps = self._deps(reads, writes)
        half = len(self.dsems) // 2
        if eng == "pool":
            i = half + self.dnext_sw
            self.dnext_sw = (self.dnext_sw + 1) % (len(self.dsems) - half)
        else:
            i = self.dnext
            self.dnext = (self.dnext + 1) % half
        k = ("d", i)
        if self.dcnt[i] > 0:
            deps[k] = max(deps.get(k, 0), self.dcnt[i])
        self._emit_waits(eng, deps, skip_self=True)
        self.dcnt[i] += 16 * n
        tok = (k, self.dcnt[i])
        sem = self.dsems[i]
        self.prog[eng].append(lambda e, fn=fn, sem=sem: fn(e, sem))
        self._update(tok, reads, writes)
        return tok

    def collective(self, fn, reads=(), writes=()):
        deps = self._deps(reads, writes)
        self._emit_waits("pool", deps, skip_self=False)
        self.ccnt += 1
        tok = (("c", 0), self.ccnt)
        sem = self.csem
        self.prog["pool"].append(lambda e, fn=fn, sem=sem: fn(e).then_inc(sem, 1))
        self._update(tok, reads, writes)
        return tok

    def wait_all(self, eng, toks):
        deps = {}
        for t in toks:
            if t is None:
                continue
            k, v = t
            if deps.get(k, 0) < v:
                deps[k] = v
        self._emit_waits(eng, deps, skip_self=False)

    def flush(self):
        allt = [(("e", e), self.cnt[e]) for e in self.ENG if self.cnt[e] > 0]
        allt += [(("d", i), self.dcnt[i]) for i in range(len(self.dsems)) if self.dcnt[i] > 0]
        for e in self.ENG:
            self.wait_all(e, allt)
        nc = self.nc
        prog = self.prog
        self.nblk += 1
        with nc.Block(no_gpsimd_drain=True) as block:

            @block.tensor
            def _(e):
                for f in prog["pe"]:
                    f(e)

            @block.scalar
            def _(e):
                for f in prog["act"]:
                    f(e)

            @block.vector
            def _(e):
                for f in prog["dve"]:
                    f(e)

            @block.gpsimd
            def _(e):
                for f in prog["pool"]:
                    f(e)

            @block.sync
            def _(e):
                for f in prog["sp"]:
                    f(e)

        self.prog = {e: [] for e in self.ENG}


class Ctx:
    pass


class Deferred:
    def __init__(self):
        self.q = []

    def op(self, *a, **k):
        self.q.append(("op", a, k))

    def dma(self, *a, **k):
        self.q.append(("dma", a, k))

    def call(self, fn):
        self.q.append(("call", (fn,), {}))

    def replay(self, kb, upto):
        while self.pos < min(upto, len(self.q)):
            m, a, k = self.q[self.pos]
            self.pos += 1
            if m == "call":
                a[0]()
            else:
                getattr(kb, m)(*a, **k)

    pos = 0


def _mm_group(ps_ap, pairs):
    def f(e):
        n = len(pairs)
        for i, (l, r) in enumerate(pairs):
            ins = e.matmul(ps_ap, lhsT=l, rhs=r, start=(i == 0), stop=(i == n - 1))
        return ins

    return f


def build(mode="full"):
    nc = bass.Bass("TRN2", target_bir_lowering=False)

    def din(name, shape, dt=F32):
        return nc.dram_tensor(name, list(shape), dt, kind="ExternalInput").ap()

    def dout(name, shape, dt=F32):
        return nc.dram_tensor(name, list(shape), dt, kind="ExternalOutput").ap()

    def dint(name, shape, dt=F32):
        return nc.dram_tensor(name, list(shape), dt).ap()

    do1 = mode in ("full", "s1")
    do2 = mode in ("full", "s2")
    do3 = mode in ("full", "s3")
    cst_d = din("cst", [128, 4, 128])
    gcols_d = din("gcols", [128, 6, 8])
    if do1:
        x = din("x", [2048, 1024])
        w1a = din("w1_a", [1024, 2816]); w3a = din("w3_a", [1024, 2816]); w2a = din("w2_a", [2816, 1024])
    if do2:
        win_d = din("win", [1024, 520])
        mask_d = din("maskc", [128, 640])
        bfbc_d = din("bfbc", [128, 2])
        colp_d = din("colp", [128, 3, 4])
        rowp_d = din("rowp", [3, 512])
        braw_d = din("braw", [128, 2, 512])
        craw_d = din("craw", [128, 2, 512])
        dcol_d = din("dcol", [128, 1])
    if do3:
        idx_d = din("idxT", [128, 16], mybir.dt.uint32)
        wglu_d = din("w_glu", [512, 512]); bglu_d = din("bglu", [128, 4])
        wout_d = din("w_out", [1024, 1024])
        w1b = din("w1_b", [1024, 2816]); w3b = din("w3_b", [1024, 2816]); w2b_d = din("w2_b", [2816, 1024])
        wgate_d = din("w_gate", [1024, 1024]); wproj_d = din("w_proj", [256, 1024])
        p_d = din("p", [2048, 256])
        out_d = dout("out", [2048, 1024])
    if mode == "s1":
        dbg_h = dout("dbg_h", [1024, 2048])
        uT_loc = [dout("dbg_u%d" % t, [1024, 512], BF16) for t in range(4)]
    elif mode == "full":
        uT_loc = [dint("uT_loc%d" % t, [1024, 512], BF16) for t in range(4)]
    if mode == "s2":
        uT_all = [din("uT_all%d" % t, [4096, 512], BF16) for t in range(4)]
        mix_loc = [dout("dbg_mix%d" % k, [256, 1024]) for k in range(8)]
    elif mode == "full":
        uT_all = [dint("uT_all%d" % t, [4096, 512], BF16) for t in range(4)]
        mix_loc = [dint("mix_loc%d" % k, [256, 1024]) for k in range(8)]
    if mode == "s3":
        h1T_d = din("h1T", [1024, 2048])
        mix_big = din("mix_all", [8192, 1024])
    elif mode == "full":
        mix_big = dint("mix_all_big", [8192, 1024])
    if do3:
        mix_all = [mix_big[k * 1024:(k + 1) * 1024, :] for k in range(8)]
    t_uloc = [Tok() for _ in range(4)]; t_uall = [Tok() for _ in range(4)]
    t_mloc = [Tok() for _ in range(8)]; t_mall = [Tok() for _ in range(8)]

    with ExitStack() as top:
        kb = KB(nc, top)

        _uid = [0]

        def sb(st, name, shape, dt):
            _uid[0] += 1
            return st.enter_context(nc.sbuf_tensor("%s_u%d" % (name, _uid[0]), list(shape), dt))

        ps = top.enter_context(nc.psum_tensor("ps", [128, 8, 512], F32))
        PB = [Tok() for _ in range(8)]
        t_h = [[Tok() for _ in range(4)] for _ in range(8)]
        cst = sb(top, "cst_s", [128, 4, 128], F32)
        ident = cst[:, 0, :]
        triF = cst[:, 1, :]
        onesF = cst[:, 2, :]
        selF = cst[:, 3, 0:64]
        ones_bf = sb(top, "ones_bf", [128, 128], BF16)
        gcols = sb(top, "gcols_s", [128, 6, 8], F32)
        epsc = sb(top, "epsc", [128, 1], F32)
        t_c = Tok()
        hstack = ExitStack()
        hT = sb(hstack, "hT", [128, 8, 2048], F32)
        if mode == "full":
            hT_park = dint("hT_park", [1024, 2048])
        t_park = Tok()

        def ldc(e, sem):
            e.dma_start(out=cst[:], in_=cst_d).then_inc(sem, 16)
            e.dma_start(out=gcols[:], in_=gcols_d).then_inc(sem, 16)

        kb.dma("sp", ldc, 2, writes=[t_c])
        kb.op("pool", lambda e: e.memset(ones_bf[:], 1.0), writes=[t_c])
        kb.op("pool", lambda e: e.memset(epsc[:], EPS), writes=[t_c])

        def th_all(tt):
            return [t_h[dc][tt] for dc in range(8)]

        def make_norm_bufs(st, B=None):
            B = B or Ctx()
            B.sq = sb(st, "sq", [128, 8, 512], BF16); B.t_sq = Tok()
            B.rt = sb(st, "rt", [128, 512], F32); B.t_rt = Tok()
            return B

        def make_ffn_bufs(st):
            B = make_norm_bufs(st)
            B.xn = sb(st, "xn", [128, 8, 1024], BF16); B.t_xn = [Tok(), Tok()]
            B.G = sb(st, "G", [128, 22, 1024], BF16); B.t_G = [[Tok(), Tok()] for _ in range(22)]
            B.w2b = sb(st, "w2b", [128, 22, 1024], BF16); B.t_w2 = [Tok() for _ in range(11)]
            B.w1g = [sb(st, "w1g%d" % i, [128, 8, 256], BF16) for i in range(2)]
            B.w3g = [sb(st, "w3g%d" % i, [128, 8, 256], BF16) for i in range(2)]
            B.t_wg = [Tok(), Tok()]
            B.s1 = [sb(st, "s1_%d" % i, [128, 512], F32) for i in range(2)]; B.t_s1 = [Tok(), Tok()]
            return B

        def rstd_from(B, src_chunks, rd, nfeat, K=None):
            K = K or kb
            k = src_chunks.shape[1]
            K.op("act", lambda e: e.activation(out=B.sq[:, 0:k, :], in_=src_chunks, func=AF.Square),
                 reads=rd, writes=[B.t_sq])
            K.op("pe", _mm_group(ps[:, 6, :], [(ones_bf[:], B.sq[:, c, :]) for c in range(k)]),
                 reads=[B.t_sq, t_c], writes=[PB[6]])
            K.op("act", lambda e: e.activation(out=B.rt[:], in_=ps[:, 6, :], func=AF.Ln, bias=epsc[:, 0:1], scale=1.0 / nfeat),
                 reads=[PB[6], t_c], writes=[B.t_rt])
            K.op("act", lambda e: e.activation(out=B.rt[:], in_=B.rt[:], func=AF.Exp, scale=-0.5), reads=[B.t_rt], writes=[B.t_rt])

        def norm_tile(B, tt, gi, dst, doff, t_dst, K=None):
            K = K or kb
            sl = slice(tt * 512, (tt + 1) * 512)
            rstd_from(B, hT[:, :, sl], th_all(tt), 1024.0, K=K)
            for dc in range(8):
                K.op("dve", lambda e, dc=dc: e.scalar_tensor_tensor(
                    out=dst[:, dc, doff:doff + 512], in0=hT[:, dc, sl], scalar=gcols[:, gi, dc:dc + 1],
                    in1=B.rt[:], op0=ALU.mult, op1=ALU.mult),
                    reads=[t_h[dc][tt], B.t_rt, t_c], writes=[t_dst])

        def ffn(B, w1, w3, w2, gi, after_tile=None):
            w1v = w1.rearrange("(c p) f -> p c f", p=128)
            w3v = w3.rearrange("(c p) f -> p c f", p=128)
            w2v = w2.rearrange("(c p) d -> p c d", p=128)
            for k in range(11):
                kb.dma("pool", lambda e, sem, k=k: e.dma_start(out=B.w2b[:, 2 * k:2 * k + 2, :], in_=w2v[:, 2 * k:2 * k + 2, :]).then_inc(sem, 16),
                       1, writes=[B.t_w2[k]])
            late = []
            for th in range(2):
                for t2 in range(2):
                    norm_tile(B, th * 2 + t2, gi, B.xn, t2 * 512, B.t_xn[t2])
                for fg in range(11):
                    s = fg % 2

                    def ldw(e, sem, s=s, fg=fg):
                        e.dma_start(out=B.w1g[s][:], in_=w1v[:, :, fg * 256:(fg + 1) * 256]).then_inc(sem, 16)
                        e.dma_start(out=B.w3g[s][:], in_=w3v[:, :, fg * 256:(fg + 1) * 256]).then_inc(sem, 16)

                    kb.dma("pool", ldw, 2, writes=[B.t_wg[s]])
                    if fg == 10:
                        for f_ in late:
                            f_()
                        late = []
                    for f2 in range(2):
                        fc = fg * 2 + f2
                        for t2 in range(2):
                            par = (fc * 2 + t2) % 2
                            b1, b3 = par, 2 + par
                            tsl = slice(t2 * 512, (t2 + 1) * 512)
                            fsl = slice(f2 * 128, (f2 + 1) * 128)
                            kb.op("pe", _mm_group(ps[:, b1, :], [(B.w1g[s][:, dc, fsl], B.xn[:, dc, tsl]) for dc in range(8)]),
                                  reads=[B.t_wg[s], B.t_xn[t2]], writes=[PB[b1]])
                            kb.op("pe", _mm_group(ps[:, b3, :], [(B.w3g[s][:, dc, fsl], B.xn[:, dc, tsl]) for dc in range(8)]),
                                  reads=[B.t_wg[s], B.t_xn[t2]], writes=[PB[b3]])
                            kb.op("act", lambda e, par=par, b1=b1: e.activation(out=B.s1[par][:], in_=ps[:, b1, :], func=AF.Silu),
                                  reads=[PB[b1]], writes=[B.t_s1[par]])
                            kb.op("dve", lambda e, par=par, b3=b3, fc=fc, tsl=tsl: e.tensor_tensor(
                                out=B.G[:, fc, tsl], in0=B.s1[par][:], in1=ps[:, b3, :], op=ALU.mult),
                                reads=[B.t_s1[par], PB[b3]], writes=[B.t_G[fc][t2]])
                for t2 in range(2):
                    tt = th * 2 + t2
                    sl = slice(tt * 512, (tt + 1) * 512)
                    tsl = slice(t2 * 512, (t2 + 1) * 512)
                    for dp in range(8):
                        b = 4 + (dp % 2)
                        kb.op("pe", _mm_group(ps[:, b, :], [(B.w2b[:, fc, dp * 128:(dp + 1) * 128], B.G[:, fc, tsl]) for fc in range(22)]),
                              reads=B.t_w2 + [B.t_G[fc][t2] for fc in range(22)], writes=[PB[b]])
                        kb.op("dve", lambda e, b=b, dp=dp, sl=sl: e.scalar_tensor_tensor(
                            out=hT[:, dp, sl], in0=ps[:, b, :], scalar=0.5, in1=hT[:, dp, sl], op0=ALU.mult, op1=ALU.add),
                            reads=[PB[b], t_h[dp][tt]], writes=[t_h[dp][tt]])
                    if after_tile is not None:
                        late += after_tile(tt)
            for f_ in late:
                f_()

        if do1:
            with ExitStack() as st:
                xin = [sb(st, "xin%d" % i, [128, 1024], F32) for i in range(2)]
                t_xin = [Tok(), Tok()]
                for tb in range(16):
                    s = tb % 2
                    kb.dma("sp", lambda e, sem, s=s, tb=tb: e.dma_start(out=xin[s][:], in_=x[tb * 128:(tb + 1) * 128, :]).then_inc(sem, 16),
                           1, writes=[t_xin[s]])
                    tt = tb // 4
                    for dcg in range(2):
                        b = 6 + dcg

                        def tr(e, s=s, dcg=dcg, b=b):
                            for k in range(4):
                                dc = dcg * 4 + k
                                ins = e.transpose(ps[:, b, k * 128:(k + 1) * 128], xin[s][:, dc * 128:(dc + 1) * 128], ident)
                            return ins

                        kb.op("pe", tr, reads=[t_xin[s], t_c], writes=[PB[b]])
                        dst = hT[:, dcg * 4:(dcg + 1) * 4, tb * 128:(tb + 1) * 128]
                        src = ps[:, b, :].rearrange("p (k t) -> p k t", k=4)
                        wr = [t_h[dc][tt] for dc in range(dcg * 4, dcg * 4 + 4)]
                        if dcg == 0:
                            kb.op("act", lambda e, dst=dst, src=src: e.copy(out=dst, in_=src), reads=[PB[b]], writes=wr)
                        else:
                            kb.op("dve", lambda e, dst=dst, src=src: e.tensor_copy(out=dst, in_=src), reads=[PB[b]], writes=wr)
                kb.flush()
            with ExitStack() as st:
                B = make_ffn_bufs(st)
                def emit_u(tt):
                    late = []
                    t2 = tt % 2
                    norm_tile(B, tt, 1, B.xn, t2 * 512, B.t_xn[t2])
                    uv = uT_loc[tt].rearrange("(c p) t -> p c t", p=128)
                    kb.dma("sp", lambda e, sem, uv=uv, t2=t2: e.dma_start(out=uv, in_=B.xn[:, :, t2 * 512:(t2 + 1) * 512]).then_inc(sem, 16),
                           1, reads=[B.t_xn[t2]], writes=[t_uloc[tt]])
                    if mode == "full":
                        late.append(lambda tt=tt: kb.collective(
                            lambda e, tt=tt: e.collective_compute("AllGather", ALU.bypass, replica_groups=GROUPS, dma_qos="P3",
                                                                  ins=[uT_loc[tt].opt()], outs=[uT_all[tt].opt()]),
                            reads=[t_uloc[tt]], writes=[t_uall[tt]]))
                        hpv = hT_park.rearrange("(c p) t -> p c t", p=128)
                        kb.dma("sp", lambda e, sem, tt=tt: e.dma_start(out=hpv[:, :, tt * 512:(tt + 1) * 512], in_=hT[:, :, tt * 512:(tt + 1) * 512]).then_inc(sem, 16), 1,
                               reads=th_all(tt), writes=[t_park])
                    return late

                ffn(B, w1a, w3a, w2a, 0, after_tile=emit_u)
                if mode == "s1":
                    hv = dbg_h.rearrange("(c p) t -> p c t", p=128)
                    for dc in range(8):
                        kb.dma("sp", lambda e, sem, dc=dc: e.dma_start(out=hv[:, dc, :], in_=hT[:, dc, :]).then_inc(sem, 16), 1,
                               reads=[t_h[dc][tt] for tt in range(4)])
                kb.flush()

        if do2:
            hstack.close()
            uav = [u_.rearrange("(r c p) t -> p r c t", r=4, p=128) for u_ in uT_all]
            winv = win_d.rearrange("(c p) f -> p c f", p=128)
            mixst = ExitStack()
            BRb = sb(mixst, "BRb", [128, 512], BF16); BIb = sb(mixst, "BIb", [128, 512], BF16)
            CRb = sb(mixst, "CRb", [128, 512], BF16); CRnb = sb(mixst, "CRnb", [128, 512], BF16); CInb = sb(mixst, "CInb", [128, 512], BF16)
            COS = sb(mixst, "COS", [128, 4, 512], F32); SIN = sb(mixst, "SIN", [128, 4, 512], F32)
            Cw = sb(mixst, "Cw", [128, 16, 4], F32)
            dcol = sb(mixst, "dcol", [128, 1], F32)
            ws = sb(mixst, "ws", [128, 8, 128], BF16)
            t_R = Tok(); t_p = Tok(); t_ws = Tok()
            mk = sb(mixst, "mk", [128, 640], BF16); t_mk = Tok()
            kb.dma("pool", lambda e, sem: e.dma_start(out=mk[:], in_=mask_d).then_inc(sem, 16), 1, writes=[t_mk])
            t_zi = [[Tok() for _ in range(4)] for _ in range(2)]
            mix_w = [[] for _ in range(8)]

            def mix_written(k, tok):
                mix_w[k].append(tok)
                if len(mix_w[k]) == 6 and mode == "full":
                    kb.collective(lambda e, k=k: e.collective_compute("AllGather", ALU.bypass, replica_groups=GROUPS, dma_qos="P3",
                                                                      ins=[mix_loc[k].opt()], outs=[mix_all[k].opt()]),
                                  reads=mix_w[k], writes=[t_mall[k]])

            kb.dma("pool", lambda e, sem: e.dma_start(out=ws[:], in_=winv[:, :, 388:516]).then_inc(sem, 16), 1, writes=[t_ws])

            def emit_prep(K, st):
                rowp = sb(st, "rowp", [128, 3, 512], F32)
                colp = sb(st, "colp", [128, 3, 4], F32)
                braw = sb(st, "braw", [128, 2, 512], F32)
                craw = sb(st, "craw", [128, 2, 512], F32)

                def ldp(e, sem):
                    e.dma_start(out=rowp[:], in_=rowp_d.partition_broadcast(128)).then_inc(sem, 16)
                    e.dma_start(out=colp[:], in_=colp_d).then_inc(sem, 16)
                    e.dma_start(out=braw[:], in_=braw_d).then_inc(sem, 16)
                    e.dma_start(out=craw[:], in_=craw_d).then_inc(sem, 16)
                    e.dma_start(out=dcol[:], in_=dcol_d).then_inc(sem, 16)

                K.dma("sp", ldp, 5, writes=[t_p])
                R = sb(st, "Rw", [128, 12, 512], F32)
                TT = sb(st, "TT", [128, 4, 4, 256], F32)
                RD = [t_p, t_R]

                def dv(fn):
                    K.op("dve", fn, reads=RD, writes=[t_R])

                def ac(fn):
                    K.op("act", fn, reads=RD, writes=[t_R])

                def tt_(o, a, b, op):
                    dv(lambda e: e.tensor_tensor(out=o, in0=a, in1=b, op=op))

                def csq(c, s, t0, t1, t2):
                    tt_(t0, c, c, ALU.mult)
                    tt_(t1, s, s, ALU.mult)
                    tt_(t2, c, s, ALU.mult)
                    tt_(c, t0, t1, ALU.subtract)
                    dv(lambda e: e.tensor_scalar(out=s, in0=t2, scalar1=2.0, scalar2=None, op0=ALU.mult))

                hp_t = sb(st, "hp_t", [128, 1], F32)
                K.op("pool", lambda e: e.memset(hp_t[:], float(np.pi / 2)), writes=[t_R])
                halfpi = hp_t[:, 0:1]

                def cossin(th, c, s, t0, t1, t2):
                    ac(lambda e: e.activation(out=s, in_=th, func=AF.Sin, scale=1.0 / 16))
                    ac(lambda e: e.activation(out=c, in_=th, func=AF.Sin, scale=1.0 / 16, bias=halfpi))
                    for _ in range(4):
                        csq(c, s, t0, t1, t2)

                arR, aiR, ldR = rowp[:, 0, :], rowp[:, 1, :], rowp[:, 2, :]
                ac(lambda e: e.activation(out=R[:, 0, :], in_=ldR, func=AF.Exp))
                tt_(R[:, 1, :], R[:, 0, :], aiR, ALU.mult)
                tt_(R[:, 2, :], R[:, 0, :], arR, ALU.mult)
                ac(lambda e: e.activation(out=R[:, 2, :], in_=R[:, 2, :], func=AF.Exp))
                cossin(R[:, 1, :], R[:, 3, :], R[:, 4, :], R[:, 5, :], R[:, 6, :], R[:, 7, :])
                tt_(R[:, 3, :], R[:, 3, :], R[:, 2, :], ALU.mult)
                tt_(R[:, 4, :], R[:, 4, :], R[:, 2, :], ALU.mult)
                dv(lambda e: e.tensor_scalar(out=R[:, 3, :], in0=R[:, 3, :], scalar1=-1.0, scalar2=None, op0=ALU.add))
                tt_(R[:, 5, :], arR, arR, ALU.mult)
                tt_(R[:, 6, :], aiR, aiR, ALU.mult)
                tt_(R[:, 10, :], R[:, 5, :], R[:, 6, :], ALU.add)
                dv(lambda e: e.reciprocal(out=R[:, 10, :], in_=R[:, 10, :]))
                tt_(R[:, 5, :], R[:, 3, :], arR, ALU.mult)
                tt_(R[:, 6, :], R[:, 4, :], aiR, ALU.mult)
                tt_(R[:, 8, :], R[:, 5, :], R[:, 6, :], ALU.add)
                tt_(R[:, 8, :], R[:, 8, :], R[:, 10, :], ALU.mult)
                tt_(R[:, 5, :], R[:, 4, :], arR, ALU.mult)
                tt_(R[:, 6, :], R[:, 3, :], aiR, ALU.mult)
                tt_(R[:, 9, :], R[:, 5, :], R[:, 6, :], ALU.subtract)
                tt_(R[:, 9, :], R[:, 9, :], R[:, 10, :], ALU.mult)
                tt_(R[:, 5, :], R[:, 8, :], braw[:, 0, :], ALU.mult)
                tt_(R[:, 6, :], R[:, 9, :], braw[:, 1, :], ALU.mult)
                tt_(BRb[:], R[:, 5, :], R[:, 6, :], ALU.subtract)
                tt_(R[:, 5, :], R[:, 8, :], braw[:, 1, :], ALU.mult)
                tt_(R[:, 6, :], R[:, 9, :], braw[:, 0, :], ALU.mult)
                tt_(BIb[:], R[:, 5, :], R[:, 6, :], ALU.add)
                dv(lambda e: e.tensor_copy(out=CRb[:], in_=craw[:, 0, :]))
                dv(lambda e: e.tensor_scalar(out=CRnb[:], in0=craw[:, 0, :], scalar1=-1.0, scalar2=None, op0=ALU.mult))
                dv(lambda e: e.tensor_scalar(out=CInb[:], in0=craw[:, 1, :], scalar1=-1.0, scalar2=None, op0=ALU.mult))
                arC, aiC, ldC = colp[:, 0, :], colp[:, 1, :], colp[:, 2, :]
                ac(lambda e: e.activation(out=Cw[:, 0, :], in_=ldC, func=AF.Exp))
                tt_(Cw[:, 1, :], Cw[:, 0, :], aiC, ALU.mult)
                tt_(Cw[:, 2, :], Cw[:, 0, :], arC, ALU.mult)
                ac(lambda e: e.activation(out=Cw[:, 2, :], in_=Cw[:, 2, :], func=AF.Exp))
                cossin(Cw[:, 1, :], Cw[:, 3, :], Cw[:, 4, :], Cw[:, 5, :], Cw[:, 6, :], Cw[:, 7, :])
                K.op("pool", lambda e: e.memset(COS[:, :, 0:1], 1.0), reads=RD, writes=[t_R])
                K.op("pool", lambda e: e.memset(SIN[:, :, 0:1], 0.0), reads=RD, writes=[t_R])
                for m in range(9):
                    n = 1 << m
                    cm = Cw[:, 3, :].unsqueeze(2).broadcast_to([128, 4, n])
                    sm_ = Cw[:, 4, :].unsqueeze(2).broadcast_to([128, 4, n])
                    a0, a1, a2, a3 = (TT[:, k, :, 0:n] for k in range(4))
                    tt_(a0, COS[:, :, 0:n], cm, ALU.mult)
                    tt_(a1, SIN[:, :, 0:n], sm_, ALU.mult)
                    tt_(a2, COS[:, :, 0:n], sm_, ALU.mult)
                    tt_(a3, SIN[:, :, 0:n], cm, ALU.mult)
                    tt_(COS[:, :, n:2 * n], a0, a1, ALU.subtract)
                    tt_(SIN[:, :, n:2 * n], a2, a3, ALU.add)
                    csq(Cw[:, 3, :], Cw[:, 4, :], Cw[:, 5, :], Cw[:, 6, :], Cw[:, 7, :])
                K.op("pool", lambda e: e.memset(Cw[:, 8:12, :], 0.0), reads=RD, writes=[t_R])

            def load_ut(ut, t_ut, n, gt):
                s = n % 2
                r, lt = gt // 4, gt % 4
                kb.dma("sp", lambda e, sem: e.dma_start(out=ut[s][:], in_=uav[lt][:, r, :, :]).then_inc(sem, 16),
                       1, reads=[t_uall[lt]], writes=[t_ut[s]])
                return s

            def ssm_emit(K, S, tiles, t0):
                pending = None

                def back(gt, q):
                    g2 = q % 2
                    P16 = S.pr[g2]; TP = S.t_pr[g2]

                    def cproj(e, P16=P16):
                        n = 0
                        for gp in range(4):
                            gsl = slice(gp * 128, (gp + 1) * 128)
                            for (w, idx) in ((CRb, 0), (CRnb, 1), (CInb, 2), (CInb, 3)):
                                ins = e.matmul(ps[:, 5, :], lhsT=w[:, gsl], rhs=P16[gp * 4 + idx][:], start=(n == 0), stop=(n == 15))
                                n += 1
                        return ins

                    K.op("pe", cproj, reads=TP + [t_R], writes=[PB[5]])
                    ssl = slice((gt - t0) * 512, (gt - t0 + 1) * 512)
                    K.op("dve", lambda e, g2=g2, ssl=ssl: e.scalar_tensor_tensor(out=S.yp[g2][:], in0=S.sT[:, ssl], scalar=dcol[:, 0:1], in1=ps[:, 5, :],
                                                                                op0=ALU.mult, op1=ALU.add),
                         reads=[PB[5], S.t_sT, t_p], writes=[S.t_yp[g2]])
                    K.op("act", lambda e, g2=g2: e.activation(out=S.yp[g2][:], in_=S.yp[g2][:], func=AF.Gelu_apprx_tanh),
                         reads=[S.t_yp[g2]], writes=[S.t_yp[g2]])
                    t_w = Tok()
                    K.dma("sp", lambda e, sem, g2=g2, gt=gt: e.dma_start(out=mix_loc[gt // 2][128:256, (gt % 2) * 512:(gt % 2) * 512 + 512], in_=S.yp[g2][:]).then_inc(sem, 16),
                          1, reads=[S.t_yp[g2]], writes=[t_w])
                    K.call(lambda gt=gt, t_w=t_w: mix_written(gt // 2, t_w))

                for q, gt in enumerate(tiles):
                    ssl = slice((gt - t0) * 512, (gt - t0 + 1) * 512)
                    P16 = S.pr[q % 2]; TP = S.t_pr[q % 2]
                    for gp in range(4):
                        a = (q * 4 + gp) % 2
                        W = S.wk[a]; TW = S.t_wk[a]
                        gsl = slice(gp * 128, (gp + 1) * 128)
                        br, bi = 6, 7
                        K.op("pe", lambda e, gsl=gsl, ssl=ssl: e.matmul(ps[:, 6, :], lhsT=BRb[:, gsl], rhs=S.sT[:, ssl], start=True, stop=True),
                             reads=[t_R, S.t_sT], writes=[PB[6]])
                        K.op("pe", lambda e, gsl=gsl, ssl=ssl: e.matmul(ps[:, 7, :], lhsT=BIb[:, gsl], rhs=S.sT[:, ssl], start=True, stop=True),
                             reads=[t_R, S.t_sT], writes=[PB[7]])
                        cosg, sing = COS[:, gp, :], SIN[:, gp, :]
                        K.op("dve", lambda e, W=W, cosg=cosg: e.tensor_tensor(out=W[0][:], in0=ps[:, 6, :], in1=cosg, op=ALU.mult),
                             reads=[PB[6], t_R], writes=[TW[0]])
                        K.op("dve", lambda e, W=W, sing=sing: e.tensor_tensor(out=W[1][:], in0=ps[:, 7, :], in1=sing, op=ALU.mult),
                             reads=[PB[7], t_R], writes=[TW[1]])
                        K.op("dve", lambda e, W=W, cosg=cosg: e.tensor_tensor(out=W[2][:], in0=ps[:, 7, :], in1=cosg, op=ALU.mult),
                             reads=[PB[7], t_R], writes=[TW[2]])
                        K.op("dve", lambda e, W=W, sing=sing: e.tensor_tensor(out=W[3][:], in0=ps[:, 6, :], in1=sing, op=ALU.mult),
                             reads=[PB[6], t_R], writes=[TW[3]])
                        K.op("pool", lambda e, W=W: e.tensor_tensor(out=W[4][:], in0=W[0][:], in1=W[1][:], op=ALU.add),
                             reads=[TW[0], TW[1]], writes=[TW[4]])
                        K.op("pool", lambda e, W=W: e.tensor_tensor(out=W[5][:], in0=W[2][:], in1=W[3][:], op=ALU.subtract),
                             reads=[TW[2], TW[3]], writes=[TW[5]])
                        zin = gt % 2
                        zout = (gt + 1) % 2
                        rho_b = Cw[:, 2, gp:gp + 1].broadcast_to([128, 512])
                        K.op("dve", lambda e, W=W, rho_b=rho_b, zin=zin, gp=gp: e.tensor_tensor_scan(
                            out=W[6][:], data0=rho_b, data1=W[4][:], initial=Cw[:, 8 + zin, gp:gp + 1], op0=ALU.mult, op1=ALU.add),
                            reads=[TW[4], t_R, t_zi[zin][gp]], writes=[TW[6]])
                        K.op("dve", lambda e, W=W, rho_b=rho_b, zin=zin, gp=gp: e.tensor_tensor_scan(
                            out=W[7][:], data0=rho_b, data1=W[5][:], initial=Cw[:, 10 + zin, gp:gp + 1], op0=ALU.mult, op1=ALU.add),
                            reads=[TW[5], t_R, t_zi[zin][gp]], writes=[TW[7]])
                        c5 = Cw[:, 3, gp:gp + 1]; s5 = Cw[:, 4, gp:gp + 1]
                        tmpa = Cw[:, 12, gp:gp + 1]; tmpb = Cw[:, 13, gp:gp + 1]
                        t_tmp = Tok()
                        K.op("dve", lambda e, W=W, s5=s5, tmpa=tmpa: e.tensor_tensor(out=tmpa, in0=W[7][:, 511:512], in1=s5, op=ALU.mult),
                             reads=[TW[7], t_R], writes=[t_tmp])
                        K.op("dve", lambda e, W=W, c5=c5, tmpa=tmpa, zout=zout, gp=gp: e.scalar_tensor_tensor(
                            out=Cw[:, 8 + zout, gp:gp + 1], in0=W[6][:, 511:512], scalar=c5, in1=tmpa, op0=ALU.mult, op1=ALU.subtract),
                            reads=[TW[6], t_tmp, t_R], writes=[t_zi[zout][gp]])
                        K.op("dve", lambda e, W=W, c5=c5, tmpb=tmpb: e.tensor_tensor(out=tmpb, in0=W[7][:, 511:512], in1=c5, op=ALU.mult),
                             reads=[TW[7], t_R], writes=[t_tmp])
                        K.op("dve", lambda e, W=W, s5=s5, tmpb=tmpb, zout=zout, gp=gp: e.scalar_tensor_tensor(
                            out=Cw[:, 10 + zout, gp:gp + 1], in0=W[6][:, 511:512], scalar=s5, in1=tmpb, op0=ALU.mult, op1=ALU.add),
                            reads=[TW[6], t_tmp, t_R], writes=[t_zi[zout][gp]])
                        for idx, (src, tab) in enumerate(((6, cosg), (7, sing), (6, sing), (7, cosg))):
                            K.op("pool", lambda e, W=W, P16=P16, src=src, tab=tab, j=gp * 4 + idx: e.tensor_tensor(
                                out=P16[j][:], in0=W[src][:], in1=tab, op=ALU.mult),
                                reads=[TW[src], t_R], writes=[TP[gp * 4 + idx]])
                        if gp == 2 and pending is not None:
                            back(*pending)
                            pending = None
                    pending = (gt, q)
                back(*pending)

            import os
            PREP_INLINE = os.environ.get("PREP_INLINE", "1") == "1"
            CLAMP = False
            if not PREP_INLINE:
                with ExitStack() as pst:
                    emit_prep(kb, pst)
                    kb.flush()
            for hh in range(2):
                with ExitStack() as st:
                    QA = sb(st, "QA", [128, 8192], BF16)
                    KA = sb(st, "KA", [128, 8192], BF16)
                    V = sb(st, "V", [128, 64, 128], BF16)
                    t_Q = [Tok() for _ in range(16)]; t_K = [Tok() for _ in range(16)]; t_V = [Tok() for _ in range(16)]
                    t_qrow = Tok()
                    wh = sb(st, "wh", [128, 8, 194], BF16)
                    wq = wh[:, :, 0:64]; wk = wh[:, :, 64:128]; wvf = wh[:, :, 128:194]
                    t_w = Tok()
                    ut = [sb(st, "ut%d" % i, [128, 8, 512], BF16) for i in range(2)]; t_ut = [Tok(), Tok()]
                    zf = sb(st, "zf", [128, 64], F32); t_zf = Tok()
                    bfbc = sb(st, "bfbc_s", [128, 2], F32)
                    sm = sb(st, "sm", [128, 8, 64], F32)
                    t_sm = Tok()
                    b8T = sb(st, "b8T", [64, 128], BF16); t_b8T = Tok()
                    biasT = sb(st, "biasT", [128, 16, 64], F32); t_bias = Tok()
                    clampT = sb(st, "clampT", [128, 16, 4], F32)
                    Pt = [sb(st, "Pt%d" % i, [128, 512], BF16) for i in range(3)]; t_P = [Tok() for _ in range(3)]
                    Rf = sb(st, "Rf", [128, 512], F32); t_Rf = Tok()
                    Osb = sb(st, "Osb", [64, 512], F32); t_Osb = Tok()
                    ot = [sb(st, "ot%d" % i, [64, 512], F32) for i in range(2)]; t_ot = [Tok(), Tok()]
                    S = Ctx()
                    S.sT = sb(st, "sT", [128, 4096], BF16); S.t_sT = Tok()
                    Dp = Deferred()
                    if hh == 0 and PREP_INLINE:
                        with ExitStack() as pst:
                            emit_prep(Dp, pst)
                    S.wk = [[sb(st, "wk%d_%d" % (a, k), [128, 512], F32) for k in range(8)] for a in range(2)]
                    S.t_wk = [[Tok() for _ in range(8)] for _ in range(2)]
                    S.pr = [[sb(st, "pr%d_%d" % (a, k), [128, 512], BF16) for k in range(16)] for a in range(2)]
                    S.t_pr = [[Tok() for _ in range(16)] for _ in range(2)]
                    S.yp = [sb(st, "yp%d" % i, [128, 512], F32) for i in range(2)]; S.t_yp = [Tok(), Tok()]
                    t0s = 8 * hh

                    def ldw(e, sem, hh=hh):
                        e.dma_start(out=wh[:], in_=winv[:, :, 194 * hh:194 * (hh + 1)]).then_inc(sem, 16)

                    kb.dma("pool", ldw, 1, writes=[t_w])
                    t_bf = Tok()
                    kb.dma("sp", lambda e, sem: e.dma_start(out=bfbc[:], in_=bfbc_d).then_inc(sem, 16), 1, writes=[t_bf])
                    kb.op("pool", lambda e: e.memset(V[:, :, 64:128], 1.0), writes=t_V)
                    kb.op("pool", lambda e: e.memset(KA[64:65, :], 1.0), writes=t_K)
                    kb.op("pool", lambda e: e.memset(Rf[:], 0.0), writes=[t_Rf])
                    kb.op("pool", lambda e: e.memset(sm[:, 7, :], 1.0), writes=[t_sm])
                    order = [r * 4 + lt for lt in range(4) for r in range(4)]
                    n_prep = len(Dp.q)
                    for n, gt in enumerate(order):
                        s = load_ut(ut, t_ut, n, gt)
                        par = n % 2
                        bq, bk, bv, bf_, bs = par, 2 + par, 4 + par, 6, 7
                        gsl = slice(gt * 512, (gt + 1) * 512)
                        kb.op("pe", _mm_group(ps[0:64, bq, :], [(wq[:, dc, :], ut[s][:, dc, :]) for dc in range(8)]),
                              reads=[t_w, t_ut[s]], writes=[PB[bq]])
                        kb.op("act", lambda e, gsl=gsl, bq=bq: e.copy(out=QA[0:64, gsl], in_=ps[0:64, bq, :]), reads=[PB[bq]], writes=[t_Q[gt]])
                        kb.op("pe", _mm_group(ps[0:64, bk, :], [(wk[:, dc, :], ut[s][:, dc, :]) for dc in range(8)]),
                              reads=[t_w, t_ut[s]], writes=[PB[bk]])
                        kb.op("dve", lambda e, gsl=gsl, bk=bk: e.tensor_copy(out=KA[0:64, gsl], in_=ps[0:64, bk, :]), reads=[PB[bk]], writes=[t_K[gt]])

                        def vproj(e, s=s, bv=bv):
                            for blk in range(4):
                                for dc in range(8):
                                    ins = e.matmul(ps[:, bv, blk * 66:(blk + 1) * 66], lhsT=ut[s][:, dc, blk * 128:(blk + 1) * 128],
                                                   rhs=wvf[:, dc, :], start=(dc == 0), stop=(dc == 7))
                            return ins

                        kb.op("pe", vproj, reads=[t_w, t_ut[s]], writes=[PB[bv]])
                        pv3 = ps[:, bv, 0:264].rearrange("p (b d) -> p b d", b=4)
                        kb.op("act", lambda e, gt=gt, pv3=pv3: e.copy(out=V[:, gt * 4:(gt + 1) * 4, 0:64], in_=pv3[:, :, 0:64]),
                              reads=[PB[bv]], writes=[t_V[gt]])
                        kb.op("act", lambda e, gt=gt, hh=hh, pv3=pv3: e.copy(out=zf[:, gt * 4:(gt + 1) * 4], in_=pv3[:, :, 64 + hh]),
                              reads=[PB[bv]], writes=[t_zf])
                        if t0s <= gt < t0s + 8:
                            kb.op("pe", _mm_group(ps[:, bs, :], [(ws[:, dc, :], ut[s][:, dc, :]) for dc in range(8)]),
                                  reads=[t_ws, t_ut[s]], writes=[PB[bs]])
                            kb.op("act", lambda e, gt=gt, t0s=t0s, bs=bs: e.copy(out=S.sT[:, (gt - t0s) * 512:(gt - t0s + 1) * 512], in_=ps[:, bs, :]),
                                  reads=[PB[bs]], writes=[S.t_sT])
                        Dp.replay(kb, (n_prep * (n + 1)) // 12)
                    Dp.replay(kb, n_prep)
                    kb.op("act", lambda e, hh=hh: e.activation(out=sm[:, 0, :], in_=zf[:], func=AF.Sigmoid, bias=bfbc[:, hh:hh + 1], scale=1.0),
                          reads=[t_zf, t_bf], writes=[t_sm])
                    kb.op("act", lambda e: e.activation(out=sm[:, 0, :], in_=sm[:, 0, :], func=AF.Ln), reads=[t_sm], writes=[t_sm])

                    def cums(e):
                        e.matmul(ps[:, 4, 0:64], lhsT=triF, rhs=sm[:, 0, :], start=True, stop=True)
                        return e.matmul(ps[:, 4, 64:128], lhsT=onesF, rhs=sm[:, 0, :], start=True, stop=True)

                    kb.op("pe", cums, reads=[t_sm, t_c], writes=[PB[4]])
                    kb.op("dve", lambda e: e.tensor_copy(out=sm[:, 1:3, :], in_=ps[:, 4, 0:128].rearrange("p (a b) -> p a b", a=2)),
                          reads=[PB[4]], writes=[t_sm])
                    kb.op("dve", lambda e: e.tensor_tensor_scan(out=sm[:, 3, :], data0=sm[:, 7, :], data1=sm[:, 2, :], initial=0.0,
                                                                op0=ALU.mult, op1=ALU.add), reads=[t_sm], writes=[t_sm])
                    kb.op("dve", lambda e: e.tensor_tensor(out=sm[:, 4, :], in0=sm[:, 3, :], in1=sm[:, 2, :], op=ALU.subtract),
                          reads=[t_sm], writes=[t_sm])
                    kb.op("dve", lambda e: e.tensor_tensor(out=sm[:, 5, :], in0=sm[:, 1, :], in1=sm[:, 4, :], op=ALU.add),
                          reads=[t_sm], writes=[t_sm])
                    offs4 = sm[:, 4, :].rearrange("p (i f) -> p i f", f=4)
                    kb.op("dve", lambda e: e.tensor_tensor(out=sm[:, 6, :].rearrange("p (i f) -> p i f", f=4),
                                                           in0=sm[:, 5, :].rearrange("p (i f) -> p i f", f=4),
                                                           in1=offs4[:, :, 0:1].broadcast_to([128, 16, 4]), op=ALU.subtract),
                          reads=[t_sm], writes=[t_sm])
                    kb.op("dve", lambda e: e.tensor_scalar(out=sm[:, 6, :], in0=sm[:, 6, :], scalar1=8.0, scalar2=None, op0=ALU.mult),
                          reads=[t_sm], writes=[t_sm])
                    kb.op("pe", lambda e: e.transpose(ps[0:64, 5, 0:128], sm[:, 6, :], ident), reads=[t_sm, t_c], writes=[PB[5]])
                    kb.op("act", lambda e: e.copy(out=b8T[:], in_=ps[0:64, 5, 0:128]), reads=[PB[5]], writes=[t_b8T])
                    kb.dma("sp", lambda e, sem: e.dma_start(out=QA[64:65, :].rearrange("o (b t) -> o b t", t=128), in_=b8T[:]).then_inc(sem, 16),
                           1, reads=[t_b8T], writes=[t_qrow])
                    for i in range(16):
                        nk = 4 * i + 4
                        kb.op("dve", lambda e, i=i, nk=nk: e.tensor_scalar(out=biasT[:, i, 0:nk], in0=sm[:, 5, 0:nk], scalar1=-1.0,
                                                                           scalar2=sm[:, 4, 4 * i:4 * i + 1], op0=ALU.mult, op1=ALU.add),
                              reads=[t_sm], writes=[t_bias])
                        if CLAMP:
                            kb.op("act", lambda e, i=i: e.activation(out=clampT[:, i, :], in_=biasT[:, i, 4 * i:4 * i + 4], func=AF.Identity,
                                                                     scale=-8.0, bias=240.0),
                                  reads=[t_bias], writes=[t_bias])
                    D = Deferred()
                    ssm_emit(D, S, list(range(t0s, t0s + 8)), t0s)
                    n_ssm = len(D.q)
                    steps_total = 544
                    steps = 0
                    mv = [m_.rearrange("(a p) t -> p a t", p=64) for m_ in mix_loc]
                    norm_late = []
                    for i in range(16):
                        nk = 4 * i + 4
                        qsl = slice(i * 512, (i + 1) * 512)
                        ob = 3 + (i % 2)

                        def s_op(kk, i=i, qsl=qsl):
                            c0 = max(0, kk - 4 * i) * 128
                            diag = kk >= 4 * i

                            def f(e, kk=kk, qsl=qsl, c0=c0, diag=diag):
                                ins = e.matmul(ps[:, kk % 3, c0:512], lhsT=KA[0:65, kk * 128:(kk + 1) * 128],
                                               rhs=QA[0:65, qsl.start + c0:qsl.stop], start=True, stop=not diag)
                                if diag:
                                    ins = e.matmul(ps[:, kk % 3, c0:512], lhsT=mk[:, 0:128], rhs=mk[:, 128:128 + 512 - c0], start=False, stop=True)
                                return ins

                            kb.op("pe", f, reads=[t_K[kk // 4], t_Q[i], t_qrow, t_mk], writes=[PB[kk % 3]])

                        def p_op(kk, i=i):
                            c0 = max(0, kk - 4 * i) * 128
                            kb.op("act", lambda e, kk=kk, i=i, c0=c0: e.activation(out=Pt[kk % 3][:, c0:512], in_=ps[:, kk % 3, c0:512], func=AF.Exp,
                                                                                    bias=biasT[:, i, kk:kk + 1], scale=0.125),
                                  reads=[PB[kk % 3], t_bias], writes=[t_P[kk % 3]])

                        def pv_op(kk, ob=ob, nk=nk, i=i):
                            c0 = max(0, kk - 4 * i) * 128
                            kb.op("pe", lambda e, kk=kk, ob=ob, nk=nk, c0=c0: e.matmul(ps[:, ob, c0:512], lhsT=V[:, kk, :], rhs=Pt[kk % 3][:, c0:512],
                                                                                        start=(kk == 0), stop=(kk == nk - 1)),
                                  reads=[t_V[kk // 4], t_P[kk % 3]], writes=[PB[ob]])

                        s_op(0)
                        if nk > 1:
                            s_op(1)
                        for kk in range(nk):
                            p_op(kk)
                            if kk + 2 < nk:
                                s_op(kk + 2)
                            pv_op(kk)
                            steps += 1
                            D.replay(kb, (n_ssm * steps) // steps_total)
                            if kk == 3:
                                for f_ in norm_late:
                                    f_()
                                norm_late = []
                        kb.op("act", lambda e, ob=ob: e.activation(out=Rf[64:128, :], in_=ps[64:128, ob, :], func=AF.Ln), reads=[PB[ob]], writes=[t_Rf])
                        kb.op("act", lambda e: e.activation(out=Rf[64:128, :], in_=Rf[64:128, :], func=AF.Exp, scale=-1.0), reads=[t_Rf], writes=[t_Rf])
                        kb.op("act", lambda e, ob=ob: e.copy(out=Osb[:], in_=ps[0:64, ob, :]), reads=[PB[ob]], writes=[t_Osb])

                        def norm_b(i=i, hh=hh):
                            o = i % 2
                            kb.op("pe", lambda e: e.matmul(ps[0:64, 5, :], lhsT=selF, rhs=Rf[:], start=True, stop=True),
                                  reads=[t_Rf, t_c], writes=[PB[5]])
                            kb.op("dve", lambda e, o=o: e.tensor_tensor(out=ot[o][:], in0=Osb[:], in1=ps[0:64, 5, :], op=ALU.mult),
                                  reads=[t_Osb, PB[5]], writes=[t_ot[o]])
                            t_wr = Tok()
                            kb.dma("sp", lambda e, sem, o=o, i=i, hh=hh: e.dma_start(out=mv[i // 2][:, hh, (i % 2) * 512:(i % 2) * 512 + 512], in_=ot[o][:]).then_inc(sem, 16),
                                   1, reads=[t_ot[o]], writes=[t_wr])
                            mix_written(i // 2, t_wr)

                        norm_late.append(norm_b)
                    for f_ in norm_late:
                        f_()
                    D.replay(kb, n_ssm)
                    kb.flush()
            mixst.close()

        if do3:
            if mode == "full":
                hstack = ExitStack()
                hT = sb(hstack, "hT3", [128, 8, 2048], F32)
                h1T_d = hT_park
            def reload_h(tts, after):
                if mode in ("s3", "full"):
                    hv = h1T_d.rearrange("(c p) t -> p c t", p=128)
                    for tt in tts:
                        for hf in range(2):
                            kb.dma("sp", lambda e, sem, tt=tt, hf=hf: e.dma_start(out=hT[:, hf * 4:hf * 4 + 4, tt * 512:(tt + 1) * 512],
                                                                                  in_=hv[:, hf * 4:hf * 4 + 4, tt * 512:(tt + 1) * 512]).then_inc(sem, 16), 1,
                                   reads=[t_park] + after, writes=[t_h[dc][tt] for dc in range(hf * 4, hf * 4 + 4)])

            with ExitStack() as st:
                Bs3 = [make_norm_bufs(st), make_norm_bufs(st)]
                idxs = sb(st, "idxs", [128, 16], mybir.dt.uint32)
                bglu = sb(st, "bglu_s", [128, 4], F32)
                wglub = sb(st, "wglub", [128, 4, 512], BF16)
                woutb = sb(st, "woutb", [128, 8, 1024], BF16)
                t_w = Tok()
                selp = [sb(st, "selp%d" % i, [128, 8, 1024], F32) for i in range(2)]
                t_selp = [[Tok(), Tok()] for _ in range(2)]
                ygb2 = [sb(st, "ygb%d" % i, [128, 4, 512], BF16) for i in range(2)]; t_ygb2 = [Tok(), Tok()]
                gt2 = [sb(st, "gate%d" % i, [128, 512], F32) for i in range(2)]; t_gt2 = [Tok(), Tok()]
                mxb2 = [sb(st, "mxb%d" % i, [128, 8, 512], BF16) for i in range(2)]; t_mx2 = [Tok(), Tok()]

                def ldw3(e, sem):
                    e.dma_start(out=idxs[:], in_=idx_d).then_inc(sem, 16)
                    e.dma_start(out=bglu[:], in_=bglu_d).then_inc(sem, 16)

                t_w0 = Tok()
                kb.dma("sp", ldw3, 2, writes=[t_w0])

                def ldw3b(e, sem):
                    e.dma_start(out=wglub[:], in_=wglu_d.rearrange("(c p) f -> p c f", p=128)).then_inc(sem, 16)
                    e.dma_start(out=woutb[:, 0:4, :], in_=wout_d.rearrange("(c p) f -> p c f", p=128)[:, 0:4, :]).then_inc(sem, 16)
                    e.dma_start(out=woutb[:, 4:8, :], in_=wout_d.rearrange("(c p) f -> p c f", p=128)[:, 4:8, :]).then_inc(sem, 16)

                kb.dma("pool", ldw3b, 3, writes=[t_w])
                for pp in range(2):
                    def gat(e, sem, pp=pp):
                        for r in range(4):
                            for two in range(2):
                                col = (pp * 4 + r) * 2 + two
                                e.indirect_dma_start(out=selp[pp][:, two * 4 + r, :], out_offset=None, in_=mix_big,
                                                     in_offset=bass.IndirectOffsetOnAxis(ap=idxs[:, col:col + 1], axis=0)).then_inc(sem, 16)

                    kb.dma("pool", gat, 8, reads=[t_w0] + ([t_mall[k] for k in range(pp, 8, 2)] if mode == "full" else []), writes=t_selp[pp])
                    reload_h([2 * pp, 2 * pp + 1], [])
                def chain(K, tt):
                    pp, hq = tt // 2, tt % 2
                    sel = selp[pp][:, :, hq * 512:(hq + 1) * 512]
                    t_sel = t_selp[pp][hq]
                    ygb, t_ygb = ygb2[tt % 2], t_ygb2[tt % 2]
                    mxb, t_mx = mxb2[tt % 2], t_mx2[tt % 2]
                    B = Bs3[tt % 2]
                    K.op("act", lambda e, ygb=ygb, sel=sel: e.copy(out=ygb[:], in_=sel[:, 4:8, :]), reads=[t_sel], writes=[t_ygb])
                    rstd_from(B, sel[:, 0:4, :], [t_sel], 512.0, K=K)
                    for k in range(4):
                        K.op("dve", lambda e, k=k, mxb=mxb, sel=sel, B=B: e.scalar_tensor_tensor(out=mxb[:, k, :], in0=sel[:, k, :], scalar=gcols[:, 5, k:k + 1],
                                                                                                 in1=B.rt[:], op0=ALU.mult, op1=ALU.mult),
                             reads=[t_sel, B.t_rt, t_c], writes=[t_mx])
                    for cp in range(4):
                        b = cp % 2
                        gt_, t_gt = gt2[cp % 2], t_gt2[cp % 2]
                        K.op("pe", _mm_group(ps[:, b, :], [(wglub[:, c, cp * 128:(cp + 1) * 128], ygb[:, c, :]) for c in range(4)]),
                             reads=[t_w, t_ygb], writes=[PB[b]])
                        K.op("act", lambda e, b=b, cp=cp, gt_=gt_: e.activation(out=gt_[:], in_=ps[:, b, :], func=AF.Sigmoid, bias=bglu[:, cp:cp + 1], scale=1.0),
                             reads=[PB[b], t_w0], writes=[t_gt])
                        K.op("dve", lambda e, cp=cp, sel=sel, gt_=gt_: e.tensor_tensor(out=sel[:, 4 + cp, :], in0=sel[:, 4 + cp, :], in1=gt_[:], op=ALU.mult),
                             reads=[t_gt, t_sel], writes=[t_sel])
                    rstd_from(B, sel[:, 4:8, :], [t_sel], 512.0, K=K)
                    for k in range(4, 8):
                        K.op("dve", lambda e, k=k, mxb=mxb, sel=sel, B=B: e.scalar_tensor_tensor(out=mxb[:, k, :], in0=sel[:, k, :], scalar=gcols[:, 5, k:k + 1],
                                                                                                 in1=B.rt[:], op0=ALU.mult, op1=ALU.mult),
                             reads=[t_sel, B.t_rt, t_c], writes=[t_mx])

                chain(kb, 0)
                for tt in range(4):
                    sl = slice(tt * 512, (tt + 1) * 512)
                    mxb, t_mx = mxb2[tt % 2], t_mx2[tt % 2]
                    Dn = Deferred()
                    if tt < 3:
                        chain(Dn, tt + 1)
                    nq = len(Dn.q)
                    for dp in range(8):
                        b = 2 + (dp % 2)
                        kb.op("pe", _mm_group(ps[:, b, :], [(woutb[:, k, dp * 128:(dp + 1) * 128], mxb[:, k, :]) for k in range(8)]),
                              reads=[t_w, t_mx], writes=[PB[b]])
                        kb.op("dve", lambda e, b=b, dp=dp, sl=sl: e.tensor_tensor(out=hT[:, dp, sl], in0=ps[:, b, :], in1=hT[:, dp, sl], op=ALU.add),
                              reads=[PB[b], t_h[dp][tt]], writes=[t_h[dp][tt]])
                        Dn.replay(kb, (nq * (dp + 1)) // 7)
                    Dn.replay(kb, nq)
                kb.flush()
            with ExitStack() as st:
                B = make_ffn_bufs(st)
                ffn(B, w1b, w3b, w2b_d, 2)
                kb.flush()
            with ExitStack() as st:
                Bs = [make_norm_bufs(st), make_norm_bufs(st)]
                xn4 = [sb(st, "xn3_%d" % i, [128, 8, 512], BF16) for i in range(4)]; t_xn4 = [Tok() for _ in range(4)]
                wgb = sb(st, "wgb", [128, 8, 1024], BF16)
                wpb = sb(st, "wpb", [128, 2, 1024], BF16)
                t_w = Tok()
                pin = [sb(st, "pin%d" % i, [128, 256], F32) for i in range(2)]; t_pin = [Tok(), Tok()]
                pT = sb(st, "pT", [128, 2, 2048], BF16); t_pT = [Tok() for _ in range(4)]
                sg = [sb(st, "sg%d" % i, [128, 512], F32) for i in range(2)]; t_sg = [Tok(), Tok()]
                yT2 = [sb(st, "yT%d" % i, [128, 8, 512], F32) for i in range(2)]; t_yT2 = [Tok(), Tok()]
                otl = [sb(st, "otl%d" % i, [128, 1024], F32) for i in range(2)]; t_otl = [Tok(), Tok()]

                def ldw4(e, sem):
                    wgv = wgate_d.rearrange("(c p) f -> p c f", p=128)
                    e.dma_start(out=wgb[:, 0:4, :], in_=wgv[:, 0:4, :]).then_inc(sem, 16)
                    e.dma_start(out=wgb[:, 4:8, :], in_=wgv[:, 4:8, :]).then_inc(sem, 16)
                    e.dma_start(out=wpb[:], in_=wproj_d.rearrange("(c p) f -> p c f", p=128)).then_inc(sem, 16)

                kb.dma("pool", ldw4, 3, writes=[t_w])
                for tt in range(4):
                    norm_tile(Bs[tt % 2], tt, 3, xn4[tt], 0, t_xn4[tt])
                for tb in range(16):
                    s = tb % 2
                    kb.dma("sp", lambda e, sem, s=s, tb=tb: e.dma_start(out=pin[s][:], in_=p_d[tb * 128:(tb + 1) * 128, :]).then_inc(sem, 16),
                           1, writes=[t_pin[s]])
                    b = 6 + (tb % 2)

                    def trp(e, s=s, b=b):
                        e.transpose(ps[:, b, 0:128], pin[s][:, 0:128], ident)
                        return e.transpose(ps[:, b, 128:256], pin[s][:, 128:256], ident)

                    kb.op("pe", trp, reads=[t_pin[s], t_c], writes=[PB[b]])
                    kb.op("act", lambda e, b=b, tb=tb: e.copy(out=pT[:, :, tb * 128:(tb + 1) * 128],
                                                              in_=ps[:, b, 0:256].rearrange("p (k t) -> p k t", k=2)),
                          reads=[PB[b]], writes=[t_pT[tb // 4]])
                for tt in range(4):
                    sl = slice(tt * 512, (tt + 1) * 512)
                    xn = xn4[tt]
                    for dp in range(8):
                        par = dp % 2
                        bg, bp = par, 2 + par
                        dsl = slice(dp * 128, (dp + 1) * 128)
                        kb.op("pe", _mm_group(ps[:, bg, :], [(wgb[:, k, dsl], xn[:, k, :]) for k in range(8)]),
                              reads=[t_w, t_xn4[tt]], writes=[PB[bg]])
                        kb.op("pe", _mm_group(ps[:, bp, :], [(wpb[:, k, dsl], pT[:, k, sl]) for k in range(2)]),
                              reads=[t_w, t_pT[tt]], writes=[PB[bp]])
                        kb.op("act", lambda e, par=par, bg=bg: e.activation(out=sg[par][:], in_=ps[:, bg, :], func=AF.Sigmoid),
                              reads=[PB[bg]], writes=[t_sg[par]])
                        kb.op("dve", lambda e, par=par, bp=bp: e.tensor_tensor(out=sg[par][:], in0=sg[par][:], in1=ps[:, bp, :], op=ALU.mult),
                              reads=[t_sg[par], PB[bp]], writes=[t_sg[par]])
                        kb.op("pool", lambda e, par=par, dp=dp, sl=sl: e.tensor_tensor(out=hT[:, dp, sl], in0=hT[:, dp, sl], in1=sg[par][:], op=ALU.add),
                              reads=[t_sg[par], t_h[dp][tt]], writes=[t_h[dp][tt]])
                for tt in range(4):
                    yT, t_yT = yT2[tt % 2], t_yT2[tt % 2]
                    norm_tile(Bs[tt % 2], tt, 4, yT, 0, t_yT)
                    for tb in range(4):
                        o = tb % 2
                        for dcg in range(2):
                            b = 4 + dcg

                            def trb(e, tb=tb, dcg=dcg, b=b, yT=yT):
                                for k in range(4):
                                    ins = e.transpose(ps[:, b, k * 128:(k + 1) * 128], yT[:, dcg * 4 + k, tb * 128:(tb + 1) * 128], ident)
                                return ins

                            kb.op("pe", trb, reads=[t_yT, t_c], writes=[PB[b]])
                            if dcg == 0:
                                kb.op("act", lambda e, o=o, b=b: e.copy(out=otl[o][:, 0:512], in_=ps[:, b, :]), reads=[PB[b]], writes=[t_otl[o]])
                            else:
                                kb.op("dve", lambda e, o=o, b=b: e.tensor_copy(out=otl[o][:, 512:1024], in_=ps[:, b, :]), reads=[PB[b]], writes=[t_otl[o]])
                        r0 = (tt * 4 + tb) * 128
                        kb.dma("sp", lambda e, sem, o=o, r0=r0: e.dma_start(out=out_d[r0:r0 + 128, :], in_=otl[o][:]).then_inc(sem, 16),
                               1, reads=[t_otl[o]])
                kb.flush()
        hstack.close()
    return nc


def _gcols(inputs):
    gs = [inputs["g_ffn1"][0], inputs["g_mix"][0], inputs["g_ffn2"][0], inputs["g_ple"][0], inputs["g_final"],
          np.concatenate([inputs["g_attn_out"][0], inputs["g_ssm_out"][0]])]
    out = np.zeros((128, 6, 8), np.float32)
    for i, g in enumerate(gs):
        out[:, i, :] = np.asarray(g, np.float32).reshape(8, 128).T
    return out


def _consts():
    cst = np.zeros((128, 4, 128), np.float32)
    cst[:, 0, :] = np.eye(128, dtype=np.float32)
    cst[:, 1, :] = np.triu(np.ones((128, 128), np.float32))
    cst[:, 2, :] = 1.0
    for m in range(64):
        cst[64 + m, 3, m] = 1.0
    return cst


def _mask_const():
    m = np.zeros((128, 640), np.float32)
    r = np.arange(128)
    m[:, 0:128] = np.where(r[None, :] > r[:, None], -30000.0, 0.0)
    m[:, 128:256] = np.eye(128, dtype=np.float32)
    return m


def _mixer_params(inputs, j):
    w_in = inputs["w_in"][0]
    win = np.zeros((1024, 520), np.float32)
    for hh in range(2):
        c0 = 194 * hh
        hd = 128 * j + 64 * hh
        win[:, c0:c0 + 64] = w_in[:, hd:hd + 64]
        win[:, c0 + 64:c0 + 128] = w_in[:, 512 + hd:512 + hd + 64]
        win[:, c0 + 128:c0 + 192] = w_in[:, 1024 + hd:1024 + hd + 64]
        win[:, c0 + 192:c0 + 194] = w_in[:, 1536 + 2 * j:1536 + 2 * j + 2]
    win[:, 388:516] = w_in[:, 1544 + 128 * j:1544 + 128 * (j + 1)]
    bfbc = np.broadcast_to(inputs["b_f"][0][2 * j:2 * j + 2][None, :], (128, 2)).astype(np.float32).copy()
    a_re, a_im, log_dt = inputs["a_re"][0], inputs["a_im"][0], inputs["log_dt"][0]
    b_re, b_im, c_re, c_im = inputs["b_re"][0], inputs["b_im"][0], inputs["c_re"][0], inputs["c_im"][0]
    colp = np.zeros((128, 3, 4), np.float32)
    rowp = np.zeros((3, 512), np.float32)
    braw = np.zeros((128, 2, 512), np.float32)
    craw = np.zeros((128, 2, 512), np.float32)
    dcol = np.zeros((128, 1), np.float32)
    for gl in range(8):
        g = 8 * j + gl
        gp, half = gl // 2, gl % 2
        st = slice(half * 64, half * 64 + 64)
        colp[st, 0, gp] = a_re[g]; colp[st, 1, gp] = a_im[g]; colp[st, 2, gp] = log_dt[g]
        rs = slice(gp * 128 + half * 64, gp * 128 + half * 64 + 64)
        rowp[0, rs] = a_re[g]; rowp[1, rs] = a_im[g]; rowp[2, rs] = log_dt[g]
        ch = slice(16 * gl, 16 * gl + 16)
        braw[ch, 0, rs] = b_re[g].T
        braw[ch, 1, rs] = b_im[g].T
        cs = slice(gp * 128 + 16 * gl, gp * 128 + 16 * gl + 16)
        craw[st, 0, cs] = c_re[g].T
        craw[st, 1, cs] = c_im[g].T
        dcol[ch, 0] = inputs["d_skip"][0][g]
    return dict(win=win, bfbc=bfbc, colp=colp, rowp=rowp, braw=braw, craw=craw, dcol=dcol, maskc=_mask_const())


def make_in_maps(inputs, mode="full"):
    cst = _consts()
    gc = _gcols(inputs)
    f = lambda k: np.ascontiguousarray(inputs[k][0], dtype=np.float32)
    shared = {"cst": cst, "gcols": gc}
    if mode in ("full", "s1"):
        shared.update({"w1_a": f("w1_a"), "w3_a": f("w3_a"), "w2_a": f("w2_a")})
    if mode in ("full", "s3"):
        shared.update({"w_glu": f("w_glu"), "w_out": f("w_out"), "w1_b": f("w1_b"), "w3_b": f("w3_b"), "w2_b": f("w2_b"),
                       "w_gate": f("w_ple_gate"), "w_proj": f("w_ple_proj"),
                       "bglu": np.ascontiguousarray(inputs["b_glu"][0].reshape(4, 128).T, dtype=np.float32)})
    maps = []
    for c in range(8):
        b, j = c // 4, c % 4
        m = dict(shared)
        if mode in ("full", "s1"):
            m["x"] = np.ascontiguousarray(inputs["x"][b, j * 2048:(j + 1) * 2048, :], dtype=np.float32)
        if mode in ("full", "s2"):
            m.update(_mixer_params(inputs, j))
        if mode in ("full", "s3"):
            idx = np.zeros((128, 16), np.uint32)
            for kk in range(2):
                for r in range(4):
                    for two in range(2):
                        idx[:, (kk * 4 + r) * 2 + two] = (((2 * j + kk) * 4 + r) * 2 + two) * 128 + np.arange(128)
            m["idxT"] = idx
            m["p"] = np.ascontiguousarray(inputs["p"][0, b, j * 2048:(j + 1) * 2048, :], dtype=np.float32)
        maps.append(m)
    return maps


def kernel(**inputs):
    inputs = {k: np.asarray(v) for k, v in inputs.items()}
    nc = build("full")
    maps = make_in_maps(inputs, "full")
    res = run_bass_kernel_spmd(nc, maps, core_ids=list(range(8)))
    out = np.zeros((2, 8192, 1024), np.float32)
    for c in range(8):
        b, j = c // 4, c % 4
        out[b, j * 2048:(j + 1) * 2048, :] = res.results[c]["out"]
    return out
```

```python
import numpy as np
import concourse.bass as bass
import concourse.mybir as mybir
from concourse.bass_utils import run_bass_kernel_spmd
from contextlib import ExitStack

F32 = mybir.dt.float32
BF16 = mybir.dt.bfloat16
AF = mybir.ActivationFunctionType
ALU = mybir.AluOpType
EPS = 1e-6
GROUPS = [[0, 1, 2, 3], [4, 5, 6, 7]]


class Tok:
    __slots__ = ("w", "r")

    def __init__(self):
        self.w = None
        self.r = []


class KB:
    ENG = ["pe", "act", "dve", "pool", "sp"]

    def __init__(self, nc, stack, n_dma_sems=16):
        self.nc = nc
        self.prog = {e: [] for e in self.ENG}
        self.sems = {e: stack.enter_context(nc.semaphore("s_" + e)) for e in self.ENG}
        self.cnt = {e: 0 for e in self.ENG}
        self.dsems = [stack.enter_context(nc.semaphore("dq%d" % i)) for i in range(n_dma_sems)]
        self.dcnt = [0] * n_dma_sems
        self.dnext = 0
        self.dnext_sw = 0
        self.csem = stack.enter_context(nc.semaphore("cc"))
        self.ccnt = 0
        self.waited = {e: {} for e in self.ENG}
        self.stack = stack
        self.nblk = 0

    def _sem(self, k):
        if k[0] == "e":
            return self.sems[k[1]]
        if k[0] == "c":
            return self.csem
        return self.dsems[k[1]]

    def _deps(self, reads, writes):
        deps = {}

        def add(tok):
            if tok is None:
                return
            k, v = tok
            if deps.get(k, 0) < v:
                deps[k] = v

        for b in reads:
            add(b.w)
        for b in writes:
            add(b.w)
            for t in b.r:
                add(t)
        return deps

    def _emit_waits(self, eng, deps, skip_self):
        for k, v in deps.items():
            if skip_self and k == ("e", eng):
                continue
            if self.waited[eng].get(k, 0) >= v:
                continue
            self.waited[eng][k] = v
            sem = self._sem(k)
            self.prog[eng].append(lambda e, sem=sem, v=v: e.wait_ge(sem, v))

    def _update(self, tok, reads, writes):
        for b in writes:
            b.w = tok
            b.r = []
        for b in reads:
            if b not in writes:
                b.r.append(tok)
                if len(b.r) > 64:
                    b.r = b.r[-64:]

    def op(self, eng, fn, reads=(), writes=()):
        deps = self._deps(reads, writes)
        self._emit_waits(eng, deps, skip_self=(eng == "pe"))
        self.cnt[eng] += 1
        tok = (("e", eng), self.cnt[eng])
        sem = self.sems[eng]
        self.prog[eng].append(lambda e, fn=fn, sem=sem: fn(e).then_inc(sem, 1))
        self._update(tok, reads, writes)
        return tok

    def dma(self, eng, fn, n, reads=(), writes=()):
        deps = self._deps(reads, writes)
        half = len(self.dsems) // 2
        if eng == "pool":
            i = half + self.dnext_sw
            self.dnext_sw = (self.dnext_sw + 1) % (len(self.dsems) - half)
        else:
            i = self.dnext
            self.dnext = (self.dnext + 1) % half
        k = ("d", i)
        if self.dcnt[i] > 0:
            deps[k] = max(deps.get(k, 0), self.dcnt[i])
        self._emit_waits(eng, deps, skip_self=True)
        self.dcnt[i] += 16 * n
        tok = (k, self.dcnt[i])
        sem = self.dsems[i]
        self.prog[eng].append(lambda e, fn=fn, sem=sem: fn(e, sem))
        self._update(tok, reads, writes)
        return tok

    def collective(self, fn, reads=(), writes=()):
        deps = self._deps(reads, writes)
        self._emit_waits("pool", deps, skip_self=False)
        self.ccnt += 1
        tok = (("c", 0), self.ccnt)
        sem = self.csem
        self.prog["pool"].append(lambda e, fn=fn, sem=sem: fn(e).then_inc(sem, 1))
        self._update(tok, reads, writes)
        return tok

    def wait_all(self, eng, toks):
        deps = {}
        for t in toks:
            if t is None:
                continue
            k, v = t
            if deps.get(k, 0) < v:
                deps[k] = v
        self._emit_waits(eng, deps, skip_self=False)

    def flush(self):
        allt = [(("e", e), self.cnt[e]) for e in self.ENG if self.cnt[e] > 0]
        allt += [(("d", i), self.dcnt[i]) for i in range(len(self.dsems)) if self.dcnt[i] > 0]
        for e in self.ENG:
            self.wait_all(e, allt)
        nc = self.nc
        prog = self.prog
        self.nblk += 1
        with nc.Block(no_gpsimd_drain=True) as block:

            @block.tensor
            def _(e):
                for f in prog["pe"]:
                    f(e)

            @block.scalar
            def _(e):
                for f in prog["act"]:
                    f(e)

            @block.vector
            def _(e):
                for f in prog["dve"]:
                    f(e)

            @block.gpsimd
            def _(e):
                for f in prog["pool"]:
                    f(e)

            @block.sync
            def _(e):
                for f in prog["sp"]:
                    f(e)

        self.prog = {e: [] for e in self.ENG}


class Ctx:
    pass


class Deferred:
    def __init__(self):
        self.q = []

    def op(self, *a, **k):
        self.q.append(("op", a, k))

    def dma(self, *a, **k):
        self.q.append(("dma", a, k))

    def call(self, fn):
        self.q.append(("call", (fn,), {}))

    def replay(self, kb, upto):
        while self.pos < min(upto, len(self.q)):
            m, a, k = self.q[self.pos]
            self.pos += 1
            if m == "call":
                a[0]()
            else:
                getattr(kb, m)(*a, **k)

    pos = 0


def _mm_group(ps_ap, pairs):
    def f(e):
        n = len(pairs)
        for i, (l, r) in enumerate(pairs):
            ins = e.matmul(ps_ap, lhsT=l, rhs=r, start=(i == 0), stop=(i == n - 1))
        return ins

    return f


def build(mode="full"):
    nc = bass.Bass("TRN2", target_bir_lowering=False)

    def din(name, shape, dt=F32):
        return nc.dram_tensor(name, list(shape), dt, kind="ExternalInput").ap()

    def dout(name, shape, dt=F32):
        return nc.dram_tensor(name, list(shape), dt, kind="ExternalOutput").ap()

    def dint(name, shape, dt=F32):
        return nc.dram_tensor(name, list(shape), dt).ap()

    do1 = mode in ("full", "s1")
    do2 = mode in ("full", "s2")
    do3 = mode in ("full", "s3")
    cst_d = din("cst", [128, 4, 128])
    gcols_d = din("gcols", [128, 6, 8])
    if do1:
        x = din("x", [2048, 1024])
        w1a = din("w1_a", [1024, 2816]); w3a = din("w3_a", [1024, 2816]); w2a = din("w2_a", [2816, 1024])
    if do2:
        win_d = din("win", [1024, 520])
        mask_d = din("maskc", [128, 640])
        bfbc_d = din("bfbc", [128, 2])
        colp_d = din("colp", [128, 3, 4])
        rowp_d = din("rowp", [3, 512])
        braw_d = din("braw", [128, 2, 512])
        craw_d = din("craw", [128, 2, 512])
        dcol_d = din("dcol", [128, 1])
    if do3:
        idx_d = din("idxT", [128, 16], mybir.dt.uint32)
        wglu_d = din("w_glu", [512, 512]); bglu_d = din("bglu", [128, 4])
        wout_d = din("w_out", [1024, 1024])
        w1b = din("w1_b", [1024, 2816]); w3b = din("w3_b", [1024, 2816]); w2b_d = din("w2_b", [2816, 1024])
        wgate_d = din("w_gate", [1024, 1024]); wproj_d = din("w_proj", [256, 1024])
        p_d = din("p", [2048, 256])
        out_d = dout("out", [2048, 1024])
    if mode == "s1":
        dbg_h = dout("dbg_h", [1024, 2048])
        uT_loc = [dout("dbg_u%d" % t, [1024, 512], BF16) for t in range(4)]
    elif mode == "full":
        uT_loc = [dint("uT_loc%d" % t, [1024, 512], BF16) for t in range(4)]
    if mode == "s2":
        uT_all = [din("uT_all%d" % t, [4096, 512], BF16) for t in range(4)]
        mix_loc = [dout("dbg_mix%d" % k, [256, 1024]) for k in range(8)]
    elif mode == "full":
        uT_all = [dint("uT_all%d" % t, [4096, 512], BF16) for t in range(4)]
        mix_loc = [dint("mix_loc%d" % k, [256, 1024]) for k in range(8)]
    if mode == "s3":
        h1T_d = din("h1T", [1024, 2048])
        mix_big = din("mix_all", [8192, 1024])
    elif mode == "full":
        mix_big = dint("mix_all_big", [8192, 1024])
    if do3:
        mix_all = [mix_big[k * 1024:(k + 1) * 1024, :] for k in range(8)]
    t_uloc = [Tok() for _ in range(4)]; t_uall = [Tok() for _ in range(4)]
    t_mloc = [Tok() for _ in range(8)]; t_mall = [Tok() for _ in range(8)]

    with ExitStack() as top:
        kb = KB(nc, top)

        _uid = [0]

        def sb(st, name, shape, dt):
            _uid[0] += 1
            return st.enter_context(nc.sbuf_tensor("%s_u%d" % (name, _uid[0]), list(shape), dt))

        ps = top.enter_context(nc.psum_tensor("ps", [128, 8, 512], F32))
        PB = [Tok() for _ in range(8)]
        t_h = [[Tok() for _ in range(4)] for _ in range(8)]
        cst = sb(top, "cst_s", [128, 4, 128], F32)
        ident = cst[:, 0, :]
        triF = cst[:, 1, :]
        onesF = cst[:, 2, :]
        selF = cst[:, 3, 0:64]
        ones_bf = sb(top, "ones_bf", [128, 128], BF16)
        gcols = sb(top, "gcols_s", [128, 6, 8], F32)
        epsc = sb(top, "epsc", [128, 1], F32)
        t_c = Tok()
        hstack = ExitStack()
        hT = sb(hstack, "hT", [128, 8, 2048], F32)
        if mode == "full":
            hT_park = dint("hT_park", [1024, 2048])
        t_park = Tok()

        def ldc(e, sem):
            e.dma_start(out=cst[:], in_=cst_d).then_inc(sem, 16)
            e.dma_start(out=gcols[:], in_=gcols_d).then_inc(sem, 16)

        kb.dma("sp", ldc, 2, writes=[t_c])
        kb.op("pool", lambda e: e.memset(ones_bf[:], 1.0), writes=[t_c])
        kb.op("pool", lambda e: e.memset(epsc[:], EPS), writes=[t_c])

        def th_all(tt):
            return [t_h[dc][tt] for dc in range(8)]

        def make_norm_bufs(st, B=None):
            B = B or Ctx()
            B.sq = sb(st, "sq", [128, 8, 512], BF16); B.t_sq = Tok()
            B.rt = sb(st, "rt", [128, 512], F32); B.t_rt = Tok()
            return B

        def make_ffn_bufs(st):
            B = make_norm_bufs(st)
            B.xn = sb(st, "xn", [128, 8, 1024], BF16); B.t_xn = [Tok(), Tok()]
            B.G = sb(st, "G", [128, 22, 1024], BF16); B.t_G = [[Tok(), Tok()] for _ in range(22)]
            B.w2b = sb(st, "w2b", [128, 22, 1024], BF16); B.t_w2 = [Tok() for _ in range(11)]
            B.w1g = [sb(st, "w1g%d" % i, [128, 8, 256], BF16) for i in range(2)]
            B.w3g = [sb(st, "w3g%d" % i, [128, 8, 256], BF16) for i in range(2)]
            B.t_wg = [Tok(), Tok()]
            B.s1 = [sb(st, "s1_%d" % i, [128, 512], F32) for i in range(2)]; B.t_s1 = [Tok(), Tok()]
            return B

        def rstd_from(B, src_chunks, rd, nfeat, K=None):
            K = K or kb
            k = src_chunks.shape[1]
            K.op("act", lambda e: e.activation(out=B.sq[:, 0:k, :], in_=src_chunks, func=AF.Square),
                 reads=rd, writes=[B.t_sq])
            K.op("pe", _mm_group(ps[:, 6, :], [(ones_bf[:], B.sq[:, c, :]) for c in range(k)]),
                 reads=[B.t_sq, t_c], writes=[PB[6]])
            K.op("act", lambda e: e.activation(out=B.rt[:], in_=ps[:, 6, :], func=AF.Ln, bias=epsc[:, 0:1], scale=1.0 / nfeat),
                 reads=[PB[6], t_c], writes=[B.t_rt])
            K.op("act", lambda e: e.activation(out=B.rt[:], in_=B.rt[:], func=AF.Exp, scale=-0.5), reads=[B.t_rt], writes=[B.t_rt])

        def norm_tile(B, tt, gi, dst, doff, t_dst, K=None):
            K = K or kb
            sl = slice(tt * 512, (tt + 1) * 512)
            rstd_from(B, hT[:, :, sl], th_all(tt), 1024.0, K=K)
            for dc in range(8):
                K.op("dve", lambda e, dc=dc: e.scalar_tensor_tensor(
                    out=dst[:, dc, doff:doff + 512], in0=hT[:, dc, sl], scalar=gcols[:, gi, dc:dc + 1],
                    in1=B.rt[:], op0=ALU.mult, op1=ALU.mult),
                    reads=[t_h[dc][tt], B.t_rt, t_c], writes=[t_dst])

        def ffn(B, w1, w3, w2, gi, after_tile=None):
            w1v = w1.rearrange("(c p) f -> p c f", p=128)
            w3v = w3.rearrange("(c p) f -> p c f", p=128)
            w2v = w2.rearrange("(c p) d -> p c d", p=128)
            for k in range(11):
                kb.dma("pool", lambda e, sem, k=k: e.dma_start(out=B.w2b[:, 2 * k:2 * k + 2, :], in_=w2v[:, 2 * k:2 * k + 2, :]).then_inc(sem, 16),
                       1, writes=[B.t_w2[k]])
            late = []
            for th in range(2):
                for t2 in range(2):
                    norm_tile(B, th * 2 + t2, gi, B.xn, t2 * 512, B.t_xn[t2])
                for fg in range(11):
                    s = fg % 2

                    def ldw(e, sem, s=s, fg=fg):
                        e.dma_start(out=B.w1g[s][:], in_=w1v[:, :, fg * 256:(fg + 1) * 256]).then_inc(sem, 16)
                        e.dma_start(out=B.w3g[s][:], in_=w3v[:, :, fg * 256:(fg + 1) * 256]).then_inc(sem, 16)

                    kb.dma("pool", ldw, 2, writes=[B.t_wg[s]])
                    if fg == 10:
                        for f_ in late:
                            f_()
                        late = []
                    for f2 in range(2):
                        fc = fg * 2 + f2
                        for t2 in range(2):
                            par = (fc * 2 + t2) % 2
                            b1, b3 = par, 2 + par
                            tsl = slice(t2 * 512, (t2 + 1) * 512)
                            fsl = slice(f2 * 128, (f2 + 1) * 128)
                            kb.op("pe", _mm_group(ps[:, b1, :], [(B.w1g[s][:, dc, fsl], B.xn[:, dc, tsl]) for dc in range(8)]),
                                  reads=[B.t_wg[s], B.t_xn[t2]], writes=[PB[b1]])
                            kb.op("pe", _mm_group(ps[:, b3, :], [(B.w3g[s][:, dc, fsl], B.xn[:, dc, tsl]) for dc in range(8)]),
                                  reads=[B.t_wg[s], B.t_xn[t2]], writes=[PB[b3]])
                            kb.op("act", lambda e, par=par, b1=b1: e.activation(out=B.s1[par][:], in_=ps[:, b1, :], func=AF.Silu),
                                  reads=[PB[b1]], writes=[B.t_s1[par]])
                            kb.op("dve", lambda e, par=par, b3=b3, fc=fc, tsl=tsl: e.tensor_tensor(
                                out=B.G[:, fc, tsl], in0=B.s1[par][:], in1=ps[:, b3, :], op=ALU.mult),
                                reads=[B.t_s1[par], PB[b3]], writes=[B.t_G[fc][t2]])
                for t2 in range(2):
                    tt = th * 2 + t2
                    sl = slice(tt * 512, (tt + 1) * 512)
                    tsl = slice(t2 * 512, (t2 + 1) * 512)
                    for dp in range(8):
                        b = 4 + (dp % 2)
                        kb.op("pe", _mm_group(ps[:, b, :], [(B.w2b[:, fc, dp * 128:(dp + 1) * 128], B.G[:, fc, tsl]) for fc in range(22)]),
                              reads=B.t_w2 + [B.t_G[fc][t2] for fc in range(22)], writes=[PB[b]])
                        kb.op("dve", lambda e, b=b, dp=dp, sl=sl: e.scalar_tensor_tensor(
                            out=hT[:, dp, sl], in0=ps[:, b, :], scalar=0.5, in1=hT[:, dp, sl], op0=ALU.mult, op1=ALU.add),
                            reads=[PB[b], t_h[dp][tt]], writes=[t_h[dp][tt]])
                    if after_tile is not None:
                        late += after_tile(tt)
            for f_ in late:
                f_()

        if do1:
            with ExitStack() as st:
                xin = [sb(st, "xin%d" % i, [128, 1024], F32) for i in range(2)]
                t_xin = [Tok(), Tok()]
                for tb in range(16):
                    s = tb % 2
                    kb.dma("sp", lambda e, sem, s=s, tb=tb: e.dma_start(out=xin[s][:], in_=x[tb * 128:(tb + 1) * 128, :]).then_inc(sem, 16),
                           1, writes=[t_xin[s]])
                    tt = tb // 4
                    for dcg in range(2):
                        b = 6 + dcg

                        def tr(e, s=s, dcg=dcg, b=b):
                            for k in range(4):
                                dc = dcg * 4 + k
                                ins = e.transpose(ps[:, b, k * 128:(k + 1) * 128], xin[s][:, dc * 128:(dc + 1) * 128], ident)
                            return ins

                        kb.op("pe", tr, reads=[t_xin[s], t_c], writes=[PB[b]])
                        dst = hT[:, dcg * 4:(dcg + 1) * 4, tb * 128:(tb + 1) * 128]
                        src = ps[:, b, :].rearrange("p (k t) -> p k t", k=4)
                        wr = [t_h[dc][tt] for dc in range(dcg * 4, dcg * 4 + 4)]
                        if dcg == 0:
                            kb.op("act", lambda e, dst=dst, src=src: e.copy(out=dst, in_=src), reads=[PB[b]], writes=wr)
                        else:
                            kb.op("dve", lambda e, dst=dst, src=src: e.tensor_copy(out=dst, in_=src), reads=[PB[b]], writes=wr)
                kb.flush()
            with ExitStack() as st:
                B = make_ffn_bufs(st)
                def emit_u(tt):
                    late = []
                    t2 = tt % 2
                    norm_tile(B, tt, 1, B.xn, t2 * 512, B.t_xn[t2])
                    uv = uT_loc[tt].rearrange("(c p) t -> p c t", p=128)
                    kb.dma("sp", lambda e, sem, uv=uv, t2=t2: e.dma_start(out=uv, in_=B.xn[:, :, t2 * 512:(t2 + 1) * 512]).then_inc(sem, 16),
                           1, reads=[B.t_xn[t2]], writes=[t_uloc[tt]])
                    if mode == "full":
                        late.append(lambda tt=tt: kb.collective(
                            lambda e, tt=tt: e.collective_compute("AllGather", ALU.bypass, replica_groups=GROUPS, dma_qos="P3",
                                                                  ins=[uT_loc[tt].opt()], outs=[uT_all[tt].opt()]),
                            reads=[t_uloc[tt]], writes=[t_uall[tt]]))
                        hpv = hT_park.rearrange("(c p) t -> p c t", p=128)
                        kb.dma("sp", lambda e, sem, tt=tt: e.dma_start(out=hpv[:, :, tt * 512:(tt + 1) * 512], in_=hT[:, :, tt * 512:(tt + 1) * 512]).then_inc(sem, 16), 1,
                               reads=th_all(tt), writes=[t_park])
                    return late

                ffn(B, w1a, w3a, w2a, 0, after_tile=emit_u)
                if mode == "s1":
                    hv = dbg_h.rearrange("(c p) t -> p c t", p=128)
                    for dc in range(8):
                        kb.dma("sp", lambda e, sem, dc=dc: e.dma_start(out=hv[:, dc, :], in_=hT[:, dc, :]).then_inc(sem, 16), 1,
                               reads=[t_h[dc][tt] for tt in range(4)])
                kb.flush()

        if do2:
            hstack.close()
            uav = [u_.rearrange("(r c p) t -> p r c t", r=4, p=128) for u_ in uT_all]
            winv = win_d.rearrange("(c p) f -> p c f", p=128)
            mixst = ExitStack()
            BRb = sb(mixst, "BRb", [128, 512], BF16); BIb = sb(mixst, "BIb", [128, 512], BF16)
            CRb = sb(mixst, "CRb", [128, 512], BF16); CRnb = sb(mixst, "CRnb", [128, 512], BF16); CInb = sb(mixst, "CInb", [128, 512], BF16)
            COS = sb(mixst, "COS", [128, 4, 512], F32); SIN = sb(mixst, "SIN", [128, 4, 512], F32)
            Cw = sb(mixst, "Cw", [128, 16, 4], F32)
            dcol = sb(mixst, "dcol", [128, 1], F32)
            ws = sb(mixst, "ws", [128, 8, 128], BF16)
            t_R = Tok(); t_p = Tok(); t_ws = Tok()
            mk = sb(mixst, "mk", [128, 640], BF16); t_mk = Tok()
            kb.dma("pool", lambda e, sem: e.dma_start(out=mk[:], in_=mask_d).then_inc(sem, 16), 1, writes=[t_mk])
            t_zi = [[Tok() for _ in range(4)] for _ in range(2)]
            mix_w = [[] for _ in range(8)]

            def mix_written(k, tok):
                mix_w[k].append(tok)
                if len(mix_w[k]) == 6 and mode == "full":
                    kb.collective(lambda e, k=k: e.collective_compute("AllGather", ALU.bypass, replica_groups=GROUPS, dma_qos="P3",
                                                                      ins=[mix_loc[k].opt()], outs=[mix_all[k].opt()]),
                                  reads=mix_w[k], writes=[t_mall[k]])

            kb.dma("pool", lambda e, sem: e.dma_start(out=ws[:], in_=winv[:, :, 388:516]).then_inc(sem, 16), 1, writes=[t_ws])

            def emit_prep(K, st):
                rowp = sb(st, "rowp", [128, 3, 512], F32)
                colp = sb(st, "colp", [128, 3, 4], F32)
                braw = sb(st, "braw", [128, 2, 512], F32)
                craw = sb(st, "craw", [128, 2, 512], F32)

                def ldp(e, sem):
                    e.dma_start(out=rowp[:], in_=rowp_d.partition_broadcast(128)).then_inc(sem, 16)
                    e.dma_start(out=colp[:], in_=colp_d).then_inc(sem, 16)
                    e.dma_start(out=braw[:], in_=braw_d).then_inc(sem, 16)
                    e.dma_start(out=craw[:], in_=craw_d).then_inc(sem, 16)
                    e.dma_start(out=dcol[:], in_=dcol_d).then_inc(sem, 16)

                K.dma("sp", ldp, 5, writes=[t_p])
                R = sb(st, "Rw", [128, 12, 512], F32)
                TT = sb(st, "TT", [128, 4, 4, 256], F32)
                RD = [t_p, t_R]

                def dv(fn):
                    K.op("dve", fn, reads=RD, writes=[t_R])

                def ac(fn):
                    K.op("act", fn, reads=RD, writes=[t_R])

                def tt_(o, a, b, op):
                    dv(lambda e: e.tensor_tensor(out=o, in0=a, in1=b, op=op))

                def csq(c, s, t0, t1, t2):
                    tt_(t0, c, c, ALU.mult)
                    tt_(t1, s, s, ALU.mult)
                    tt_(t2, c, s, ALU.mult)
                    tt_(c, t0, t1, ALU.subtract)
                    dv(lambda e: e.tensor_scalar(out=s, in0=t2, scalar1=2.0, scalar2=None, op0=ALU.mult))

                hp_t = sb(st, "hp_t", [128, 1], F32)
                K.op("pool", lambda e: e.memset(hp_t[:], float(np.pi / 2)), writes=[t_R])
                halfpi = hp_t[:, 0:1]

                def cossin(th, c, s, t0, t1, t2):
                    ac(lambda e: e.activation(out=s, in_=th, func=AF.Sin, scale=1.0 / 16))
                    ac(lambda e: e.activation(out=c, in_=th, func=AF.Sin, scale=1.0 / 16, bias=halfpi))
                    for _ in range(4):
                        csq(c, s, t0, t1, t2)

                arR, aiR, ldR = rowp[:, 0, :], rowp[:, 1, :], rowp[:, 2, :]
                ac(lambda e: e.activation(out=R[:, 0, :], in_=ldR, func=AF.Exp))
                tt_(R[:, 1, :], R[:, 0, :], aiR, ALU.mult)
                tt_(R[:, 2, :], R[:, 0, :], arR, ALU.mult)
                ac(lambda e: e.activation(out=R[:, 2, :], in_=R[:, 2, :], func=AF.Exp))
                cossin(R[:, 1, :], R[:, 3, :], R[:, 4, :], R[:, 5, :], R[:, 6, :], R[:, 7, :])
                tt_(R[:, 3, :], R[:, 3, :], R[:, 2, :], ALU.mult)
                tt_(R[:, 4, :], R[:, 4, :], R[:, 2, :], ALU.mult)
                dv(lambda e: e.tensor_scalar(out=R[:, 3, :], in0=R[:, 3, :], scalar1=-1.0, scalar2=None, op0=ALU.add))
                tt_(R[:, 5, :], arR, arR, ALU.mult)
                tt_(R[:, 6, :], aiR, aiR, ALU.mult)
                tt_(R[:, 10, :], R[:, 5, :], R[:, 6, :], ALU.add)
                dv(lambda e: e.reciprocal(out=R[:, 10, :], in_=R[:, 10, :]))
                tt_(R[:, 5, :], R[:, 3, :], arR, ALU.mult)
                tt_(R[:, 6, :], R[:, 4, :], aiR, ALU.mult)
                tt_(R[:, 8, :], R[:, 5, :], R[:, 6, :], ALU.add)
                tt_(R[:, 8, :], R[:, 8, :], R[:, 10, :], ALU.mult)
                tt_(R[:, 5, :], R[:, 4, :], arR, ALU.mult)
                tt_(R[:, 6, :], R[:, 3, :], aiR, ALU.mult)
                tt_(R[:, 9, :], R[:, 5, :], R[:, 6, :], ALU.subtract)
                tt_(R[:, 9, :], R[:, 9, :], R[:, 10, :], ALU.mult)
                tt_(R[:, 5, :], R[:, 8, :], braw[:, 0, :], ALU.mult)
                tt_(R[:, 6, :], R[:, 9, :], braw[:, 1, :], ALU.mult)
                tt_(BRb[:], R[:, 5, :], R[:, 6, :], ALU.subtract)
                tt_(R[:, 5, :], R[:, 8, :], braw[:, 1, :], ALU.mult)
                tt_(R[:, 6, :], R[:, 9, :], braw[:, 0, :], ALU.mult)
                tt_(BIb[:], R[:, 5, :], R[:, 6, :], ALU.add)
                dv(lambda e: e.tensor_copy(out=CRb[:], in_=craw[:, 0, :]))
                dv(lambda e: e.tensor_scalar(out=CRnb[:], in0=craw[:, 0, :], scalar1=-1.0, scalar2=None, op0=ALU.mult))
                dv(lambda e: e.tensor_scalar(out=CInb[:], in0=craw[:, 1, :], scalar1=-1.0, scalar2=None, op0=ALU.mult))
                arC, aiC, ldC = colp[:, 0, :], colp[:, 1, :], colp[:, 2, :]
                ac(lambda e: e.activation(out=Cw[:, 0, :], in_=ldC, func=AF.Exp))
                tt_(Cw[:, 1, :], Cw[:, 0, :], aiC, ALU.mult)
                tt_(Cw[:, 2, :], Cw[:, 0, :], arC, ALU.mult)
                ac(lambda e: e.activation(out=Cw[:, 2, :], in_=Cw[:, 2, :], func=AF.Exp))
                cossin(Cw[:, 1, :], Cw[:, 3, :], Cw[:, 4, :], Cw[:, 5, :], Cw[:, 6, :], Cw[:, 7, :])
                K.op("pool", lambda e: e.memset(COS[:, :, 0:1], 1.0), reads=RD, writes=[t_R])
                K.op("pool", lambda e: e.memset(SIN[:, :, 0:1], 0.0), reads=RD, writes=[t_R])
                for m in range(9):
                    n = 1 << m
                    cm = Cw[:, 3, :].unsqueeze(2).broadcast_to([128, 4, n])
                    sm_ = Cw[:, 4, :].unsqueeze(2).broadcast_to([128, 4, n])
                    a0, a1, a2, a3 = (TT[:, k, :, 0:n] for k in range(4))
                    tt_(a0, COS[:, :, 0:n], cm, ALU.mult)
                    tt_(a1, SIN[:, :, 0:n], sm_, ALU.mult)
                    tt_(a2, COS[:, :, 0:n], sm_, ALU.mult)
                    tt_(a3, SIN[:, :, 0:n], cm, ALU.mult)
                    tt_(COS[:, :, n:2 * n], a0, a1, ALU.subtract)
                    tt_(SIN[:, :, n:2 * n], a2, a3, ALU.add)
                    csq(Cw[:, 3, :], Cw[:, 4, :], Cw[:, 5, :], Cw[:, 6, :], Cw[:, 7, :])
                K.op("pool", lambda e: e.memset(Cw[:, 8:12, :], 0.0), reads=RD, writes=[t_R])

            def load_ut(ut, t_ut, n, gt):
                s = n % 2
                r, lt = gt // 4, gt % 4
                kb.dma("sp", lambda e, sem: e.dma_start(out=ut[s][:], in_=uav[lt][:, r, :, :]).then_inc(sem, 16),
                       1, reads=[t_uall[lt]], writes=[t_ut[s]])
                return s

            def ssm_emit(K, S, tiles, t0):
                pending = None

                def back(gt, q):
                    g2 = q % 2
                    P16 = S.pr[g2]; TP = S.t_pr[g2]

                    def cproj(e, P16=P16):
                        n = 0
                        for gp in range(4):
                            gsl = slice(gp * 128, (gp + 1) * 128)
                            for (w, idx) in ((CRb, 0), (CRnb, 1), (CInb, 2), (CInb, 3)):
                                ins = e.matmul(ps[:, 5, :], lhsT=w[:, gsl], rhs=P16[gp * 4 + idx][:], start=(n == 0), stop=(n == 15))
                                n += 1
                        return ins

                    ssl = slice((gt - t0) * 512, (gt - t0 + 1) * 512)

                    def y_unit(cproj=cproj, TP=TP, g2=g2, ssl=ssl):
                        kb.op("pe", cproj, reads=TP + [t_R], writes=[PB[5]])
                        kb.op("dve", lambda e, g2=g2, ssl=ssl: e.scalar_tensor_tensor(out=S.yp[g2][:], in0=S.sT[:, ssl], scalar=dcol[:, 0:1], in1=ps[:, 5, :],
                                                                                    op0=ALU.mult, op1=ALU.add),
                              reads=[PB[5], S.t_sT, t_p], writes=[S.t_yp[g2]])

                    K.call(y_unit)
                    K.op("act", lambda e, g2=g2: e.activation(out=S.yp[g2][:], in_=S.yp[g2][:], func=AF.Gelu_apprx_tanh),
                         reads=[S.t_yp[g2]], writes=[S.t_yp[g2]])
                    t_w = Tok()
                    K.dma("sp", lambda e, sem, g2=g2, gt=gt: e.dma_start(out=mix_loc[gt // 2][128:256, (gt % 2) * 512:(gt % 2) * 512 + 512], in_=S.yp[g2][:]).then_inc(sem, 16),
                          1, reads=[S.t_yp[g2]], writes=[t_w])
                    K.call(lambda gt=gt, t_w=t_w: mix_written(gt // 2, t_w))

                for q, gt in enumerate(tiles):
                    ssl = slice((gt - t0) * 512, (gt - t0 + 1) * 512)
                    P16 = S.pr[q % 2]; TP = S.t_pr[q % 2]
                    for gp in range(4):
                        a = (q * 4 + gp) % 2
                        W = S.wk[a]; TW = S.t_wk[a]
                        gsl = slice(gp * 128, (gp + 1) * 128)
                        br, bi = 6, 7
                        K.op("pe", lambda e, gsl=gsl, ssl=ssl: e.matmul(ps[:, 6, :], lhsT=BRb[:, gsl], rhs=S.sT[:, ssl], start=True, stop=True),
                             reads=[t_R, S.t_sT], writes=[PB[6]])
                        K.op("pe", lambda e, gsl=gsl, ssl=ssl: e.matmul(ps[:, 7, :], lhsT=BIb[:, gsl], rhs=S.sT[:, ssl], start=True, stop=True),
                             reads=[t_R, S.t_sT], writes=[PB[7]])
                        cosg, sing = COS[:, gp, :], SIN[:, gp, :]
                        K.op("dve", lambda e, W=W, cosg=cosg: e.tensor_tensor(out=W[0][:], in0=ps[:, 6, :], in1=cosg, op=ALU.mult),
                             reads=[PB[6], t_R], writes=[TW[0]])
                        K.op("dve", lambda e, W=W, sing=sing: e.tensor_tensor(out=W[1][:], in0=ps[:, 7, :], in1=sing, op=ALU.mult),
                             reads=[PB[7], t_R], writes=[TW[1]])
                        K.op("dve", lambda e, W=W, cosg=cosg: e.tensor_tensor(out=W[2][:], in0=ps[:, 7, :], in1=cosg, op=ALU.mult),
                             reads=[PB[7], t_R], writes=[TW[2]])
                        K.op("dve", lambda e, W=W, sing=sing: e.tensor_tensor(out=W[3][:], in0=ps[:, 6, :], in1=sing, op=ALU.mult),
                             reads=[PB[6], t_R], writes=[TW[3]])
                        K.op("pool", lambda e, W=W: e.tensor_tensor(out=W[4][:], in0=W[0][:], in1=W[1][:], op=ALU.add),
                             reads=[TW[0], TW[1]], writes=[TW[4]])
                        K.op("pool", lambda e, W=W: e.tensor_tensor(out=W[5][:], in0=W[2][:], in1=W[3][:], op=ALU.subtract),
                             reads=[TW[2], TW[3]], writes=[TW[5]])
                        zin = gt % 2
                        zout = (gt + 1) % 2
                        rho_b = Cw[:, 2, gp:gp + 1].broadcast_to([128, 512])
                        K.op("dve", lambda e, W=W, rho_b=rho_b, zin=zin, gp=gp: e.tensor_tensor_scan(
                            out=W[6][:], data0=rho_b, data1=W[4][:], initial=Cw[:, 8 + zin, gp:gp + 1], op0=ALU.mult, op1=ALU.add),
                            reads=[TW[4], t_R, t_zi[zin][gp]], writes=[TW[6]])
                        K.op("dve", lambda e, W=W, rho_b=rho_b, zin=zin, gp=gp: e.tensor_tensor_scan(
                            out=W[7][:], data0=rho_b, data1=W[5][:], initial=Cw[:, 10 + zin, gp:gp + 1], op0=ALU.mult, op1=ALU.add),
                            reads=[TW[5], t_R, t_zi[zin][gp]], writes=[TW[7]])
                        c5 = Cw[:, 3, gp:gp + 1]; s5 = Cw[:, 4, gp:gp + 1]
                        tmpa = Cw[:, 12, gp:gp + 1]; tmpb = Cw[:, 13, gp:gp + 1]
                        t_tmp = Tok()
                        K.op("dve", lambda e, W=W, s5=s5, tmpa=tmpa: e.tensor_tensor(out=tmpa, in0=W[7][:, 511:512], in1=s5, op=ALU.mult),
                             reads=[TW[7], t_R], writes=[t_tmp])
                        K.op("dve", lambda e, W=W, c5=c5, tmpa=tmpa, zout=zout, gp=gp: e.scalar_tensor_tensor(
                            out=Cw[:, 8 + zout, gp:gp + 1], in0=W[6][:, 511:512], scalar=c5, in1=tmpa, op0=ALU.mult, op1=ALU.subtract),
                            reads=[TW[6], t_tmp, t_R], writes=[t_zi[zout][gp]])
                        K.op("dve", lambda e, W=W, c5=c5, tmpb=tmpb: e.tensor_tensor(out=tmpb, in0=W[7][:, 511:512], in1=c5, op=ALU.mult),
                             reads=[TW[7], t_R], writes=[t_tmp])
                        K.op("dve", lambda e, W=W, s5=s5, tmpb=tmpb, zout=zout, gp=gp: e.scalar_tensor_tensor(
                            out=Cw[:, 10 + zout, gp:gp + 1], in0=W[6][:, 511:512], scalar=s5, in1=tmpb, op0=ALU.mult, op1=ALU.add),
                            reads=[TW[6], t_tmp, t_R], writes=[t_zi[zout][gp]])
                        for idx, (src, tab) in enumerate(((6, cosg), (7, sing), (6, sing), (7, cosg))):
                            K.op("pool", lambda e, W=W, P16=P16, src=src, tab=tab, j=gp * 4 + idx: e.tensor_tensor(
                                out=P16[j][:], in0=W[src][:], in1=tab, op=ALU.mult),
                                reads=[TW[src], t_R], writes=[TP[gp * 4 + idx]])
                        if gp == 2 and pending is not None:
                            back(*pending)
                            pending = None
                    pending = (gt, q)
                back(*pending)

            import os
            PREP_INLINE = os.environ.get("PREP_INLINE", "1") == "1"
            CLAMP = False
            if not PREP_INLINE:
                with ExitStack() as pst:
                    emit_prep(kb, pst)
                    kb.flush()
            for hh in range(2):
                with ExitStack() as st:
                    QA = sb(st, "QA", [128, 8192], BF16)
                    KA = sb(st, "KA", [128, 8192], BF16)
                    V = sb(st, "V", [128, 64, 128], BF16)
                    t_Q = [Tok() for _ in range(16)]; t_K = [Tok() for _ in range(16)]; t_V = [Tok() for _ in range(16)]
                    t_qrow = Tok()
                    wh = sb(st, "wh", [128, 8, 194], BF16)
                    wq = wh[:, :, 0:64]; wk = wh[:, :, 64:128]; wvf = wh[:, :, 128:194]
                    t_w = Tok()
                    ut = [sb(st, "ut%d" % i, [128, 8, 512], BF16) for i in range(2)]; t_ut = [Tok(), Tok()]
                    zf = sb(st, "zf", [128, 64], F32); t_zf = Tok()
                    bfbc = sb(st, "bfbc_s", [128, 2], F32)
                    sm = sb(st, "sm", [128, 8, 64], F32)
                    t_sm = Tok()
                    b8T = sb(st, "b8T", [64, 128], BF16); t_b8T = Tok()
                    biasT = sb(st, "biasT", [128, 16, 64], F32); t_bias = Tok()
                    clampT = sb(st, "clampT", [128, 16, 4], F32)
                    Pt = [sb(st, "Pt%d" % i, [128, 512], BF16) for i in range(3)]; t_P = [Tok() for _ in range(3)]
                    Rf = sb(st, "Rf", [128, 512], F32); t_Rf = Tok()
                    Osb = sb(st, "Osb", [64, 512], F32); t_Osb = Tok()
                    ot = [sb(st, "ot%d" % i, [64, 512], F32) for i in range(2)]; t_ot = [Tok(), Tok()]
                    S = Ctx()
                    S.sT = sb(st, "sT", [128, 4096], BF16); S.t_sT = Tok()
                    Dp = Deferred()
                    if hh == 0 and PREP_INLINE:
                        with ExitStack() as pst:
                            emit_prep(Dp, pst)
                    S.wk = [[sb(st, "wk%d_%d" % (a, k), [128, 512], F32) for k in range(8)] for a in range(2)]
                    S.t_wk = [[Tok() for _ in range(8)] for _ in range(2)]
                    S.pr = [[sb(st, "pr%d_%d" % (a, k), [128, 512], BF16) for k in range(16)] for a in range(2)]
                    S.t_pr = [[Tok() for _ in range(16)] for _ in range(2)]
                    S.yp = [sb(st, "yp%d" % i, [128, 512], F32) for i in range(2)]; S.t_yp = [Tok(), Tok()]
                    t0s = 8 * hh

                    def ldw(e, sem, hh=hh):
                        e.dma_start(out=wh[:], in_=winv[:, :, 194 * hh:194 * (hh + 1)]).then_inc(sem, 16)

                    kb.dma("pool", ldw, 1, writes=[t_w])
                    t_bf = Tok()
                    kb.dma("sp", lambda e, sem: e.dma_start(out=bfbc[:], in_=bfbc_d).then_inc(sem, 16), 1, writes=[t_bf])
                    kb.op("pool", lambda e: e.memset(V[:, :, 64:128], 1.0), writes=t_V)
                    kb.op("pool", lambda e: e.memset(KA[64:65, :], 1.0), writes=t_K)
                    kb.op("pool", lambda e: e.memset(Rf[:], 0.0), writes=[t_Rf])
                    kb.op("pool", lambda e: e.memset(sm[:, 7, :], 1.0), writes=[t_sm])
                    order = [r * 4 + lt for lt in range(4) for r in range(4)]
                    n_prep = len(Dp.q)
                    for n, gt in enumerate(order):
                        s = load_ut(ut, t_ut, n, gt)
                        par = n % 2
                        bq, bk, bv, bf_, bs = par, 2 + par, 4 + par, 6, 7
                        gsl = slice(gt * 512, (gt + 1) * 512)
                        kb.op("pe", _mm_group(ps[0:64, bq, :], [(wq[:, dc, :], ut[s][:, dc, :]) for dc in range(8)]),
                              reads=[t_w, t_ut[s]], writes=[PB[bq]])
                        kb.op("act", lambda e, gsl=gsl, bq=bq: e.copy(out=QA[0:64, gsl], in_=ps[0:64, bq, :]), reads=[PB[bq]], writes=[t_Q[gt]])
                        kb.op("pe", _mm_group(ps[0:64, bk, :], [(wk[:, dc, :], ut[s][:, dc, :]) for dc in range(8)]),
                              reads=[t_w, t_ut[s]], writes=[PB[bk]])
                        kb.op("dve", lambda e, gsl=gsl, bk=bk: e.tensor_copy(out=KA[0:64, gsl], in_=ps[0:64, bk, :]), reads=[PB[bk]], writes=[t_K[gt]])

                        def vproj(e, s=s, bv=bv):
                            for blk in range(4):
                                for dc in range(8):
                                    ins = e.matmul(ps[:, bv, blk * 66:(blk + 1) * 66], lhsT=ut[s][:, dc, blk * 128:(blk + 1) * 128],
                                                   rhs=wvf[:, dc, :], start=(dc == 0), stop=(dc == 7))
                            return ins

                        kb.op("pe", vproj, reads=[t_w, t_ut[s]], writes=[PB[bv]])
                        pv3 = ps[:, bv, 0:264].rearrange("p (b d) -> p b d", b=4)
                        kb.op("act", lambda e, gt=gt, pv3=pv3: e.copy(out=V[:, gt * 4:(gt + 1) * 4, 0:64], in_=pv3[:, :, 0:64]),
                              reads=[PB[bv]], writes=[t_V[gt]])
                        kb.op("act", lambda e, gt=gt, hh=hh, pv3=pv3: e.copy(out=zf[:, gt * 4:(gt + 1) * 4], in_=pv3[:, :, 64 + hh]),
                              reads=[PB[bv]], writes=[t_zf])
                        if t0s <= gt < t0s + 8:
                            kb.op("pe", _mm_group(ps[:, bs, :], [(ws[:, dc, :], ut[s][:, dc, :]) for dc in range(8)]),
                                  reads=[t_ws, t_ut[s]], writes=[PB[bs]])
                            kb.op("act", lambda e, gt=gt, t0s=t0s, bs=bs: e.copy(out=S.sT[:, (gt - t0s) * 512:(gt - t0s + 1) * 512], in_=ps[:, bs, :]),
                                  reads=[PB[bs]], writes=[S.t_sT])
                        Dp.replay(kb, (n_prep * (n + 1)) // 12)
                    Dp.replay(kb, n_prep)
                    kb.op("act", lambda e, hh=hh: e.activation(out=sm[:, 0, :], in_=zf[:], func=AF.Sigmoid, bias=bfbc[:, hh:hh + 1], scale=1.0),
                          reads=[t_zf, t_bf], writes=[t_sm])
                    kb.op("act", lambda e: e.activation(out=sm[:, 0, :], in_=sm[:, 0, :], func=AF.Ln), reads=[t_sm], writes=[t_sm])

                    def cums(e):
                        e.matmul(ps[:, 4, 0:64], lhsT=triF, rhs=sm[:, 0, :], start=True, stop=True)
                        return e.matmul(ps[:, 4, 64:128], lhsT=onesF, rhs=sm[:, 0, :], start=True, stop=True)

                    kb.op("pe", cums, reads=[t_sm, t_c], writes=[PB[4]])
                    kb.op("dve", lambda e: e.tensor_copy(out=sm[:, 1:3, :], in_=ps[:, 4, 0:128].rearrange("p (a b) -> p a b", a=2)),
                          reads=[PB[4]], writes=[t_sm])
                    kb.op("dve", lambda e: e.tensor_tensor_scan(out=sm[:, 3, :], data0=sm[:, 7, :], data1=sm[:, 2, :], initial=0.0,
                                                                op0=ALU.mult, op1=ALU.add), reads=[t_sm], writes=[t_sm])
                    kb.op("dve", lambda e: e.tensor_tensor(out=sm[:, 4, :], in0=sm[:, 3, :], in1=sm[:, 2, :], op=ALU.subtract),
                          reads=[t_sm], writes=[t_sm])
                    kb.op("dve", lambda e: e.tensor_tensor(out=sm[:, 5, :], in0=sm[:, 1, :], in1=sm[:, 4, :], op=ALU.add),
                          reads=[t_sm], writes=[t_sm])
                    offs4 = sm[:, 4, :].rearrange("p (i f) -> p i f", f=4)
                    kb.op("dve", lambda e: e.tensor_tensor(out=sm[:, 6, :].rearrange("p (i f) -> p i f", f=4),
                                                           in0=sm[:, 5, :].rearrange("p (i f) -> p i f", f=4),
                                                           in1=offs4[:, :, 0:1].broadcast_to([128, 16, 4]), op=ALU.subtract),
                          reads=[t_sm], writes=[t_sm])
                    kb.op("dve", lambda e: e.tensor_scalar(out=sm[:, 6, :], in0=sm[:, 6, :], scalar1=8.0, scalar2=None, op0=ALU.mult),
                          reads=[t_sm], writes=[t_sm])
                    kb.op("pe", lambda e: e.transpose(ps[0:64, 5, 0:128], sm[:, 6, :], ident), reads=[t_sm, t_c], writes=[PB[5]])
                    kb.op("act", lambda e: e.copy(out=b8T[:], in_=ps[0:64, 5, 0:128]), reads=[PB[5]], writes=[t_b8T])
                    kb.dma("sp", lambda e, sem: e.dma_start(out=QA[64:65, :].rearrange("o (b t) -> o b t", t=128), in_=b8T[:]).then_inc(sem, 16),
                           1, reads=[t_b8T], writes=[t_qrow])
                    for i in range(16):
                        nk = 4 * i + 4
                        kb.op("dve", lambda e, i=i, nk=nk: e.tensor_scalar(out=biasT[:, i, 0:nk], in0=sm[:, 5, 0:nk], scalar1=-1.0,
                                                                           scalar2=sm[:, 4, 4 * i:4 * i + 1], op0=ALU.mult, op1=ALU.add),
                              reads=[t_sm], writes=[t_bias])
                        if CLAMP:
                            kb.op("act", lambda e, i=i: e.activation(out=clampT[:, i, :], in_=biasT[:, i, 4 * i:4 * i + 4], func=AF.Identity,
                                                                     scale=-8.0, bias=240.0),
                                  reads=[t_bias], writes=[t_bias])
                    D = Deferred()
                    ssm_emit(D, S, list(range(t0s, t0s + 8)), t0s)
                    n_ssm = len(D.q)
                    steps_total = 544
                    steps = 0
                    mv = [m_.rearrange("(a p) t -> p a t", p=64) for m_ in mix_loc]
                    norm_late = []
                    for i in range(16):
                        nk = 4 * i + 4
                        qsl = slice(i * 512, (i + 1) * 512)
                        ob = 3 + (i % 2)

                        def s_op(kk, i=i, qsl=qsl):
                            c0 = max(0, kk - 4 * i) * 128
                            diag = kk >= 4 * i

                            def f(e, kk=kk, qsl=qsl, c0=c0, diag=diag):
                                ins = e.matmul(ps[:, kk % 3, c0:512], lhsT=KA[0:65, kk * 128:(kk + 1) * 128],
                                               rhs=QA[0:65, qsl.start + c0:qsl.stop], start=True, stop=not diag)
                                if diag:
                                    ins = e.matmul(ps[:, kk % 3, c0:512], lhsT=mk[:, 0:128], rhs=mk[:, 128:128 + 512 - c0], start=False, stop=True)
                                return ins

                            kb.op("pe", f, reads=[t_K[kk // 4], t_Q[i], t_qrow, t_mk], writes=[PB[kk % 3]])

                        def p_op(kk, i=i):
                            c0 = max(0, kk - 4 * i) * 128
                            kb.op("act", lambda e, kk=kk, i=i, c0=c0: e.activation(out=Pt[kk % 3][:, c0:512], in_=ps[:, kk % 3, c0:512], func=AF.Exp,
                                                                                    bias=biasT[:, i, kk:kk + 1], scale=0.125),
                                  reads=[PB[kk % 3], t_bias], writes=[t_P[kk % 3]])

                        def pv_op(kk, ob=ob, nk=nk, i=i):
                            c0 = max(0, kk - 4 * i) * 128
                            kb.op("pe", lambda e, kk=kk, ob=ob, nk=nk, c0=c0: e.matmul(ps[:, ob, c0:512], lhsT=V[:, kk, :], rhs=Pt[kk % 3][:, c0:512],
                                                                                        start=(kk == 0), stop=(kk == nk - 1)),
                                  reads=[t_V[kk // 4], t_P[kk % 3]], writes=[PB[ob]])

                        s_op(0)
                        if nk > 1:
                            s_op(1)
                        for kk in range(nk):
                            p_op(kk)
                            if kk + 2 < nk:
                                s_op(kk + 2)
                            pv_op(kk)
                            steps += 1
                            D.replay(kb, (n_ssm * steps) // steps_total)
                            if kk == 3:
                                for f_ in norm_late:
                                    f_()
                                norm_late = []
                        kb.op("act", lambda e, ob=ob: e.activation(out=Rf[64:128, :], in_=ps[64:128, ob, :], func=AF.Ln), reads=[PB[ob]], writes=[t_Rf])
                        kb.op("act", lambda e: e.activation(out=Rf[64:128, :], in_=Rf[64:128, :], func=AF.Exp, scale=-1.0), reads=[t_Rf], writes=[t_Rf])
                        kb.op("act", lambda e, ob=ob: e.copy(out=Osb[:], in_=ps[0:64, ob, :]), reads=[PB[ob]], writes=[t_Osb])

                        def norm_b(i=i, hh=hh):
                            o = i % 2
                            kb.op("pe", lambda e: e.matmul(ps[0:64, 5, :], lhsT=selF, rhs=Rf[:], start=True, stop=True),
                                  reads=[t_Rf, t_c], writes=[PB[5]])
                            kb.op("dve", lambda e, o=o: e.tensor_tensor(out=ot[o][:], in0=Osb[:], in1=ps[0:64, 5, :], op=ALU.mult),
                                  reads=[t_Osb, PB[5]], writes=[t_ot[o]])
                            t_wr = Tok()
                            kb.dma("sp", lambda e, sem, o=o, i=i, hh=hh: e.dma_start(out=mv[i // 2][:, hh, (i % 2) * 512:(i % 2) * 512 + 512], in_=ot[o][:]).then_inc(sem, 16),
                                   1, reads=[t_ot[o]], writes=[t_wr])
                            mix_written(i // 2, t_wr)

                        norm_late.append(norm_b)
                    for f_ in norm_late:
                        f_()
                    D.replay(kb, n_ssm)
                    kb.flush()
            mixst.close()

        if do3:
            if mode == "full":
                hstack = ExitStack()
                hT = sb(hstack, "hT3", [128, 8, 2048], F32)
                h1T_d = hT_park
            def reload_h(tts, after):
                if mode in ("s3", "full"):
                    hv = h1T_d.rearrange("(c p) t -> p c t", p=128)
                    for tt in tts:
                        for hf in range(2):
                            kb.dma("sp", lambda e, sem, tt=tt, hf=hf: e.dma_start(out=hT[:, hf * 4:hf * 4 + 4, tt * 512:(tt + 1) * 512],
                                                                                  in_=hv[:, hf * 4:hf * 4 + 4, tt * 512:(tt + 1) * 512]).then_inc(sem, 16), 1,
                                   reads=[t_park] + after, writes=[t_h[dc][tt] for dc in range(hf * 4, hf * 4 + 4)])

            with ExitStack() as st:
                Bs3 = [make_norm_bufs(st), make_norm_bufs(st)]
                idxs = sb(st, "idxs", [128, 16], mybir.dt.uint32)
                bglu = sb(st, "bglu_s", [128, 4], F32)
                wglub = sb(st, "wglub", [128, 4, 512], BF16)
                woutb = sb(st, "woutb", [128, 8, 1024], BF16)
                t_w = Tok()
                selp = [sb(st, "selp%d" % i, [128, 8, 1024], F32) for i in range(2)]
                t_selp = [[Tok(), Tok()] for _ in range(2)]
                ygb2 = [sb(st, "ygb%d" % i, [128, 4, 512], BF16) for i in range(2)]; t_ygb2 = [Tok(), Tok()]
                gt2 = [sb(st, "gate%d" % i, [128, 512], F32) for i in range(2)]; t_gt2 = [Tok(), Tok()]
                mxb2 = [sb(st, "mxb%d" % i, [128, 8, 512], BF16) for i in range(2)]; t_mx2 = [Tok(), Tok()]

                def ldw3(e, sem):
                    e.dma_start(out=idxs[:], in_=idx_d).then_inc(sem, 16)
                    e.dma_start(out=bglu[:], in_=bglu_d).then_inc(sem, 16)

                t_w0 = Tok()
                kb.dma("sp", ldw3, 2, writes=[t_w0])

                def ldw3b(e, sem):
                    e.dma_start(out=wglub[:], in_=wglu_d.rearrange("(c p) f -> p c f", p=128)).then_inc(sem, 16)
                    e.dma_start(out=woutb[:, 0:4, :], in_=wout_d.rearrange("(c p) f -> p c f", p=128)[:, 0:4, :]).then_inc(sem, 16)
                    e.dma_start(out=woutb[:, 4:8, :], in_=wout_d.rearrange("(c p) f -> p c f", p=128)[:, 4:8, :]).then_inc(sem, 16)

                kb.dma("pool", ldw3b, 3, writes=[t_w])
                for pp in range(2):
                    def gat(e, sem, pp=pp):
                        for r in range(4):
                            for two in range(2):
                                col = (pp * 4 + r) * 2 + two
                                e.indirect_dma_start(out=selp[pp][:, two * 4 + r, :], out_offset=None, in_=mix_big,
                                                     in_offset=bass.IndirectOffsetOnAxis(ap=idxs[:, col:col + 1], axis=0)).then_inc(sem, 16)

                    kb.dma("pool", gat, 8, reads=[t_w0] + ([t_mall[k] for k in range(pp, 8, 2)] if mode == "full" else []), writes=t_selp[pp])
                    reload_h([2 * pp, 2 * pp + 1], [])
                def chain(K, tt):
                    pp, hq = tt // 2, tt % 2
                    sel = selp[pp][:, :, hq * 512:(hq + 1) * 512]
                    t_sel = t_selp[pp][hq]
                    ygb, t_ygb = ygb2[tt % 2], t_ygb2[tt % 2]
                    mxb, t_mx = mxb2[tt % 2], t_mx2[tt % 2]
                    B = Bs3[tt % 2]
                    K.op("act", lambda e, ygb=ygb, sel=sel: e.copy(out=ygb[:], in_=sel[:, 4:8, :]), reads=[t_sel], writes=[t_ygb])
                    rstd_from(B, sel[:, 0:4, :], [t_sel], 512.0, K=K)
                    for k in range(4):
                        K.op("dve", lambda e, k=k, mxb=mxb, sel=sel, B=B: e.scalar_tensor_tensor(out=mxb[:, k, :], in0=sel[:, k, :], scalar=gcols[:, 5, k:k + 1],
                                                                                                 in1=B.rt[:], op0=ALU.mult, op1=ALU.mult),
                             reads=[t_sel, B.t_rt, t_c], writes=[t_mx])
                    for cp in range(4):
                        b = cp % 2
                        gt_, t_gt = gt2[cp % 2], t_gt2[cp % 2]
                        K.op("pe", _mm_group(ps[:, b, :], [(wglub[:, c, cp * 128:(cp + 1) * 128], ygb[:, c, :]) for c in range(4)]),
                             reads=[t_w, t_ygb], writes=[PB[b]])
                        K.op("act", lambda e, b=b, cp=cp, gt_=gt_: e.activation(out=gt_[:], in_=ps[:, b, :], func=AF.Sigmoid, bias=bglu[:, cp:cp + 1], scale=1.0),
                             reads=[PB[b], t_w0], writes=[t_gt])
                        K.op("dve", lambda e, cp=cp, sel=sel, gt_=gt_: e.tensor_tensor(out=sel[:, 4 + cp, :], in0=sel[:, 4 + cp, :], in1=gt_[:], op=ALU.mult),
                             reads=[t_gt, t_sel], writes=[t_sel])
                    rstd_from(B, sel[:, 4:8, :], [t_sel], 512.0, K=K)
                    for k in range(4, 8):
                        K.op("dve", lambda e, k=k, mxb=mxb, sel=sel, B=B: e.scalar_tensor_tensor(out=mxb[:, k, :], in0=sel[:, k, :], scalar=gcols[:, 5, k:k + 1],
                                                                                                 in1=B.rt[:], op0=ALU.mult, op1=ALU.mult),
                             reads=[t_sel, B.t_rt, t_c], writes=[t_mx])

                chain(kb, 0)
                for tt in range(4):
                    sl = slice(tt * 512, (tt + 1) * 512)
                    mxb, t_mx = mxb2[tt % 2], t_mx2[tt % 2]
                    Dn = Deferred()
                    if tt < 3:
                        chain(Dn, tt + 1)
                    nq = len(Dn.q)
                    for dp in range(8):
                        b = 2 + (dp % 2)
                        kb.op("pe", _mm_group(ps[:, b, :], [(woutb[:, k, dp * 128:(dp + 1) * 128], mxb[:, k, :]) for k in range(8)]),
                              reads=[t_w, t_mx], writes=[PB[b]])
                        kb.op("dve", lambda e, b=b, dp=dp, sl=sl: e.tensor_tensor(out=hT[:, dp, sl], in0=ps[:, b, :], in1=hT[:, dp, sl], op=ALU.add),
                              reads=[PB[b], t_h[dp][tt]], writes=[t_h[dp][tt]])
                        Dn.replay(kb, (nq * (dp + 1)) // 7)
                    Dn.replay(kb, nq)
                kb.flush()
            with ExitStack() as st:
                B = make_ffn_bufs(st)
                ffn(B, w1b, w3b, w2b_d, 2)
                kb.flush()
            with ExitStack() as st:
                Bs = [make_norm_bufs(st), make_norm_bufs(st)]
                xn4 = [sb(st, "xn3_%d" % i, [128, 8, 512], BF16) for i in range(4)]; t_xn4 = [Tok() for _ in range(4)]
                wgb = sb(st, "wgb", [128, 8, 1024], BF16)
                wpb = sb(st, "wpb", [128, 2, 1024], BF16)
                t_w = Tok()
                pin = [sb(st, "pin%d" % i, [128, 256], F32) for i in range(2)]; t_pin = [Tok(), Tok()]
                pT = sb(st, "pT", [128, 2, 2048], BF16); t_pT = [Tok() for _ in range(4)]
                sg = [sb(st, "sg%d" % i, [128, 512], F32) for i in range(2)]; t_sg = [Tok(), Tok()]
                yT2 = [sb(st, "yT%d" % i, [128, 8, 512], F32) for i in range(2)]; t_yT2 = [Tok(), Tok()]
                otl = [sb(st, "otl%d" % i, [128, 1024], F32) for i in range(2)]; t_otl = [Tok(), Tok()]

                def ldw4(e, sem):
                    wgv = wgate_d.rearrange("(c p) f -> p c f", p=128)
                    e.dma_start(out=wgb[:, 0:4, :], in_=wgv[:, 0:4, :]).then_inc(sem, 16)
                    e.dma_start(out=wgb[:, 4:8, :], in_=wgv[:, 4:8, :]).then_inc(sem, 16)
                    e.dma_start(out=wpb[:], in_=wproj_d.rearrange("(c p) f -> p c f", p=128)).then_inc(sem, 16)

                kb.dma("pool", ldw4, 3, writes=[t_w])
                for tt in range(4):
                    norm_tile(Bs[tt % 2], tt, 3, xn4[tt], 0, t_xn4[tt])
                for tb in range(16):
                    s = tb % 2
                    kb.dma("sp", lambda e, sem, s=s, tb=tb: e.dma_start(out=pin[s][:], in_=p_d[tb * 128:(tb + 1) * 128, :]).then_inc(sem, 16),
                           1, writes=[t_pin[s]])
                    b = 6 + (tb % 2)

                    def trp(e, s=s, b=b):
                        e.transpose(ps[:, b, 0:128], pin[s][:, 0:128], ident)
                        return e.transpose(ps[:, b, 128:256], pin[s][:, 128:256], ident)

                    kb.op("pe", trp, reads=[t_pin[s], t_c], writes=[PB[b]])
                    kb.op("act", lambda e, b=b, tb=tb: e.copy(out=pT[:, :, tb * 128:(tb + 1) * 128],
                                                              in_=ps[:, b, 0:256].rearrange("p (k t) -> p k t", k=2)),
                          reads=[PB[b]], writes=[t_pT[tb // 4]])
                for tt in range(4):
                    sl = slice(tt * 512, (tt + 1) * 512)
                    xn = xn4[tt]
                    for dp in range(8):
                        par = dp % 2
                        bg, bp = par, 2 + par
                        dsl = slice(dp * 128, (dp + 1) * 128)
                        kb.op("pe", _mm_group(ps[:, bg, :], [(wgb[:, k, dsl], xn[:, k, :]) for k in range(8)]),
                              reads=[t_w, t_xn4[tt]], writes=[PB[bg]])
                        kb.op("pe", _mm_group(ps[:, bp, :], [(wpb[:, k, dsl], pT[:, k, sl]) for k in range(2)]),
                              reads=[t_w, t_pT[tt]], writes=[PB[bp]])
                        kb.op("act", lambda e, par=par, bg=bg: e.activation(out=sg[par][:], in_=ps[:, bg, :], func=AF.Sigmoid),
                              reads=[PB[bg]], writes=[t_sg[par]])
                        kb.op("dve", lambda e, par=par, bp=bp: e.tensor_tensor(out=sg[par][:], in0=sg[par][:], in1=ps[:, bp, :], op=ALU.mult),
                              reads=[t_sg[par], PB[bp]], writes=[t_sg[par]])
                        kb.op("pool", lambda e, par=par, dp=dp, sl=sl: e.tensor_tensor(out=hT[:, dp, sl], in0=hT[:, dp, sl], in1=sg[par][:], op=ALU.add),
                              reads=[t_sg[par], t_h[dp][tt]], writes=[t_h[dp][tt]])
                for tt in range(4):
                    yT, t_yT = yT2[tt % 2], t_yT2[tt % 2]
                    norm_tile(Bs[tt % 2], tt, 4, yT, 0, t_yT)
                    for tb in range(4):
                        o = tb % 2
                        for dcg in range(2):
                            b = 4 + dcg

                            def trb(e, tb=tb, dcg=dcg, b=b, yT=yT):
                                for k in range(4):
                                    ins = e.transpose(ps[:, b, k * 128:(k + 1) * 128], yT[:, dcg * 4 + k, tb * 128:(tb + 1) * 128], ident)
                                return ins

                            kb.op("pe", trb, reads=[t_yT, t_c], writes=[PB[b]])
                            if dcg == 0:
                                kb.op("act", lambda e, o=o, b=b: e.copy(out=otl[o][:, 0:512], in_=ps[:, b, :]), reads=[PB[b]], writes=[t_otl[o]])
                            else:
                                kb.op("dve", lambda e, o=o, b=b: e.tensor_copy(out=otl[o][:, 512:1024], in_=ps[:, b, :]), reads=[PB[b]], writes=[t_otl[o]])
                        r0 = (tt * 4 + tb) * 128
                        kb.dma("sp", lambda e, sem, o=o, r0=r0: e.dma_start(out=out_d[r0:r0 + 128, :], in_=otl[o][:]).then_inc(sem, 16),
                               1, reads=[t_otl[o]])
                kb.flush()
        hstack.close()
    return nc


def _gcols(inputs):
    gs = [inputs["g_ffn1"][0], inputs["g_mix"][0], inputs["g_ffn2"][0], inputs["g_ple"][0], inputs["g_final"],
          np.concatenate([inputs["g_attn_out"][0], inputs["g_ssm_out"][0]])]
    out = np.zeros((128, 6, 8), np.float32)
    for i, g in enumerate(gs):
        out[:, i, :] = np.asarray(g, np.float32).reshape(8, 128).T
    return out


def _consts():
    cst = np.zeros((128, 4, 128), np.float32)
    cst[:, 0, :] = np.eye(128, dtype=np.float32)
    cst[:, 1, :] = np.triu(np.ones((128, 128), np.float32))
    cst[:, 2, :] = 1.0
    for m in range(64):
        cst[64 + m, 3, m] = 1.0
    return cst


def _mask_const():
    m = np.zeros((128, 640), np.float32)
    r = np.arange(128)
    m[:, 0:128] = np.where(r[None, :] > r[:, None], -30000.0, 0.0)
    m[:, 128:256] = np.eye(128, dtype=np.float32)
    return m


def _mixer_params(inputs, j):
    w_in = inputs["w_in"][0]
    win = np.zeros((1024, 520), np.float32)
    for hh in range(2):
        c0 = 194 * hh
        hd = 128 * j + 64 * hh
        win[:, c0:c0 + 64] = w_in[:, hd:hd + 64]
        win[:, c0 + 64:c0 + 128] = w_in[:, 512 + hd:512 + hd + 64]
        win[:, c0 + 128:c0 + 192] = w_in[:, 1024 + hd:1024 + hd + 64]
        win[:, c0 + 192:c0 + 194] = w_in[:, 1536 + 2 * j:1536 + 2 * j + 2]
    win[:, 388:516] = w_in[:, 1544 + 128 * j:1544 + 128 * (j + 1)]
    bfbc = np.broadcast_to(inputs["b_f"][0][2 * j:2 * j + 2][None, :], (128, 2)).astype(np.float32).copy()
    a_re, a_im, log_dt = inputs["a_re"][0], inputs["a_im"][0], inputs["log_dt"][0]
    b_re, b_im, c_re, c_im = inputs["b_re"][0], inputs["b_im"][0], inputs["c_re"][0], inputs["c_im"][0]
    colp = np.zeros((128, 3, 4), np.float32)
    rowp = np.zeros((3, 512), np.float32)
    braw = np.zeros((128, 2, 512), np.float32)
    craw = np.zeros((128, 2, 512), np.float32)
    dcol = np.zeros((128, 1), np.float32)
    for gl in range(8):
        g = 8 * j + gl
        gp, half = gl // 2, gl % 2
        st = slice(half * 64, half * 64 + 64)
        colp[st, 0, gp] = a_re[g]; colp[st, 1, gp] = a_im[g]; colp[st, 2, gp] = log_dt[g]
        rs = slice(gp * 128 + half * 64, gp * 128 + half * 64 + 64)
        rowp[0, rs] = a_re[g]; rowp[1, rs] = a_im[g]; rowp[2, rs] = log_dt[g]
        ch = slice(16 * gl, 16 * gl + 16)
        braw[ch, 0, rs] = b_re[g].T
        braw[ch, 1, rs] = b_im[g].T
        cs = slice(gp * 128 + 16 * gl, gp * 128 + 16 * gl + 16)
        craw[st, 0, cs] = c_re[g].T
        craw[st, 1, cs] = c_im[g].T
        dcol[ch, 0] = inputs["d_skip"][0][g]
    return dict(win=win, bfbc=bfbc, colp=colp, rowp=rowp, braw=braw, craw=craw, dcol=dcol, maskc=_mask_const())


def make_in_maps(inputs, mode="full"):
    cst = _consts()
    gc = _gcols(inputs)
    f = lambda k: np.ascontiguousarray(inputs[k][0], dtype=np.float32)
    shared = {"cst": cst, "gcols": gc}
    if mode in ("full", "s1"):
        shared.update({"w1_a": f("w1_a"), "w3_a": f("w3_a"), "w2_a": f("w2_a")})
    if mode in ("full", "s3"):
        shared.update({"w_glu": f("w_glu"), "w_out": f("w_out"), "w1_b": f("w1_b"), "w3_b": f("w3_b"), "w2_b": f("w2_b"),
                       "w_gate": f("w_ple_gate"), "w_proj": f("w_ple_proj"),
                       "bglu": np.ascontiguousarray(inputs["b_glu"][0].reshape(4, 128).T, dtype=np.float32)})
    maps = []
    for c in range(8):
        b, j = c // 4, c % 4
        m = dict(shared)
        if mode in ("full", "s1"):
            m["x"] = np.ascontiguousarray(inputs["x"][b, j * 2048:(j + 1) * 2048, :], dtype=np.float32)
        if mode in ("full", "s2"):
            m.update(_mixer_params(inputs, j))
        if mode in ("full", "s3"):
            idx = np.zeros((128, 16), np.uint32)
            for kk in range(2):
                for r in range(4):
                    for two in range(2):
                        idx[:, (kk * 4 + r) * 2 + two] = (((2 * j + kk) * 4 + r) * 2 + two) * 128 + np.arange(128)
            m["idxT"] = idx
            m["p"] = np.ascontiguousarray(inputs["p"][0, b, j * 2048:(j + 1) * 2048, :], dtype=np.float32)
        maps.append(m)
    return maps


def kernel(**inputs):
    inputs = {k: np.asarray(v) for k, v in inputs.items()}
    nc = build("full")
    maps = make_in_maps(inputs, "full")
    res = run_bass_kernel_spmd(nc, maps, core_ids=list(range(8)))
    out = np.zeros((2, 8192, 1024), np.float32)
    for c in range(8):
        b, j = c // 4, c % 4
        out[b, j * 2048:(j + 1) * 2048, :] = res.results[c]["out"]
    return out
```

```python
import numpy as np
import concourse.bass as bass
import concourse.mybir as mybir
from concourse.bass_utils import run_bass_kernel_spmd
from contextlib import ExitStack

F32 = mybir.dt.float32
BF16 = mybir.dt.bfloat16
AF = mybir.ActivationFunctionType
ALU = mybir.AluOpType
EPS = 1e-6
GROUPS = [[0, 1, 2, 3], [4, 5, 6, 7]]


class Tok:
    __slots__ = ("w", "r")

    def __init__(self):
        self.w = None
        self.r = []


class KB:
    ENG = ["pe", "act", "dve", "pool", "sp"]

    def __init__(self, nc, stack, n_dma_sems=16):
        self.nc = nc
        self.prog = {e: [] for e in self.ENG}
        self.sems = {e: stack.enter_context(nc.semaphore("s_" + e)) for e in self.ENG}
        self.cnt = {e: 0 for e in self.ENG}
        self.dsems = [stack.enter_context(nc.semaphore("dq%d" % i)) for i in range(n_dma_sems)]
        self.dcnt = [0] * n_dma_sems
        self.dnext = 0
        self.dnext_sw = 0
        self.csem = stack.enter_context(nc.semaphore("cc"))
        self.ccnt = 0
        self.waited = {e: {} for e in self.ENG}
        self.stack = stack
        self.nblk = 0

    def _sem(self, k):
        if k[0] == "e":
            return self.sems[k[1]]
        if k[0] == "c":
            return self.csem
        return self.dsems[k[1]]

    def _deps(self, reads, writes):
        deps = {}

        def add(tok):
            if tok is None:
                return
            k, v = tok
            if deps.get(k, 0) < v:
                deps[k] = v

        for b in reads:
            add(b.w)
        for b in writes:
            add(b.w)
            for t in b.r:
                add(t)
        return deps

    def _emit_waits(self, eng, deps, skip_self):
        for k, v in deps.items():
            if skip_self and k == ("e", eng):
                continue
            if self.waited[eng].get(k, 0) >= v:
                continue
            self.waited[eng][k] = v
            sem = self._sem(k)
            self.prog[eng].append(lambda e, sem=sem, v=v: e.wait_ge(sem, v))

    def _update(self, tok, reads, writes):
        for b in writes:
            b.w = tok
            b.r = []
        for b in reads:
            if b not in writes:
                b.r.append(tok)
                if len(b.r) > 64:
                    b.r = b.r[-64:]

    def op(self, eng, fn, reads=(), writes=()):
        deps = self._deps(reads, writes)
        self._emit_waits(eng, deps, skip_self=(eng == "pe"))
        self.cnt[eng] += 1
        tok = (("e", eng), self.cnt[eng])
        sem = self.sems[eng]
        self.prog[eng].append(lambda e, fn=fn, sem=sem: fn(e).then_inc(sem, 1))
        self._update(tok, reads, writes)
        return tok

    def dma(self, eng, fn, n, reads=(), writes=()):
        deps = self._deps(reads, writes)
        half = len(self.dsems) // 2
        if eng == "pool":
            i = half + self.dnext_sw
            self.dnext_sw = (self.dnext_sw + 1) % (len(self.dsems) - half)
        else:
            i = self.dnext
            self.dnext = (self.dnext + 1) % half
        k = ("d", i)
        if self.dcnt[i] > 0:
            deps[k] = max(deps.get(k, 0), self.dcnt[i])
        self._emit_waits(eng, deps, skip_self=True)
        self.dcnt[i] += 16 * n
        tok = (k, self.dcnt[i])
        sem = self.dsems[i]
        self.prog[eng].append(lambda e, fn=fn, sem=sem: fn(e, sem))
        self._update(tok, reads, writes)
        return tok

    def collective(self, fn, reads=(), writes=()):
        deps = self._deps(reads, writes)
        self._emit_waits("pool", deps, skip_self=False)
        self.ccnt += 1
        tok = (("c", 0), self.ccnt)
        sem = self.csem
        self.prog["pool"].append(lambda e, fn=fn, sem=sem: fn(e).then_inc(sem, 1))
        self._update(tok, reads, writes)
        return tok

    def wait_all(self, eng, toks):
        deps = {}
        for t in toks:
            if t is None:
                continue
            k, v = t
            if deps.get(k, 0) < v:
                deps[k] = v
        self._emit_waits(eng, deps, skip_self=False)

    def flush(self):
        allt = [(("e", e), self.cnt[e]) for e in self.ENG if self.cnt[e] > 0]
        allt += [(("d", i), self.dcnt[i]) for i in range(len(self.dsems)) if self.dcnt[i] > 0]
        for e in self.ENG:
            self.wait_all(e, allt)
        nc = self.nc
        prog = self.prog
        self.nblk += 1
        with nc.Block(no_gpsimd_drain=True) as block:

            @block.tensor
            def _(e):
                for f in prog["pe"]:
                    f(e)

            @block.scalar
            def _(e):
                for f in prog["act"]:
                    f(e)

            @block.vector
            def _(e):
                for f in prog["dve"]:
                    f(e)

            @block.gpsimd
            def _(e):
                for f in prog["pool"]:
                    f(e)

            @block.sync
            def _(e):
                for f in prog["sp"]:
                    f(e)

        self.prog = {e: [] for e in self.ENG}


class Ctx:
    pass


class Deferred:
    def __init__(self):
        self.q = []

    def op(self, *a, **k):
        self.q.append(("op", a, k))

    def dma(self, *a, **k):
        self.q.append(("dma", a, k))

    def call(self, fn):
        self.q.append(("call", (fn,), {}))

    def replay(self, kb, upto):
        while self.pos < min(upto, len(self.q)):
            m, a, k = self.q[self.pos]
            self.pos += 1
            if m == "call":
                a[0]()
            else:
                getattr(kb, m)(*a, **k)

    pos = 0


def _mm_group(ps_ap, pairs):
    def f(e):
        n = len(pairs)
        for i, (l, r) in enumerate(pairs):
            ins = e.matmul(ps_ap, lhsT=l, rhs=r, start=(i == 0), stop=(i == n - 1))
        return ins

    return f


def build(mode="full"):
    nc = bass.Bass("TRN2", target_bir_lowering=False)

    def din(name, shape, dt=F32):
        return nc.dram_tensor(name, list(shape), dt, kind="ExternalInput").ap()

    def dout(name, shape, dt=F32):
        return nc.dram_tensor(name, list(shape), dt, kind="ExternalOutput").ap()

    def dint(name, shape, dt=F32):
        return nc.dram_tensor(name, list(shape), dt).ap()

    do1 = mode in ("full", "s1")
    do2 = mode in ("full", "s2")
    do3 = mode in ("full", "s3")
    cst_d = din("cst", [128, 4, 128])
    gcols_d = din("gcols", [128, 6, 8])
    if do1:
        x = din("x", [2048, 1024])
        w1a = din("w1_a", [1024, 2816]); w3a = din("w3_a", [1024, 2816]); w2a = din("w2_a", [2816, 1024])
    if do2:
        win_d = din("win", [1024, 520])
        mask_d = din("maskc", [128, 640])
        bfbc_d = din("bfbc", [128, 2])
        colp_d = din("colp", [128, 3, 4])
        rowp_d = din("rowp", [3, 512])
        braw_d = din("braw", [128, 2, 512])
        craw_d = din("craw", [128, 2, 512])
        dcol_d = din("dcol", [128, 1])
    if do3:
        idx_d = din("idxT", [128, 16], mybir.dt.uint32)
        wglu_d = din("w_glu", [512, 512]); bglu_d = din("bglu", [128, 4])
        wout_d = din("w_out", [1024, 1024])
        w1b = din("w1_b", [1024, 2816]); w3b = din("w3_b", [1024, 2816]); w2b_d = din("w2_b", [2816, 1024])
        wgate_d = din("w_gate", [1024, 1024]); wproj_d = din("w_proj", [256, 1024])
        p_d = din("p", [2048, 256])
        out_d = dout("out", [2048, 1024])
    if mode == "s1":
        dbg_h = dout("dbg_h", [1024, 2048])
        uT_loc = [dout("dbg_u%d" % t, [1024, 512], BF16) for t in range(4)]
    elif mode == "full":
        uT_loc = [dint("uT_loc%d" % t, [1024, 512], BF16) for t in range(4)]
    if mode == "s2":
        uT_all = [din("uT_all%d" % t, [4096, 512], BF16) for t in range(4)]
        mix_loc = [dout("dbg_mix%d" % k, [256, 1024]) for k in range(8)]
    elif mode == "full":
        uT_all = [dint("uT_all%d" % t, [4096, 512], BF16) for t in range(4)]
        mix_loc = [dint("mix_loc%d" % k, [256, 1024]) for k in range(8)]
    if mode == "s3":
        h1T_d = din("h1T", [1024, 2048])
        mix_big = din("mix_all", [8192, 1024])
    elif mode == "full":
        mix_big = dint("mix_all_big", [8192, 1024])
    if do3:
        mix_all = [mix_big[k * 1024:(k + 1) * 1024, :] for k in range(8)]
    t_uloc = [Tok() for _ in range(4)]; t_uall = [Tok() for _ in range(4)]
    t_mloc = [Tok() for _ in range(8)]; t_mall = [Tok() for _ in range(8)]

    with ExitStack() as top:
        kb = KB(nc, top)

        _uid = [0]

        def sb(st, name, shape, dt):
            _uid[0] += 1
            return st.enter_context(nc.sbuf_tensor("%s_u%d" % (name, _uid[0]), list(shape), dt))

        ps = top.enter_context(nc.psum_tensor("ps", [128, 8, 512], F32))
        PB = [Tok() for _ in range(8)]
        t_h = [[Tok() for _ in range(4)] for _ in range(8)]
        cst = sb(top, "cst_s", [128, 4, 128], F32)
        ident = cst[:, 0, :]
        triF = cst[:, 1, :]
        onesF = cst[:, 2, :]
        selF = cst[:, 3, 0:64]
        ones_bf = sb(top, "ones_bf", [128, 128], BF16)
        gcols = sb(top, "gcols_s", [128, 6, 8], F32)
        epsc = sb(top, "epsc", [128, 1], F32)
        t_c = Tok()
        hstack = ExitStack()
        hT = sb(hstack, "hT", [128, 8, 2048], F32)
        if mode == "full":
            hT_park = dint("hT_park", [1024, 2048])
        t_park = Tok()

        def ldc(e, sem):
            e.dma_start(out=cst[:], in_=cst_d).then_inc(sem, 16)
            e.dma_start(out=gcols[:], in_=gcols_d).then_inc(sem, 16)

        kb.dma("sp", ldc, 2, writes=[t_c])
        kb.op("pool", lambda e: e.memset(ones_bf[:], 1.0), writes=[t_c])
        kb.op("pool", lambda e: e.memset(epsc[:], EPS), writes=[t_c])

        def th_all(tt):
            return [t_h[dc][tt] for dc in range(8)]

        def make_norm_bufs(st, B=None):
            B = B or Ctx()
            B.sq = sb(st, "sq", [128, 8, 512], BF16); B.t_sq = Tok()
            B.rt = sb(st, "rt", [128, 512], F32); B.t_rt = Tok()
            return B

        def make_ffn_bufs(st):
            B = make_norm_bufs(st)
            B.xn = sb(st, "xn", [128, 8, 1024], BF16); B.t_xn = [Tok(), Tok()]
            B.G = sb(st, "G", [128, 22, 1024], BF16); B.t_G = [[Tok(), Tok()] for _ in range(22)]
            B.w2b = sb(st, "w2b", [128, 22, 1024], BF16); B.t_w2 = [Tok() for _ in range(11)]
            B.w1g = [sb(st, "w1g%d" % i, [128, 8, 256], BF16) for i in range(2)]
            B.w3g = [sb(st, "w3g%d" % i, [128, 8, 256], BF16) for i in range(2)]
            B.t_wg = [Tok(), Tok()]
            B.s1 = [sb(st, "s1_%d" % i, [128, 512], F32) for i in range(2)]; B.t_s1 = [Tok(), Tok()]
            return B

        def rstd_from(B, src_chunks, rd, nfeat, K=None):
            K = K or kb
            k = src_chunks.shape[1]
            K.op("act", lambda e: e.activation(out=B.sq[:, 0:k, :], in_=src_chunks, func=AF.Square),
                 reads=rd, writes=[B.t_sq])
            K.op("pe", _mm_group(ps[:, 6, :], [(ones_bf[:], B.sq[:, c, :]) for c in range(k)]),
                 reads=[B.t_sq, t_c], writes=[PB[6]])
            K.op("act", lambda e: e.activation(out=B.rt[:], in_=ps[:, 6, :], func=AF.Ln, bias=epsc[:, 0:1], scale=1.0 / nfeat),
                 reads=[PB[6], t_c], writes=[B.t_rt])
            K.op("act", lambda e: e.activation(out=B.rt[:], in_=B.rt[:], func=AF.Exp, scale=-0.5), reads=[B.t_rt], writes=[B.t_rt])

        def norm_tile(B, tt, gi, dst, doff, t_dst, K=None):
            K = K or kb
            sl = slice(tt * 512, (tt + 1) * 512)
            rstd_from(B, hT[:, :, sl], th_all(tt), 1024.0, K=K)
            for dc in range(8):
                K.op("dve", lambda e, dc=dc: e.scalar_tensor_tensor(
                    out=dst[:, dc, doff:doff + 512], in0=hT[:, dc, sl], scalar=gcols[:, gi, dc:dc + 1],
                    in1=B.rt[:], op0=ALU.mult, op1=ALU.mult),
                    reads=[t_h[dc][tt], B.t_rt, t_c], writes=[t_dst])

        def ffn(B, w1, w3, w2, gi, after_tile=None):
            w1v = w1.rearrange("(c p) f -> p c f", p=128)
            w3v = w3.rearrange("(c p) f -> p c f", p=128)
            w2v = w2.rearrange("(c p) d -> p c d", p=128)
            def load_w2():
                for k in range(11):
                    kb.dma("pool", lambda e, sem, k=k: e.dma_start(out=B.w2b[:, 2 * k:2 * k + 2, :], in_=w2v[:, 2 * k:2 * k + 2, :]).then_inc(sem, 16),
                           1, writes=[B.t_w2[k]])

            late = []
            for th in range(2):
                for t2 in range(2):
                    norm_tile(B, th * 2 + t2, gi, B.xn, t2 * 512, B.t_xn[t2])
                for fg in range(11):
                    s = fg % 2

                    def ldw(e, sem, s=s, fg=fg):
                        e.dma_start(out=B.w1g[s][:], in_=w1v[:, :, fg * 256:(fg + 1) * 256]).then_inc(sem, 16)
                        e.dma_start(out=B.w3g[s][:], in_=w3v[:, :, fg * 256:(fg + 1) * 256]).then_inc(sem, 16)

                    kb.dma("pool", ldw, 2, writes=[B.t_wg[s]])
                    if th == 0 and fg == 1:
                        load_w2()
                    if fg == 10:
                        for f_ in late:
                            f_()
                        late = []
                    for f2 in range(2):
                        fc = fg * 2 + f2
                        for t2 in range(2):
                            par = (fc * 2 + t2) % 2
                            b1, b3 = par, 2 + par
                            tsl = slice(t2 * 512, (t2 + 1) * 512)
                            fsl = slice(f2 * 128, (f2 + 1) * 128)
                            kb.op("pe", _mm_group(ps[:, b1, :], [(B.w1g[s][:, dc, fsl], B.xn[:, dc, tsl]) for dc in range(8)]),
                                  reads=[B.t_wg[s], B.t_xn[t2]], writes=[PB[b1]])
                            kb.op("pe", _mm_group(ps[:, b3, :], [(B.w3g[s][:, dc, fsl], B.xn[:, dc, tsl]) for dc in range(8)]),
                                  reads=[B.t_wg[s], B.t_xn[t2]], writes=[PB[b3]])
                            kb.op("act", lambda e, par=par, b1=b1: e.activation(out=B.s1[par][:], in_=ps[:, b1, :], func=AF.Silu),
                                  reads=[PB[b1]], writes=[B.t_s1[par]])
                            kb.op("dve", lambda e, par=par, b3=b3, fc=fc, tsl=tsl: e.tensor_tensor(
                                out=B.G[:, fc, tsl], in0=B.s1[par][:], in1=ps[:, b3, :], op=ALU.mult),
                                reads=[B.t_s1[par], PB[b3]], writes=[B.t_G[fc][t2]])
                for t2 in range(2):
                    tt = th * 2 + t2
                    sl = slice(tt * 512, (tt + 1) * 512)
                    tsl = slice(t2 * 512, (t2 + 1) * 512)
                    for dp in range(8):
                        b = 4 + (dp % 2)
                        kb.op("pe", _mm_group(ps[:, b, :], [(B.w2b[:, fc, dp * 128:(dp + 1) * 128], B.G[:, fc, tsl]) for fc in range(22)]),
                              reads=B.t_w2 + [B.t_G[fc][t2] for fc in range(22)], writes=[PB[b]])
                        kb.op("dve", lambda e, b=b, dp=dp, sl=sl: e.scalar_tensor_tensor(
                            out=hT[:, dp, sl], in0=ps[:, b, :], scalar=0.5, in1=hT[:, dp, sl], op0=ALU.mult, op1=ALU.add),
                            reads=[PB[b], t_h[dp][tt]], writes=[t_h[dp][tt]])
                    if after_tile is not None:
                        late += after_tile(tt)
            for f_ in late:
                f_()

        if do1:
            with ExitStack() as st:
                xin = [sb(st, "xin%d" % i, [128, 1024], F32) for i in range(2)]
                t_xin = [Tok(), Tok()]
                for tb in range(16):
                    s = tb % 2
                    kb.dma("sp", lambda e, sem, s=s, tb=tb: e.dma_start(out=xin[s][:], in_=x[tb * 128:(tb + 1) * 128, :]).then_inc(sem, 16),
                           1, writes=[t_xin[s]])
                    tt = tb // 4
                    for dcg in range(2):
                        b = 6 + dcg

                        def tr(e, s=s, dcg=dcg, b=b):
                            for k in range(4):
                                dc = dcg * 4 + k
                                ins = e.transpose(ps[:, b, k * 128:(k + 1) * 128], xin[s][:, dc * 128:(dc + 1) * 128], ident)
                            return ins

                        kb.op("pe", tr, reads=[t_xin[s], t_c], writes=[PB[b]])
                        dst = hT[:, dcg * 4:(dcg + 1) * 4, tb * 128:(tb + 1) * 128]
                        src = ps[:, b, :].rearrange("p (k t) -> p k t", k=4)
                        wr = [t_h[dc][tt] for dc in range(dcg * 4, dcg * 4 + 4)]
                        if dcg == 0:
                            kb.op("act", lambda e, dst=dst, src=src: e.copy(out=dst, in_=src), reads=[PB[b]], writes=wr)
                        else:
                            kb.op("dve", lambda e, dst=dst, src=src: e.tensor_copy(out=dst, in_=src), reads=[PB[b]], writes=wr)
                kb.flush()
            with ExitStack() as st:
                B = make_ffn_bufs(st)
                def emit_u(tt):
                    late = []
                    t2 = tt % 2
                    norm_tile(B, tt, 1, B.xn, t2 * 512, B.t_xn[t2])
                    uv = uT_loc[tt].rearrange("(c p) t -> p c t", p=128)
                    kb.dma("sp", lambda e, sem, uv=uv, t2=t2: e.dma_start(out=uv, in_=B.xn[:, :, t2 * 512:(t2 + 1) * 512]).then_inc(sem, 16),
                           1, reads=[B.t_xn[t2]], writes=[t_uloc[tt]])
                    if mode == "full":
                        late.append(lambda tt=tt: kb.collective(
                            lambda e, tt=tt: e.collective_compute("AllGather", ALU.bypass, replica_groups=GROUPS, dma_qos="P3",
                                                                  ins=[uT_loc[tt].opt()], outs=[uT_all[tt].opt()]),
                            reads=[t_uloc[tt]], writes=[t_uall[tt]]))
                        hpv = hT_park.rearrange("(c p) t -> p c t", p=128)
                        kb.dma("sp", lambda e, sem, tt=tt: e.dma_start(out=hpv[:, :, tt * 512:(tt + 1) * 512], in_=hT[:, :, tt * 512:(tt + 1) * 512]).then_inc(sem, 16), 1,
                               reads=th_all(tt), writes=[t_park])
                    return late

                ffn(B, w1a, w3a, w2a, 0, after_tile=emit_u)
                if mode == "s1":
                    hv = dbg_h.rearrange("(c p) t -> p c t", p=128)
                    for dc in range(8):
                        kb.dma("sp", lambda e, sem, dc=dc: e.dma_start(out=hv[:, dc, :], in_=hT[:, dc, :]).then_inc(sem, 16), 1,
                               reads=[t_h[dc][tt] for tt in range(4)])
                kb.flush()

        if do2:
            hstack.close()
            uav = [u_.rearrange("(r c p) t -> p r c t", r=4, p=128) for u_ in uT_all]
            winv = win_d.rearrange("(c p) f -> p c f", p=128)
            mixst = ExitStack()
            BRb = sb(mixst, "BRb", [128, 512], BF16); BIb = sb(mixst, "BIb", [128, 512], BF16)
            CRb = sb(mixst, "CRb", [128, 512], BF16); CRnb = sb(mixst, "CRnb", [128, 512], BF16); CInb = sb(mixst, "CInb", [128, 512], BF16)
            COS = sb(mixst, "COS", [128, 4, 512], F32); SIN = sb(mixst, "SIN", [128, 4, 512], F32)
            Cw = sb(mixst, "Cw", [128, 16, 4], F32)
            dcol = sb(mixst, "dcol", [128, 1], F32)
            ws = sb(mixst, "ws", [128, 8, 128], BF16)
            t_R = Tok(); t_p = Tok(); t_ws = Tok()
            mk = sb(mixst, "mk", [128, 640], BF16); t_mk = Tok()
            kb.dma("pool", lambda e, sem: e.dma_start(out=mk[:], in_=mask_d).then_inc(sem, 16), 1, writes=[t_mk])
            t_zi = [[Tok() for _ in range(4)] for _ in range(2)]
            mix_w = [[] for _ in range(8)]

            def mix_written(k, tok):
                mix_w[k].append(tok)
                if len(mix_w[k]) == 6 and mode == "full":
                    kb.collective(lambda e, k=k: e.collective_compute("AllGather", ALU.bypass, replica_groups=GROUPS, dma_qos="P3",
                                                                      ins=[mix_loc[k].opt()], outs=[mix_all[k].opt()]),
                                  reads=mix_w[k], writes=[t_mall[k]])

            kb.dma("pool", lambda e, sem: e.dma_start(out=ws[:], in_=winv[:, :, 388:516]).then_inc(sem, 16), 1, writes=[t_ws])

            def emit_prep(K, st):
                rowp = sb(st, "rowp", [128, 3, 512], F32)
                colp = sb(st, "colp", [128, 3, 4], F32)
                braw = sb(st, "braw", [128, 2, 512], F32)
                craw = sb(st, "craw", [128, 2, 512], F32)

                def ldp(e, sem):
                    e.dma_start(out=rowp[:], in_=rowp_d.partition_broadcast(128)).then_inc(sem, 16)
                    e.dma_start(out=colp[:], in_=colp_d).then_inc(sem, 16)
                    e.dma_start(out=braw[:], in_=braw_d).then_inc(sem, 16)
                    e.dma_start(out=craw[:], in_=craw_d).then_inc(sem, 16)
                    e.dma_start(out=dcol[:], in_=dcol_d).then_inc(sem, 16)

                K.dma("sp", ldp, 5, writes=[t_p])
                R = sb(st, "Rw", [128, 12, 512], F32)
                TT = sb(st, "TT", [128, 4, 4, 256], F32)
                RD = [t_p, t_R]

                def dv(fn):
                    K.op("dve", fn, reads=RD, writes=[t_R])

                def ac(fn):
                    K.op("act", fn, reads=RD, writes=[t_R])

                def tt_(o, a, b, op):
                    dv(lambda e: e.tensor_tensor(out=o, in0=a, in1=b, op=op))

                def csq(c, s, t0, t1, t2):
                    tt_(t0, c, c, ALU.mult)
                    tt_(t1, s, s, ALU.mult)
                    tt_(t2, c, s, ALU.mult)
                    tt_(c, t0, t1, ALU.subtract)
                    dv(lambda e: e.tensor_scalar(out=s, in0=t2, scalar1=2.0, scalar2=None, op0=ALU.mult))

                hp_t = sb(st, "hp_t", [128, 1], F32)
                K.op("pool", lambda e: e.memset(hp_t[:], float(np.pi / 2)), writes=[t_R])
                halfpi = hp_t[:, 0:1]

                def cossin(th, c, s, t0, t1, t2):
                    ac(lambda e: e.activation(out=s, in_=th, func=AF.Sin, scale=1.0 / 16))
                    ac(lambda e: e.activation(out=c, in_=th, func=AF.Sin, scale=1.0 / 16, bias=halfpi))
                    for _ in range(4):
                        csq(c, s, t0, t1, t2)

                arR, aiR, ldR = rowp[:, 0, :], rowp[:, 1, :], rowp[:, 2, :]
                ac(lambda e: e.activation(out=R[:, 0, :], in_=ldR, func=AF.Exp))
                tt_(R[:, 1, :], R[:, 0, :], aiR, ALU.mult)
                tt_(R[:, 2, :], R[:, 0, :], arR, ALU.mult)
                ac(lambda e: e.activation(out=R[:, 2, :], in_=R[:, 2, :], func=AF.Exp))
                cossin(R[:, 1, :], R[:, 3, :], R[:, 4, :], R[:, 5, :], R[:, 6, :], R[:, 7, :])
                tt_(R[:, 3, :], R[:, 3, :], R[:, 2, :], ALU.mult)
                tt_(R[:, 4, :], R[:, 4, :], R[:, 2, :], ALU.mult)
                dv(lambda e: e.tensor_scalar(out=R[:, 3, :], in0=R[:, 3, :], scalar1=-1.0, scalar2=None, op0=ALU.add))
                tt_(R[:, 5, :], arR, arR, ALU.mult)
                tt_(R[:, 6, :], aiR, aiR, ALU.mult)
                tt_(R[:, 10, :], R[:, 5, :], R[:, 6, :], ALU.add)
                dv(lambda e: e.reciprocal(out=R[:, 10, :], in_=R[:, 10, :]))
                tt_(R[:, 5, :], R[:, 3, :], arR, ALU.mult)
                tt_(R[:, 6, :], R[:, 4, :], aiR, ALU.mult)
                tt_(R[:, 8, :], R[:, 5, :], R[:, 6, :], ALU.add)
                tt_(R[:, 8, :], R[:, 8, :], R[:, 10, :], ALU.mult)
                tt_(R[:, 5, :], R[:, 4, :], arR, ALU.mult)
                tt_(R[:, 6, :], R[:, 3, :], aiR, ALU.mult)
                tt_(R[:, 9, :], R[:, 5, :], R[:, 6, :], ALU.subtract)
                tt_(R[:, 9, :], R[:, 9, :], R[:, 10, :], ALU.mult)
                tt_(R[:, 5, :], R[:, 8, :], braw[:, 0, :], ALU.mult)
                tt_(R[:, 6, :], R[:, 9, :], braw[:, 1, :], ALU.mult)
                tt_(BRb[:], R[:, 5, :], R[:, 6, :], ALU.subtract)
                tt_(R[:, 5, :], R[:, 8, :], braw[:, 1, :], ALU.mult)
                tt_(R[:, 6, :], R[:, 9, :], braw[:, 0, :], ALU.mult)
                tt_(BIb[:], R[:, 5, :], R[:, 6, :], ALU.add)
                dv(lambda e: e.tensor_copy(out=CRb[:], in_=craw[:, 0, :]))
                dv(lambda e: e.tensor_scalar(out=CRnb[:], in0=craw[:, 0, :], scalar1=-1.0, scalar2=None, op0=ALU.mult))
                dv(lambda e: e.tensor_scalar(out=CInb[:], in0=craw[:, 1, :], scalar1=-1.0, scalar2=None, op0=ALU.mult))
                arC, aiC, ldC = colp[:, 0, :], colp[:, 1, :], colp[:, 2, :]
                ac(lambda e: e.activation(out=Cw[:, 0, :], in_=ldC, func=AF.Exp))
                tt_(Cw[:, 1, :], Cw[:, 0, :], aiC, ALU.mult)
                tt_(Cw[:, 2, :], Cw[:, 0, :], arC, ALU.mult)
                ac(lambda e: e.activation(out=Cw[:, 2, :], in_=Cw[:, 2, :], func=AF.Exp))
                cossin(Cw[:, 1, :], Cw[:, 3, :], Cw[:, 4, :], Cw[:, 5, :], Cw[:, 6, :], Cw[:, 7, :])
                K.op("pool", lambda e: e.memset(COS[:, :, 0:1], 1.0), reads=RD, writes=[t_R])
                K.op("pool", lambda e: e.memset(SIN[:, :, 0:1], 0.0), reads=RD, writes=[t_R])
                for m in range(9):
                    n = 1 << m
                    cm = Cw[:, 3, :].unsqueeze(2).broadcast_to([128, 4, n])
                    sm_ = Cw[:, 4, :].unsqueeze(2).broadcast_to([128, 4, n])
                    a0, a1, a2, a3 = (TT[:, k, :, 0:n] for k in range(4))
                    tt_(a0, COS[:, :, 0:n], cm, ALU.mult)
                    tt_(a1, SIN[:, :, 0:n], sm_, ALU.mult)
                    tt_(a2, COS[:, :, 0:n], sm_, ALU.mult)
                    tt_(a3, SIN[:, :, 0:n], cm, ALU.mult)
                    tt_(COS[:, :, n:2 * n], a0, a1, ALU.subtract)
                    tt_(SIN[:, :, n:2 * n], a2, a3, ALU.add)
                    csq(Cw[:, 3, :], Cw[:, 4, :], Cw[:, 5, :], Cw[:, 6, :], Cw[:, 7, :])
                K.op("pool", lambda e: e.memset(Cw[:, 8:12, :], 0.0), reads=RD, writes=[t_R])

            def load_ut(ut, t_ut, n, gt):
                s = n % 2
                r, lt = gt // 4, gt % 4
                kb.dma("sp", lambda e, sem: e.dma_start(out=ut[s][:], in_=uav[lt][:, r, :, :]).then_inc(sem, 16),
                       1, reads=[t_uall[lt]], writes=[t_ut[s]])
                return s

            def ssm_emit(K, S, tiles, t0):
                pending = None

                def back(gt, q):
                    g2 = q % 2
                    P16 = S.pr[g2]; TP = S.t_pr[g2]

                    def cproj(e, P16=P16):
                        n = 0
                        for gp in range(4):
                            gsl = slice(gp * 128, (gp + 1) * 128)
                            for (w, idx) in ((CRb, 0), (CRnb, 1), (CInb, 2), (CInb, 3)):
                                ins = e.matmul(ps[:, 5, :], lhsT=w[:, gsl], rhs=P16[gp * 4 + idx][:], start=(n == 0), stop=(n == 15))
                                n += 1
                        return ins

                    ssl = slice((gt - t0) * 512, (gt - t0 + 1) * 512)

                    def y_unit(cproj=cproj, TP=TP, g2=g2, ssl=ssl):
                        kb.op("pe", cproj, reads=TP + [t_R], writes=[PB[5]])
                        kb.op("dve", lambda e, g2=g2, ssl=ssl: e.scalar_tensor_tensor(out=S.yp[g2][:], in0=S.sT[:, ssl], scalar=dcol[:, 0:1], in1=ps[:, 5, :],
                                                                                    op0=ALU.mult, op1=ALU.add),
                              reads=[PB[5], S.t_sT, t_p], writes=[S.t_yp[g2]])

                    K.call(y_unit)
                    K.op("act", lambda e, g2=g2: e.activation(out=S.yp[g2][:], in_=S.yp[g2][:], func=AF.Gelu_apprx_tanh),
                         reads=[S.t_yp[g2]], writes=[S.t_yp[g2]])
                    t_w = Tok()
                    K.dma("sp", lambda e, sem, g2=g2, gt=gt: e.dma_start(out=mix_loc[gt // 2][128:256, (gt % 2) * 512:(gt % 2) * 512 + 512], in_=S.yp[g2][:]).then_inc(sem, 16),
                          1, reads=[S.t_yp[g2]], writes=[t_w])
                    K.call(lambda gt=gt, t_w=t_w: mix_written(gt // 2, t_w))

                for q, gt in enumerate(tiles):
                    ssl = slice((gt - t0) * 512, (gt - t0 + 1) * 512)
                    P16 = S.pr[q % 2]; TP = S.t_pr[q % 2]
                    for gp in range(4):
                        a = (q * 4 + gp) % 2
                        W = S.wk[a]; TW = S.t_wk[a]
                        gsl = slice(gp * 128, (gp + 1) * 128)
                        br, bi = 6, 7
                        K.op("pe", lambda e, gsl=gsl, ssl=ssl: e.matmul(ps[:, 6, :], lhsT=BRb[:, gsl], rhs=S.sT[:, ssl], start=True, stop=True),
                             reads=[t_R, S.t_sT], writes=[PB[6]])
                        K.op("pe", lambda e, gsl=gsl, ssl=ssl: e.matmul(ps[:, 7, :], lhsT=BIb[:, gsl], rhs=S.sT[:, ssl], start=True, stop=True),
                             reads=[t_R, S.t_sT], writes=[PB[7]])
                        cosg, sing = COS[:, gp, :], SIN[:, gp, :]
                        K.op("dve", lambda e, W=W, cosg=cosg: e.tensor_tensor(out=W[0][:], in0=ps[:, 6, :], in1=cosg, op=ALU.mult),
                             reads=[PB[6], t_R], writes=[TW[0]])
                        K.op("dve", lambda e, W=W, sing=sing: e.tensor_tensor(out=W[1][:], in0=ps[:, 7, :], in1=sing, op=ALU.mult),
                             reads=[PB[7], t_R], writes=[TW[1]])
                        K.op("dve", lambda e, W=W, cosg=cosg: e.tensor_tensor(out=W[2][:], in0=ps[:, 7, :], in1=cosg, op=ALU.mult),
                             reads=[PB[7], t_R], writes=[TW[2]])
                        K.op("dve", lambda e, W=W, sing=sing: e.tensor_tensor(out=W[3][:], in0=ps[:, 6, :], in1=sing, op=ALU.mult),
                             reads=[PB[6], t_R], writes=[TW[3]])
                        K.op("pool", lambda e, W=W: e.tensor_tensor(out=W[4][:], in0=W[0][:], in1=W[1][:], op=ALU.add),
                             reads=[TW[0], TW[1]], writes=[TW[4]])
                        K.op("pool", lambda e, W=W: e.tensor_tensor(out=W[5][:], in0=W[2][:], in1=W[3][:], op=ALU.subtract),
                             reads=[TW[2], TW[3]], writes=[TW[5]])
                        zin = gt % 2
                        zout = (gt + 1) % 2
                        rho_b = Cw[:, 2, gp:gp + 1].broadcast_to([128, 512])
                        K.op("dve", lambda e, W=W, rho_b=rho_b, zin=zin, gp=gp: e.tensor_tensor_scan(
                            out=W[6][:], data0=rho_b, data1=W[4][:], initial=Cw[:, 8 + zin, gp:gp + 1], op0=ALU.mult, op1=ALU.add),
                            reads=[TW[4], t_R, t_zi[zin][gp]], writes=[TW[6]])
                        K.op("dve", lambda e, W=W, rho_b=rho_b, zin=zin, gp=gp: e.tensor_tensor_scan(
                            out=W[7][:], data0=rho_b, data1=W[5][:], initial=Cw[:, 10 + zin, gp:gp + 1], op0=ALU.mult, op1=ALU.add),
                            reads=[TW[5], t_R, t_zi[zin][gp]], writes=[TW[7]])
                        c5 = Cw[:, 3, gp:gp + 1]; s5 = Cw[:, 4, gp:gp + 1]
                        tmpa = Cw[:, 12, gp:gp + 1]; tmpb = Cw[:, 13, gp:gp + 1]
                        t_tmp = Tok()
                        K.op("dve", lambda e, W=W, s5=s5, tmpa=tmpa: e.tensor_tensor(out=tmpa, in0=W[7][:, 511:512], in1=s5, op=ALU.mult),
                             reads=[TW[7], t_R], writes=[t_tmp])
                        K.op("dve", lambda e, W=W, c5=c5, tmpa=tmpa, zout=zout, gp=gp: e.scalar_tensor_tensor(
                            out=Cw[:, 8 + zout, gp:gp + 1], in0=W[6][:, 511:512], scalar=c5, in1=tmpa, op0=ALU.mult, op1=ALU.subtract),
                            reads=[TW[6], t_tmp, t_R], writes=[t_zi[zout][gp]])
                        K.op("dve", lambda e, W=W, c5=c5, tmpb=tmpb: e.tensor_tensor(out=tmpb, in0=W[7][:, 511:512], in1=c5, op=ALU.mult),
                             reads=[TW[7], t_R], writes=[t_tmp])
                        K.op("dve", lambda e, W=W, s5=s5, tmpb=tmpb, zout=zout, gp=gp: e.scalar_tensor_tensor(
                            out=Cw[:, 10 + zout, gp:gp + 1], in0=W[6][:, 511:512], scalar=s5, in1=tmpb, op0=ALU.mult, op1=ALU.add),
                            reads=[TW[6], t_tmp, t_R], writes=[t_zi[zout][gp]])
                        for idx, (src, tab) in enumerate(((6, cosg), (7, sing), (6, sing), (7, cosg))):
                            K.op("pool", lambda e, W=W, P16=P16, src=src, tab=tab, j=gp * 4 + idx: e.tensor_tensor(
                                out=P16[j][:], in0=W[src][:], in1=tab, op=ALU.mult),
                                reads=[TW[src], t_R], writes=[TP[gp * 4 + idx]])
                        if gp == 2 and pending is not None:
                            back(*pending)
                            pending = None
                    pending = (gt, q)
                back(*pending)

            import os
            PREP_INLINE = os.environ.get("PREP_INLINE", "1") == "1"
            CLAMP = False
            if not PREP_INLINE:
                with ExitStack() as pst:
                    emit_prep(kb, pst)
                    kb.flush()
            for hh in range(2):
                with ExitStack() as st:
                    QA = sb(st, "QA", [128, 8192], BF16)
                    KA = sb(st, "KA", [128, 8192], BF16)
                    V = sb(st, "V", [128, 64, 128], BF16)
                    t_Q = [Tok() for _ in range(16)]; t_K = [Tok() for _ in range(16)]; t_V = [Tok() for _ in range(16)]
                    t_qrow = Tok()
                    wh = sb(st, "wh", [128, 8, 194], BF16)
                    wq = wh[:, :, 0:64]; wk = wh[:, :, 64:128]; wvf = wh[:, :, 128:194]
                    t_w = Tok()
                    ut = [sb(st, "ut%d" % i, [128, 8, 512], BF16) for i in range(2)]; t_ut = [Tok(), Tok()]
                    zf = sb(st, "zf", [128, 64], F32); t_zf = Tok()
                    bfbc = sb(st, "bfbc_s", [128, 2], F32)
                    sm = sb(st, "sm", [128, 8, 64], F32)
                    t_sm = Tok()
                    b8T = sb(st, "b8T", [64, 128], BF16); t_b8T = Tok()
                    biasT = sb(st, "biasT", [128, 16, 64], F32); t_bias = Tok()
                    clampT = sb(st, "clampT", [128, 16, 4], F32)
                    Pt = [sb(st, "Pt%d" % i, [128, 512], BF16) for i in range(3)]; t_P = [Tok() for _ in range(3)]
                    Rf = sb(st, "Rf", [128, 512], F32); t_Rf = Tok()
                    Osb = sb(st, "Osb", [64, 512], F32); t_Osb = Tok()
                    ot = [sb(st, "ot%d" % i, [64, 512], F32) for i in range(2)]; t_ot = [Tok(), Tok()]
                    S = Ctx()
                    S.sT = sb(st, "sT", [128, 4096], BF16); S.t_sT = Tok()
                    Dp = Deferred()
                    if hh == 0 and PREP_INLINE:
                        with ExitStack() as pst:
                            emit_prep(Dp, pst)
                    S.wk = [[sb(st, "wk%d_%d" % (a, k), [128, 512], F32) for k in range(8)] for a in range(2)]
                    S.t_wk = [[Tok() for _ in range(8)] for _ in range(2)]
                    S.pr = [[sb(st, "pr%d_%d" % (a, k), [128, 512], BF16) for k in range(16)] for a in range(2)]
                    S.t_pr = [[Tok() for _ in range(16)] for _ in range(2)]
                    S.yp = [sb(st, "yp%d" % i, [128, 512], F32) for i in range(2)]; S.t_yp = [Tok(), Tok()]
                    t0s = 8 * hh

                    def ldw(e, sem, hh=hh):
                        e.dma_start(out=wh[:], in_=winv[:, :, 194 * hh:194 * (hh + 1)]).then_inc(sem, 16)

                    kb.dma("pool", ldw, 1, writes=[t_w])
                    t_bf = Tok()
                    kb.dma("sp", lambda e, sem: e.dma_start(out=bfbc[:], in_=bfbc_d).then_inc(sem, 16), 1, writes=[t_bf])
                    kb.op("pool", lambda e: e.memset(V[:, :, 64:128], 1.0), writes=t_V)
                    kb.op("pool", lambda e: e.memset(KA[64:65, :], 1.0), writes=t_K)
                    kb.op("pool", lambda e: e.memset(Rf[:], 0.0), writes=[t_Rf])
                    kb.op("pool", lambda e: e.memset(sm[:, 7, :], 1.0), writes=[t_sm])
                    order = [r * 4 + lt for lt in range(4) for r in range(4)]
                    n_prep = len(Dp.q)
                    for n, gt in enumerate(order):
                        s = load_ut(ut, t_ut, n, gt)
                        par = n % 2
                        bq, bk, bv, bf_, bs = par, 2 + par, 4 + par, 6, 7
                        gsl = slice(gt * 512, (gt + 1) * 512)
                        kb.op("pe", _mm_group(ps[0:64, bq, :], [(wq[:, dc, :], ut[s][:, dc, :]) for dc in range(8)]),
                              reads=[t_w, t_ut[s]], writes=[PB[bq]])
                        kb.op("act", lambda e, gsl=gsl, bq=bq: e.copy(out=QA[0:64, gsl], in_=ps[0:64, bq, :]), reads=[PB[bq]], writes=[t_Q[gt]])
                        kb.op("pe", _mm_group(ps[0:64, bk, :], [(wk[:, dc, :], ut[s][:, dc, :]) for dc in range(8)]),
                              reads=[t_w, t_ut[s]], writes=[PB[bk]])
                        kb.op("dve", lambda e, gsl=gsl, bk=bk: e.tensor_copy(out=KA[0:64, gsl], in_=ps[0:64, bk, :]), reads=[PB[bk]], writes=[t_K[gt]])

                        def vproj(e, s=s, bv=bv):
                            for blk in range(4):
                                for dc in range(8):
                                    ins = e.matmul(ps[:, bv, blk * 66:(blk + 1) * 66], lhsT=ut[s][:, dc, blk * 128:(blk + 1) * 128],
                                                   rhs=wvf[:, dc, :], start=(dc == 0), stop=(dc == 7))
                            return ins

                        kb.op("pe", vproj, reads=[t_w, t_ut[s]], writes=[PB[bv]])
                        pv3 = ps[:, bv, 0:264].rearrange("p (b d) -> p b d", b=4)
                        kb.op("act", lambda e, gt=gt, pv3=pv3: e.copy(out=V[:, gt * 4:(gt + 1) * 4, 0:64], in_=pv3[:, :, 0:64]),
                              reads=[PB[bv]], writes=[t_V[gt]])
                        kb.op("act", lambda e, gt=gt, hh=hh, pv3=pv3: e.copy(out=zf[:, gt * 4:(gt + 1) * 4], in_=pv3[:, :, 64 + hh]),
                              reads=[PB[bv]], writes=[t_zf])
                        if t0s <= gt < t0s + 8:
                            kb.op("pe", _mm_group(ps[:, bs, :], [(ws[:, dc, :], ut[s][:, dc, :]) for dc in range(8)]),
                                  reads=[t_ws, t_ut[s]], writes=[PB[bs]])
                            kb.op("act", lambda e, gt=gt, t0s=t0s, bs=bs: e.copy(out=S.sT[:, (gt - t0s) * 512:(gt - t0s + 1) * 512], in_=ps[:, bs, :]),
                                  reads=[PB[bs]], writes=[S.t_sT])
                        Dp.replay(kb, (n_prep * (n + 1)) // 12)
                    Dp.replay(kb, n_prep)
                    kb.op("act", lambda e, hh=hh: e.activation(out=sm[:, 0, :], in_=zf[:], func=AF.Sigmoid, bias=bfbc[:, hh:hh + 1], scale=1.0),
                          reads=[t_zf, t_bf], writes=[t_sm])
                    kb.op("act", lambda e: e.activation(out=sm[:, 0, :], in_=sm[:, 0, :], func=AF.Ln), reads=[t_sm], writes=[t_sm])

                    def cums(e):
                        e.matmul(ps[:, 4, 0:64], lhsT=triF, rhs=sm[:, 0, :], start=True, stop=True)
                        return e.matmul(ps[:, 4, 64:128], lhsT=onesF, rhs=sm[:, 0, :], start=True, stop=True)

                    kb.op("pe", cums, reads=[t_sm, t_c], writes=[PB[4]])
                    kb.op("dve", lambda e: e.tensor_copy(out=sm[:, 1:3, :], in_=ps[:, 4, 0:128].rearrange("p (a b) -> p a b", a=2)),
                          reads=[PB[4]], writes=[t_sm])
                    kb.op("dve", lambda e: e.tensor_tensor_scan(out=sm[:, 3, :], data0=sm[:, 7, :], data1=sm[:, 2, :], initial=0.0,
                                                                op0=ALU.mult, op1=ALU.add), reads=[t_sm], writes=[t_sm])
                    kb.op("dve", lambda e: e.tensor_tensor(out=sm[:, 4, :], in0=sm[:, 3, :], in1=sm[:, 2, :], op=ALU.subtract),
                          reads=[t_sm], writes=[t_sm])
                    kb.op("dve", lambda e: e.tensor_tensor(out=sm[:, 5, :], in0=sm[:, 1, :], in1=sm[:, 4, :], op=ALU.add),
                          reads=[t_sm], writes=[t_sm])
                    offs4 = sm[:, 4, :].rearrange("p (i f) -> p i f", f=4)
                    kb.op("dve", lambda e: e.tensor_tensor(out=sm[:, 6, :].rearrange("p (i f) -> p i f", f=4),
                                                           in0=sm[:, 5, :].rearrange("p (i f) -> p i f", f=4),
                                                           in1=offs4[:, :, 0:1].broadcast_to([128, 16, 4]), op=ALU.subtract),
                          reads=[t_sm], writes=[t_sm])
                    kb.op("dve", lambda e: e.tensor_scalar(out=sm[:, 6, :], in0=sm[:, 6, :], scalar1=8.0, scalar2=None, op0=ALU.mult),
                          reads=[t_sm], writes=[t_sm])
                    kb.op("pe", lambda e: e.transpose(ps[0:64, 5, 0:128], sm[:, 6, :], ident), reads=[t_sm, t_c], writes=[PB[5]])
                    kb.op("act", lambda e: e.copy(out=b8T[:], in_=ps[0:64, 5, 0:128]), reads=[PB[5]], writes=[t_b8T])
                    kb.dma("sp", lambda e, sem: e.dma_start(out=QA[64:65, :].rearrange("o (b t) -> o b t", t=128), in_=b8T[:]).then_inc(sem, 16),
                           1, reads=[t_b8T], writes=[t_qrow])
                    for i in range(16):
                        nk = 4 * i + 4
                        kb.op("dve", lambda e, i=i, nk=nk: e.tensor_scalar(out=biasT[:, i, 0:nk], in0=sm[:, 5, 0:nk], scalar1=-1.0,
                                                                           scalar2=sm[:, 4, 4 * i:4 * i + 1], op0=ALU.mult, op1=ALU.add),
                              reads=[t_sm], writes=[t_bias])
                        if CLAMP:
                            kb.op("act", lambda e, i=i: e.activation(out=clampT[:, i, :], in_=biasT[:, i, 4 * i:4 * i + 4], func=AF.Identity,
                                                                     scale=-8.0, bias=240.0),
                                  reads=[t_bias], writes=[t_bias])
                    D = Deferred()
                    ssm_emit(D, S, list(range(t0s, t0s + 8)), t0s)
                    n_ssm = len(D.q)
                    steps_total = 544
                    steps = 0
                    mv = [m_.rearrange("(a p) t -> p a t", p=64) for m_ in mix_loc]
                    norm_late = []
                    for i in range(16):
                        nk = 4 * i + 4
                        qsl = slice(i * 512, (i + 1) * 512)
                        ob = 3 + (i % 2)

                        def s_op(kk, i=i, qsl=qsl):
                            c0 = max(0, kk - 4 * i) * 128
                            diag = kk >= 4 * i

                            def f(e, kk=kk, qsl=qsl, c0=c0, diag=diag):
                                ins = e.matmul(ps[:, kk % 3, c0:512], lhsT=KA[0:65, kk * 128:(kk + 1) * 128],
                                               rhs=QA[0:65, qsl.start + c0:qsl.stop], start=True, stop=not diag)
                                if diag:
                                    ins = e.matmul(ps[:, kk % 3, c0:512], lhsT=mk[:, 0:128], rhs=mk[:, 128:128 + 512 - c0], start=False, stop=True)
                                return ins

                            kb.op("pe", f, reads=[t_K[kk // 4], t_Q[i], t_qrow, t_mk], writes=[PB[kk % 3]])

                        def p_op(kk, i=i):
                            c0 = max(0, kk - 4 * i) * 128
                            kb.op("act", lambda e, kk=kk, i=i, c0=c0: e.activation(out=Pt[kk % 3][:, c0:512], in_=ps[:, kk % 3, c0:512], func=AF.Exp,
                                                                                    bias=biasT[:, i, kk:kk + 1], scale=0.125),
                                  reads=[PB[kk % 3], t_bias], writes=[t_P[kk % 3]])

                        def pv_op(kk, ob=ob, nk=nk, i=i):
                            c0 = max(0, kk - 4 * i) * 128
                            kb.op("pe", lambda e, kk=kk, ob=ob, nk=nk, c0=c0: e.matmul(ps[:, ob, c0:512], lhsT=V[:, kk, :], rhs=Pt[kk % 3][:, c0:512],
                                                                                        start=(kk == 0), stop=(kk == nk - 1)),
                                  reads=[t_V[kk // 4], t_P[kk % 3]], writes=[PB[ob]])

                        s_op(0)
                        if nk > 1:
                            s_op(1)
                        for kk in range(nk):
                            p_op(kk)
                            if kk + 2 < nk:
                                s_op(kk + 2)
                            pv_op(kk)
                            steps += 1
                            D.replay(kb, (n_ssm * steps) // steps_total)
                            if kk == 3:
                                for f_ in norm_late:
                                    f_()
                                norm_late = []
                        kb.op("act", lambda e, ob=ob: e.activation(out=Rf[64:128, :], in_=ps[64:128, ob, :], func=AF.Ln), reads=[PB[ob]], writes=[t_Rf])
                        kb.op("act", lambda e: e.activation(out=Rf[64:128, :], in_=Rf[64:128, :], func=AF.Exp, scale=-1.0), reads=[t_Rf], writes=[t_Rf])
                        kb.op("act", lambda e, ob=ob: e.copy(out=Osb[:], in_=ps[0:64, ob, :]), reads=[PB[ob]], writes=[t_Osb])

                        def norm_b(i=i, hh=hh):
                            o = i % 2
                            kb.op("pe", lambda e: e.matmul(ps[0:64, 5, :], lhsT=selF, rhs=Rf[:], start=True, stop=True),
                                  reads=[t_Rf, t_c], writes=[PB[5]])
                            kb.op("dve", lambda e, o=o: e.tensor_tensor(out=ot[o][:], in0=Osb[:], in1=ps[0:64, 5, :], op=ALU.mult),
                                  reads=[t_Osb, PB[5]], writes=[t_ot[o]])
                            t_wr = Tok()
                            kb.dma("sp", lambda e, sem, o=o, i=i, hh=hh: e.dma_start(out=mv[i // 2][:, hh, (i % 2) * 512:(i % 2) * 512 + 512], in_=ot[o][:]).then_inc(sem, 16),
                                   1, reads=[t_ot[o]], writes=[t_wr])
                            mix_written(i // 2, t_wr)

                        norm_late.append(norm_b)
                    for f_ in norm_late:
                        f_()
                    D.replay(kb, n_ssm)
                    kb.flush()
            mixst.close()

        if do3:
            if mode == "full":
                hstack = ExitStack()
                hT = sb(hstack, "hT3", [128, 8, 2048], F32)
                h1T_d = hT_park
            def reload_h(tts, after):
                if mode in ("s3", "full"):
                    hv = h1T_d.rearrange("(c p) t -> p c t", p=128)
                    for tt in tts:
                        for hf in range(2):
                            kb.dma("sp", lambda e, sem, tt=tt, hf=hf: e.dma_start(out=hT[:, hf * 4:hf * 4 + 4, tt * 512:(tt + 1) * 512],
                                                                                  in_=hv[:, hf * 4:hf * 4 + 4, tt * 512:(tt + 1) * 512]).then_inc(sem, 16), 1,
                                   reads=[t_park] + after, writes=[t_h[dc][tt] for dc in range(hf * 4, hf * 4 + 4)])

            with ExitStack() as st:
                Bs3 = [make_norm_bufs(st), make_norm_bufs(st)]
                idxs = sb(st, "idxs", [128, 16], mybir.dt.uint32)
                bglu = sb(st, "bglu_s", [128, 4], F32)
                wglub = sb(st, "wglub", [128, 4, 512], BF16)
                woutb = sb(st, "woutb", [128, 8, 1024], BF16)
                t_w = Tok()
                selp = [sb(st, "selp%d" % i, [128, 8, 1024], F32) for i in range(2)]
                t_selp = [[Tok(), Tok()] for _ in range(2)]
                ygb2 = [sb(st, "ygb%d" % i, [128, 4, 512], BF16) for i in range(2)]; t_ygb2 = [Tok(), Tok()]
                gt2 = [sb(st, "gate%d" % i, [128, 512], F32) for i in range(2)]; t_gt2 = [Tok(), Tok()]
                mxb2 = [sb(st, "mxb%d" % i, [128, 8, 512], BF16) for i in range(2)]; t_mx2 = [Tok(), Tok()]

                def ldw3(e, sem):
                    e.dma_start(out=idxs[:], in_=idx_d).then_inc(sem, 16)
                    e.dma_start(out=bglu[:], in_=bglu_d).then_inc(sem, 16)

                t_w0 = Tok()
                kb.dma("sp", ldw3, 2, writes=[t_w0])

                def ldw3b(e, sem):
                    e.dma_start(out=wglub[:], in_=wglu_d.rearrange("(c p) f -> p c f", p=128)).then_inc(sem, 16)
                    e.dma_start(out=woutb[:, 0:4, :], in_=wout_d.rearrange("(c p) f -> p c f", p=128)[:, 0:4, :]).then_inc(sem, 16)
                    e.dma_start(out=woutb[:, 4:8, :], in_=wout_d.rearrange("(c p) f -> p c f", p=128)[:, 4:8, :]).then_inc(sem, 16)

                kb.dma("pool", ldw3b, 3, writes=[t_w])
                for pp in range(2):
                    def gat(e, sem, pp=pp):
                        for r in range(4):
                            for two in range(2):
                                col = (pp * 4 + r) * 2 + two
                                e.indirect_dma_start(out=selp[pp][:, two * 4 + r, :], out_offset=None, in_=mix_big,
                                                     in_offset=bass.IndirectOffsetOnAxis(ap=idxs[:, col:col + 1], axis=0)).then_inc(sem, 16)

                    kb.dma("pool", gat, 8, reads=[t_w0] + ([t_mall[k] for k in range(pp, 8, 2)] if mode == "full" else []), writes=t_selp[pp])
                    reload_h([2 * pp, 2 * pp + 1], [])
                def chain(K, tt):
                    pp, hq = tt // 2, tt % 2
                    sel = selp[pp][:, :, hq * 512:(hq + 1) * 512]
                    t_sel = t_selp[pp][hq]
                    ygb, t_ygb = ygb2[tt % 2], t_ygb2[tt % 2]
                    mxb, t_mx = mxb2[tt % 2], t_mx2[tt % 2]
                    B = Bs3[tt % 2]
                    K.op("act", lambda e, ygb=ygb, sel=sel: e.copy(out=ygb[:], in_=sel[:, 4:8, :]), reads=[t_sel], writes=[t_ygb])
                    rstd_from(B, sel[:, 0:4, :], [t_sel], 512.0, K=K)
                    for k in range(4):
                        K.op("dve", lambda e, k=k, mxb=mxb, sel=sel, B=B: e.scalar_tensor_tensor(out=mxb[:, k, :], in0=sel[:, k, :], scalar=gcols[:, 5, k:k + 1],
                                                                                                 in1=B.rt[:], op0=ALU.mult, op1=ALU.mult),
                             reads=[t_sel, B.t_rt, t_c], writes=[t_mx])
                    for cp in range(4):
                        b = cp % 2
                        gt_, t_gt = gt2[cp % 2], t_gt2[cp % 2]
                        K.op("pe", _mm_group(ps[:, b, :], [(wglub[:, c, cp * 128:(cp + 1) * 128], ygb[:, c, :]) for c in range(4)]),
                             reads=[t_w, t_ygb], writes=[PB[b]])
                        K.op("act", lambda e, b=b, cp=cp, gt_=gt_: e.activation(out=gt_[:], in_=ps[:, b, :], func=AF.Sigmoid, bias=bglu[:, cp:cp + 1], scale=1.0),
                             reads=[PB[b], t_w0], writes=[t_gt])
                        K.op("dve", lambda e, cp=cp, sel=sel, gt_=gt_: e.tensor_tensor(out=sel[:, 4 + cp, :], in0=sel[:, 4 + cp, :], in1=gt_[:], op=ALU.mult),
                             reads=[t_gt, t_sel], writes=[t_sel])
                    rstd_from(B, sel[:, 4:8, :], [t_sel], 512.0, K=K)
                    for k in range(4, 8):
                        K.op("dve", lambda e, k=k, mxb=mxb, sel=sel, B=B: e.scalar_tensor_tensor(out=mxb[:, k, :], in0=sel[:, k, :], scalar=gcols[:, 5, k:k + 1],
                                                                                                 in1=B.rt[:], op0=ALU.mult, op1=ALU.mult),
                             reads=[t_sel, B.t_rt, t_c], writes=[t_mx])

                chain(kb, 0)
                for tt in range(4):
                    sl = slice(tt * 512, (tt + 1) * 512)
                    mxb, t_mx = mxb2[tt % 2], t_mx2[tt % 2]
                    Dn = Deferred()
                    if tt < 3:
                        chain(Dn, tt + 1)
                    nq = len(Dn.q)
                    for dp in range(8):
                        b = 2 + (dp % 2)
                        kb.op("pe", _mm_group(ps[:, b, :], [(woutb[:, k, dp * 128:(dp + 1) * 128], mxb[:, k, :]) for k in range(8)]),
                              reads=[t_w, t_mx], writes=[PB[b]])
                        kb.op("dve", lambda e, b=b, dp=dp, sl=sl: e.tensor_tensor(out=hT[:, dp, sl], in0=ps[:, b, :], in1=hT[:, dp, sl], op=ALU.add),
                              reads=[PB[b], t_h[dp][tt]], writes=[t_h[dp][tt]])
                        Dn.replay(kb, (nq * (dp + 1)) // 7)
                    Dn.replay(kb, nq)
                kb.flush()
            with ExitStack() as st:
                B = make_ffn_bufs(st)
                ffn(B, w1b, w3b, w2b_d, 2)
                kb.flush()
            with ExitStack() as st:
                Bs = [make_norm_bufs(st), make_norm_bufs(st)]
                xn4 = [sb(st, "xn3_%d" % i, [128, 8, 512], BF16) for i in range(4)]; t_xn4 = [Tok() for _ in range(4)]
                wgb = sb(st, "wgb", [128, 8, 1024], BF16)
                wpb = sb(st, "wpb", [128, 2, 1024], BF16)
                t_w = Tok()
                pin = [sb(st, "pin%d" % i, [128, 256], F32) for i in range(2)]; t_pin = [Tok(), Tok()]
                pT = sb(st, "pT", [128, 2, 2048], BF16); t_pT = [Tok() for _ in range(4)]
                sg = [sb(st, "sg%d" % i, [128, 512], F32) for i in range(2)]; t_sg = [Tok(), Tok()]
                yT2 = [sb(st, "yT%d" % i, [128, 8, 512], F32) for i in range(2)]; t_yT2 = [Tok(), Tok()]
                otl = [sb(st, "otl%d" % i, [128, 1024], F32) for i in range(2)]; t_otl = [Tok(), Tok()]

                def ldw4(e, sem):
                    wgv = wgate_d.rearrange("(c p) f -> p c f", p=128)
                    e.dma_start(out=wgb[:, 0:4, :], in_=wgv[:, 0:4, :]).then_inc(sem, 16)
                    e.dma_start(out=wgb[:, 4:8, :], in_=wgv[:, 4:8, :]).then_inc(sem, 16)
                    e.dma_start(out=wpb[:], in_=wproj_d.rearrange("(c p) f -> p c f", p=128)).then_inc(sem, 16)

                kb.dma("pool", ldw4, 3, writes=[t_w])
                for tt in range(4):
                    norm_tile(Bs[tt % 2], tt, 3, xn4[tt], 0, t_xn4[tt])
                for tb in range(16):
                    s = tb % 2
                    kb.dma("sp", lambda e, sem, s=s, tb=tb: e.dma_start(out=pin[s][:], in_=p_d[tb * 128:(tb + 1) * 128, :]).then_inc(sem, 16),
                           1, writes=[t_pin[s]])
                    b = 6 + (tb % 2)

                    def trp(e, s=s, b=b):
                        e.transpose(ps[:, b, 0:128], pin[s][:, 0:128], ident)
                        return e.transpose(ps[:, b, 128:256], pin[s][:, 128:256], ident)

                    kb.op("pe", trp, reads=[t_pin[s], t_c], writes=[PB[b]])
                    kb.op("act", lambda e, b=b, tb=tb: e.copy(out=pT[:, :, tb * 128:(tb + 1) * 128],
                                                              in_=ps[:, b, 0:256].rearrange("p (k t) -> p k t", k=2)),
                          reads=[PB[b]], writes=[t_pT[tb // 4]])
                for tt in range(4):
                    sl = slice(tt * 512, (tt + 1) * 512)
                    xn = xn4[tt]
                    for dp in range(8):
                        par = dp % 2
                        bg, bp = par, 2 + par
                        dsl = slice(dp * 128, (dp + 1) * 128)
                        kb.op("pe", _mm_group(ps[:, bg, :], [(wgb[:, k, dsl], xn[:, k, :]) for k in range(8)]),
                              reads=[t_w, t_xn4[tt]], writes=[PB[bg]])
                        kb.op("pe", _mm_group(ps[:, bp, :], [(wpb[:, k, dsl], pT[:, k, sl]) for k in range(2)]),
                              reads=[t_w, t_pT[tt]], writes=[PB[bp]])
                        kb.op("act", lambda e, par=par, bg=bg: e.activation(out=sg[par][:], in_=ps[:, bg, :], func=AF.Sigmoid),
                              reads=[PB[bg]], writes=[t_sg[par]])
                        kb.op("dve", lambda e, par=par, bp=bp: e.tensor_tensor(out=sg[par][:], in0=sg[par][:], in1=ps[:, bp, :], op=ALU.mult),
                              reads=[t_sg[par], PB[bp]], writes=[t_sg[par]])
                        kb.op("pool", lambda e, par=par, dp=dp, sl=sl: e.tensor_tensor(out=hT[:, dp, sl], in0=hT[:, dp, sl], in1=sg[par][:], op=ALU.add),
                              reads=[t_sg[par], t_h[dp][tt]], writes=[t_h[dp][tt]])
                for tt in range(4):
                    yT, t_yT = yT2[tt % 2], t_yT2[tt % 2]
                    norm_tile(Bs[tt % 2], tt, 4, yT, 0, t_yT)
                    for tb in range(4):
                        o = tb % 2
                        for dcg in range(2):
                            b = 4 + dcg

                            def trb(e, tb=tb, dcg=dcg, b=b, yT=yT):
                                for k in range(4):
                                    ins = e.transpose(ps[:, b, k * 128:(k + 1) * 128], yT[:, dcg * 4 + k, tb * 128:(tb + 1) * 128], ident)
                                return ins

                            kb.op("pe", trb, reads=[t_yT, t_c], writes=[PB[b]])
                            if dcg == 0:
                                kb.op("act", lambda e, o=o, b=b: e.copy(out=otl[o][:, 0:512], in_=ps[:, b, :]), reads=[PB[b]], writes=[t_otl[o]])
                            else:
                                kb.op("dve", lambda e, o=o, b=b: e.tensor_copy(out=otl[o][:, 512:1024], in_=ps[:, b, :]), reads=[PB[b]], writes=[t_otl[o]])
                        r0 = (tt * 4 + tb) * 128
                        kb.dma("sp", lambda e, sem, o=o, r0=r0: e.dma_start(out=out_d[r0:r0 + 128, :], in_=otl[o][:]).then_inc(sem, 16),
                               1, reads=[t_otl[o]])
                kb.flush()
        hstack.close()
    return nc


def _gcols(inputs):
    gs = [inputs["g_ffn1"][0], inputs["g_mix"][0], inputs["g_ffn2"][0], inputs["g_ple"][0], inputs["g_final"],
          np.concatenate([inputs["g_attn_out"][0], inputs["g_ssm_out"][0]])]
    out = np.zeros((128, 6, 8), np.float32)
    for i, g in enumerate(gs):
        out[:, i, :] = np.asarray(g, np.float32).reshape(8, 128).T
    return out


def _consts():
    cst = np.zeros((128, 4, 128), np.float32)
    cst[:, 0, :] = np.eye(128, dtype=np.float32)
    cst[:, 1, :] = np.triu(np.ones((128, 128), np.float32))
    cst[:, 2, :] = 1.0
    for m in range(64):
        cst[64 + m, 3, m] = 1.0
    return cst


def _mask_const():
    m = np.zeros((128, 640), np.float32)
    r = np.arange(128)
    m[:, 0:128] = np.where(r[None, :] > r[:, None], -30000.0, 0.0)
    m[:, 128:256] = np.eye(128, dtype=np.float32)
    return m


def _mixer_params(inputs, j):
    w_in = inputs["w_in"][0]
    win = np.zeros((1024, 520), np.float32)
    for hh in range(2):
        c0 = 194 * hh
        hd = 128 * j + 64 * hh
        win[:, c0:c0 + 64] = w_in[:, hd:hd + 64]
        win[:, c0 + 64:c0 + 128] = w_in[:, 512 + hd:512 + hd + 64]
        win[:, c0 + 128:c0 + 192] = w_in[:, 1024 + hd:1024 + hd + 64]
        win[:, c0 + 192:c0 + 194] = w_in[:, 1536 + 2 * j:1536 + 2 * j + 2]
    win[:, 388:516] = w_in[:, 1544 + 128 * j:1544 + 128 * (j + 1)]
    bfbc = np.broadcast_to(inputs["b_f"][0][2 * j:2 * j + 2][None, :], (128, 2)).astype(np.float32).copy()
    a_re, a_im, log_dt = inputs["a_re"][0], inputs["a_im"][0], inputs["log_dt"][0]
    b_re, b_im, c_re, c_im = inputs["b_re"][0], inputs["b_im"][0], inputs["c_re"][0], inputs["c_im"][0]
    colp = np.zeros((128, 3, 4), np.float32)
    rowp = np.zeros((3, 512), np.float32)
    braw = np.zeros((128, 2, 512), np.float32)
    craw = np.zeros((128, 2, 512), np.float32)
    dcol = np.zeros((128, 1), np.float32)
    for gl in range(8):
        g = 8 * j + gl
        gp, half = gl // 2, gl % 2
        st = slice(half * 64, half * 64 + 64)
        colp[st, 0, gp] = a_re[g]; colp[st, 1, gp] = a_im[g]; colp[st, 2, gp] = log_dt[g]
        rs = slice(gp * 128 + half * 64, gp * 128 + half * 64 + 64)
        rowp[0, rs] = a_re[g]; rowp[1, rs] = a_im[g]; rowp[2, rs] = log_dt[g]
        ch = slice(16 * gl, 16 * gl + 16)
        braw[ch, 0, rs] = b_re[g].T
        braw[ch, 1, rs] = b_im[g].T
        cs = slice(gp * 128 + 16 * gl, gp * 128 + 16 * gl + 16)
        craw[st, 0, cs] = c_re[g].T
        craw[st, 1, cs] = c_im[g].T
        dcol[ch, 0] = inputs["d_skip"][0][g]
    return dict(win=win, bfbc=bfbc, colp=colp, rowp=rowp, braw=braw, craw=craw, dcol=dcol, maskc=_mask_const())


def make_in_maps(inputs, mode="full"):
    cst = _consts()
    gc = _gcols(inputs)
    f = lambda k: np.ascontiguousarray(inputs[k][0], dtype=np.float32)
    shared = {"cst": cst, "gcols": gc}
    if mode in ("full", "s1"):
        shared.update({"w1_a": f("w1_a"), "w3_a": f("w3_a"), "w2_a": f("w2_a")})
    if mode in ("full", "s3"):
        shared.update({"w_glu": f("w_glu"), "w_out": f("w_out"), "w1_b": f("w1_b"), "w3_b": f("w3_b"), "w2_b": f("w2_b"),
                       "w_gate": f("w_ple_gate"), "w_proj": f("w_ple_proj"),
                       "bglu": np.ascontiguousarray(inputs["b_glu"][0].reshape(4, 128).T, dtype=np.float32)})
    maps = []
    for c in range(8):
        b, j = c // 4, c % 4
        m = dict(shared)
        if mode in ("full", "s1"):
            m["x"] = np.ascontiguousarray(inputs["x"][b, j * 2048:(j + 1) * 2048, :], dtype=np.float32)
        if mode in ("full", "s2"):
            m.update(_mixer_params(inputs, j))
        if mode in ("full", "s3"):
            idx = np.zeros((128, 16), np.uint32)
            for kk in range(2):
                for r in range(4):
                    for two in range(2):
                        idx[:, (kk * 4 + r) * 2 + two] = (((2 * j + kk) * 4 + r) * 2 + two) * 128 + np.arange(128)
            m["idxT"] = idx
            m["p"] = np.ascontiguousarray(inputs["p"][0, b, j * 2048:(j + 1) * 2048, :], dtype=np.float32)
        maps.append(m)
    return maps


def kernel(**inputs):
    inputs = {k: np.asarray(v) for k, v in inputs.items()}
    nc = build("full")
    maps = make_in_maps(inputs, "full")
    res = run_bass_kernel_spmd(nc, maps, core_ids=list(range(8)))
    out = np.zeros((2, 8192, 1024), np.float32)
    for c in range(8):
        b, j = c // 4, c % 4
        out[b, j * 2048:(j + 1) * 2048, :] = res.results[c]["out"]
    return out
```

```python
import numpy as np
import concourse.bass as bass
import concourse.mybir as mybir
from concourse.bass_utils import run_bass_kernel_spmd
from contextlib import ExitStack

F32 = mybir.dt.float32
BF16 = mybir.dt.bfloat16
AF = mybir.ActivationFunctionType
ALU = mybir.AluOpType
EPS = 1e-6
GROUPS = [[0, 1, 2, 3], [4, 5, 6, 7]]


class Tok:
    __slots__ = ("w", "r")

    def __init__(self):
        self.w = None
        self.r = []


class KB:
    ENG = ["pe", "act", "dve", "pool", "sp"]

    def __init__(self, nc, stack, n_dma_sems=16):
        self.nc = nc
        self.prog = {e: [] for e in self.ENG}
        self.sems = {e: stack.enter_context(nc.semaphore("s_" + e)) for e in self.ENG}
        self.cnt = {e: 0 for e in self.ENG}
        self.dsems = [stack.enter_context(nc.semaphore("dq%d" % i)) for i in range(n_dma_sems)]
        self.dcnt = [0] * n_dma_sems
        self.dnext = 0
        self.dnext_sw = 0
        self.csem = stack.enter_context(nc.semaphore("cc"))
        self.ccnt = 0
        self.waited = {e: {} for e in self.ENG}
        self.stack = stack
        self.nblk = 0

    def _sem(self, k):
        if k[0] == "e":
            return self.sems[k[1]]
        if k[0] == "c":
            return self.csem
        return self.dsems[k[1]]

    def _deps(self, reads, writes):
        deps = {}

        def add(tok):
            if tok is None:
                return
            k, v = tok
            if deps.get(k, 0) < v:
                deps[k] = v

        for b in reads:
            add(b.w)
        for b in writes:
            add(b.w)
            for t in b.r:
                add(t)
        return deps

    def _emit_waits(self, eng, deps, skip_self):
        for k, v in deps.items():
            if skip_self and k == ("e", eng):
                continue
            if self.waited[eng].get(k, 0) >= v:
                continue
            self.waited[eng][k] = v
            sem = self._sem(k)
            self.prog[eng].append(lambda e, sem=sem, v=v: e.wait_ge(sem, v))

    def _update(self, tok, reads, writes):
        for b in writes:
            b.w = tok
            b.r = []
        for b in reads:
            if b not in writes:
                b.r.append(tok)
                if len(b.r) > 64:
                    b.r = b.r[-64:]

    def op(self, eng, fn, reads=(), writes=()):
        deps = self._deps(reads, writes)
        self._emit_waits(eng, deps, skip_self=(eng == "pe"))
        self.cnt[eng] += 1
        tok = (("e", eng), self.cnt[eng])
        sem = self.sems[eng]
        self.prog[eng].append(lambda e, fn=fn, sem=sem: fn(e).then_inc(sem, 1))
        self._update(tok, reads, writes)
        return tok

    def dma(self, eng, fn, n, reads=(), writes=()):
        deps = self._deps(reads, writes)
        half = len(self.dsems) // 2
        if eng == "pool":
            i = half + self.dnext_sw
            self.dnext_sw = (self.dnext_sw + 1) % (len(self.dsems) - half)
        else:
            i = self.dnext
            self.dnext = (self.dnext + 1) % half
        k = ("d", i)
        if self.dcnt[i] > 0:
            deps[k] = max(deps.get(k, 0), self.dcnt[i])
        self._emit_waits(eng, deps, skip_self=True)
        self.dcnt[i] += 16 * n
        tok = (k, self.dcnt[i])
        sem = self.dsems[i]
        self.prog[eng].append(lambda e, fn=fn, sem=sem: fn(e, sem))
        self._update(tok, reads, writes)
        return tok

    def collective(self, fn, reads=(), writes=()):
        deps = self._deps(reads, writes)
        self._emit_waits("pool", deps, skip_self=False)
        self.ccnt += 1
        tok = (("c", 0), self.ccnt)
        sem = self.csem
        self.prog["pool"].append(lambda e, fn=fn, sem=sem: fn(e).then_inc(sem, 1))
        self._update(tok, reads, writes)
        return tok

    def wait_all(self, eng, toks):
        deps = {}
        for t in toks:
            if t is None:
                continue
            k, v = t
            if deps.get(k, 0) < v:
                deps[k] = v
        self._emit_waits(eng, deps, skip_self=False)

    def flush(self):
        allt = [(("e", e), self.cnt[e]) for e in self.ENG if self.cnt[e] > 0]
        allt += [(("d", i), self.dcnt[i]) for i in range(len(self.dsems)) if self.dcnt[i] > 0]
        for e in self.ENG:
            self.wait_all(e, allt)
        nc = self.nc
        prog = self.prog
        self.nblk += 1
        with nc.Block(no_gpsimd_drain=True) as block:

            @block.tensor
            def _(e):
                for f in prog["pe"]:
                    f(e)

            @block.scalar
            def _(e):
                for f in prog["act"]:
                    f(e)

            @block.vector
            def _(e):
                for f in prog["dve"]:
                    f(e)

            @block.gpsimd
            def _(e):
                for f in prog["pool"]:
                    f(e)

            @block.sync
            def _(e):
                for f in prog["sp"]:
                    f(e)

        self.prog = {e: [] for e in self.ENG}


class Ctx:
    pass


class Deferred:
    def __init__(self):
        self.q = []

    def op(self, *a, **k):
        self.q.append(("op", a, k))

    def dma(self, *a, **k):
        self.q.append(("dma", a, k))

    def call(self, fn):
        self.q.append(("call", (fn,), {}))

    def replay(self, kb, upto):
        while self.pos < min(upto, len(self.q)):
            m, a, k = self.q[self.pos]
            self.pos += 1
            if m == "call":
                a[0]()
            else:
                getattr(kb, m)(*a, **k)

    pos = 0


def _mm_group(ps_ap, pairs):
    def f(e):
        n = len(pairs)
        for i, (l, r) in enumerate(pairs):
            ins = e.matmul(ps_ap, lhsT=l, rhs=r, start=(i == 0), stop=(i == n - 1))
        return ins

    return f


def build(mode="full"):
    nc = bass.Bass("TRN2", target_bir_lowering=False)

    def din(name, shape, dt=F32):
        return nc.dram_tensor(name, list(shape), dt, kind="ExternalInput").ap()

    def dout(name, shape, dt=F32):
        return nc.dram_tensor(name, list(shape), dt, kind="ExternalOutput").ap()

    def dint(name, shape, dt=F32):
        return nc.dram_tensor(name, list(shape), dt).ap()

    do1 = mode in ("full", "s1")
    do2 = mode in ("full", "s2")
    do3 = mode in ("full", "s3")
    cst_d = din("cst", [128, 4, 128])
    gcols_d = din("gcols", [128, 6, 8])
    if do1:
        x = din("x", [2048, 1024])
        w1a = din("w1_a", [1024, 2816]); w3a = din("w3_a", [1024, 2816]); w2a = din("w2_a", [2816, 1024])
    if do2:
        win_d = din("win", [1024, 520])
        mask_d = din("maskc", [128, 640])
        bfbc_d = din("bfbc", [128, 2])
        colp_d = din("colp", [128, 3, 4])
        rowp_d = din("rowp", [3, 512])
        braw_d = din("braw", [128, 2, 512])
        craw_d = din("craw", [128, 2, 512])
        dcol_d = din("dcol", [128, 1])
    if do3:
        idx_d = din("idxT", [128, 16], mybir.dt.uint32)
        wglu_d = din("w_glu", [512, 512]); bglu_d = din("bglu", [128, 4])
        wout_d = din("w_out", [1024, 1024])
        w1b = din("w1_b", [1024, 2816]); w3b = din("w3_b", [1024, 2816]); w2b_d = din("w2_b", [2816, 1024])
        wgate_d = din("w_gate", [1024, 1024]); wproj_d = din("w_proj", [256, 1024])
        p_d = din("p", [2048, 256])
        out_d = dout("out", [2048, 1024])
    if mode == "s1":
        dbg_h = dout("dbg_h", [1024, 2048])
        uT_loc = [dout("dbg_u%d" % t, [1024, 512], BF16) for t in range(4)]
    elif mode == "full":
        uT_loc = [dint("uT_loc%d" % t, [1024, 512], BF16) for t in range(4)]
    if mode == "s2":
        uT_all = [din("uT_all%d" % t, [4096, 512], BF16) for t in range(4)]
        mix_loc = [dout("dbg_mix%d" % k, [256, 1024]) for k in range(8)]
    elif mode == "full":
        uT_all = [dint("uT_all%d" % t, [4096, 512], BF16) for t in range(4)]
        mix_loc = [dint("mix_loc%d" % k, [256, 1024]) for k in range(8)]
    if mode == "s3":
        h1T_d = din("h1T", [1024, 2048])
        mix_big = din("mix_all", [8192, 1024])
    elif mode == "full":
        mix_big = dint("mix_all_big", [8192, 1024])
    if do3:
        mix_all = [mix_big[k * 1024:(k + 1) * 1024, :] for k in range(8)]
    t_uloc = [Tok() for _ in range(4)]; t_uall = [Tok() for _ in range(4)]
    t_mloc = [Tok() for _ in range(8)]; t_mall = [Tok() for _ in range(8)]

    with ExitStack() as top:
        kb = KB(nc, top)

        _uid = [0]

        def sb(st, name, shape, dt):
            _uid[0] += 1
            return st.enter_context(nc.sbuf_tensor("%s_u%d" % (name, _uid[0]), list(shape), dt))

        ps = top.enter_context(nc.psum_tensor("ps", [128, 8, 512], F32))
        PB = [Tok() for _ in range(8)]
        t_h = [[Tok() for _ in range(4)] for _ in range(8)]
        cst = sb(top, "cst_s", [128, 4, 128], F32)
        ident = cst[:, 0, :]
        triF = cst[:, 1, :]
        onesF = cst[:, 2, :]
        selF = cst[:, 3, 0:64]
        ones_bf = sb(top, "ones_bf", [128, 128], BF16)
        gcols = sb(top, "gcols_s", [128, 6, 8], F32)
        epsc = sb(top, "epsc", [128, 1], F32)
        t_c = Tok()
        hstack = ExitStack()
        hT = sb(hstack, "hT", [128, 8, 2048], F32)
        if mode == "full":
            hT_park = dint("hT_park", [1024, 2048])
        t_park = Tok()

        def ldc(e, sem):
            e.dma_start(out=cst[:], in_=cst_d).then_inc(sem, 16)
            e.dma_start(out=gcols[:], in_=gcols_d).then_inc(sem, 16)

        kb.dma("sp", ldc, 2, writes=[t_c])
        kb.op("pool", lambda e: e.memset(ones_bf[:], 1.0), writes=[t_c])
        kb.op("pool", lambda e: e.memset(epsc[:], EPS), writes=[t_c])

        def th_all(tt):
            return [t_h[dc][tt] for dc in range(8)]

        def make_norm_bufs(st, B=None):
            B = B or Ctx()
            B.sq = sb(st, "sq", [128, 8, 512], BF16); B.t_sq = Tok()
            B.rt = sb(st, "rt", [128, 512], F32); B.t_rt = Tok()
            return B

        def make_ffn_bufs(st):
            B = make_norm_bufs(st)
            B.xn = sb(st, "xn", [128, 8, 1024], BF16); B.t_xn = [Tok(), Tok()]
            B.G = sb(st, "G", [128, 22, 1024], BF16); B.t_G = [[Tok(), Tok()] for _ in range(22)]
            B.w2b = sb(st, "w2b", [128, 22, 1024], BF16); B.t_w2 = [Tok() for _ in range(11)]
            B.w1g = [sb(st, "w1g%d" % i, [128, 8, 256], BF16) for i in range(2)]
            B.w3g = [sb(st, "w3g%d" % i, [128, 8, 256], BF16) for i in range(2)]
            B.t_wg = [Tok(), Tok()]
            B.s1 = [sb(st, "s1_%d" % i, [128, 512], F32) for i in range(2)]; B.t_s1 = [Tok(), Tok()]
            return B

        def rstd_from(B, src_chunks, rd, nfeat, K=None):
            K = K or kb
            k = src_chunks.shape[1]
            K.op("act", lambda e: e.activation(out=B.sq[:, 0:k, :], in_=src_chunks, func=AF.Square),
                 reads=rd, writes=[B.t_sq])
            K.op("pe", _mm_group(ps[:, 6, :], [(ones_bf[:], B.sq[:, c, :]) for c in range(k)]),
                 reads=[B.t_sq, t_c], writes=[PB[6]])
            K.op("act", lambda e: e.activation(out=B.rt[:], in_=ps[:, 6, :], func=AF.Ln, bias=epsc[:, 0:1], scale=1.0 / nfeat),
                 reads=[PB[6], t_c], writes=[B.t_rt])
            K.op("act", lambda e: e.activation(out=B.rt[:], in_=B.rt[:], func=AF.Exp, scale=-0.5), reads=[B.t_rt], writes=[B.t_rt])

        def norm_tile(B, tt, gi, dst, doff, t_dst, K=None):
            K = K or kb
            sl = slice(tt * 512, (tt + 1) * 512)
            rstd_from(B, hT[:, :, sl], th_all(tt), 1024.0, K=K)
            for dc in range(8):
                K.op("dve", lambda e, dc=dc: e.scalar_tensor_tensor(
                    out=dst[:, dc, doff:doff + 512], in0=hT[:, dc, sl], scalar=gcols[:, gi, dc:dc + 1],
                    in1=B.rt[:], op0=ALU.mult, op1=ALU.mult),
                    reads=[t_h[dc][tt], B.t_rt, t_c], writes=[t_dst])

        def ffn(B, w1, w3, w2, gi, after_tile=None):
            w1v = w1.rearrange("(c p) f -> p c f", p=128)
            w3v = w3.rearrange("(c p) f -> p c f", p=128)
            w2v = w2.rearrange("(c p) d -> p c d", p=128)
            def load_w2():
                for k in range(11):
                    kb.dma("pool", lambda e, sem, k=k: e.dma_start(out=B.w2b[:, 2 * k:2 * k + 2, :], in_=w2v[:, 2 * k:2 * k + 2, :]).then_inc(sem, 16),
                           1, writes=[B.t_w2[k]])

            late = []
            for th in range(2):
                for t2 in range(2):
                    norm_tile(B, th * 2 + t2, gi, B.xn, t2 * 512, B.t_xn[t2])
                for fg in range(11):
                    s = fg % 2

                    def ldw(e, sem, s=s, fg=fg):
                        e.dma_start(out=B.w1g[s][:], in_=w1v[:, :, fg * 256:(fg + 1) * 256]).then_inc(sem, 16)
                        e.dma_start(out=B.w3g[s][:], in_=w3v[:, :, fg * 256:(fg + 1) * 256]).then_inc(sem, 16)

                    kb.dma("pool", ldw, 2, writes=[B.t_wg[s]])
                    if th == 0 and fg == 1:
                        load_w2()
                    if fg == 10:
                        for f_ in late:
                            f_()
                        late = []
                    for f2 in range(2):
                        fc = fg * 2 + f2
                        for t2 in range(2):
                            par = (fc * 2 + t2) % 2
                            b1, b3 = par, 2 + par
                            tsl = slice(t2 * 512, (t2 + 1) * 512)
                            fsl = slice(f2 * 128, (f2 + 1) * 128)
                            kb.op("pe", _mm_group(ps[:, b1, :], [(B.w1g[s][:, dc, fsl], B.xn[:, dc, tsl]) for dc in range(8)]),
                                  reads=[B.t_wg[s], B.t_xn[t2]], writes=[PB[b1]])
                            kb.op("pe", _mm_group(ps[:, b3, :], [(B.w3g[s][:, dc, fsl], B.xn[:, dc, tsl]) for dc in range(8)]),
                                  reads=[B.t_wg[s], B.t_xn[t2]], writes=[PB[b3]])
                            kb.op("act", lambda e, par=par, b1=b1: e.activation(out=B.s1[par][:], in_=ps[:, b1, :], func=AF.Silu),
                                  reads=[PB[b1]], writes=[B.t_s1[par]])
                            kb.op("dve", lambda e, par=par, b3=b3, fc=fc, tsl=tsl: e.tensor_tensor(
                                out=B.G[:, fc, tsl], in0=B.s1[par][:], in1=ps[:, b3, :], op=ALU.mult),
                                reads=[B.t_s1[par], PB[b3]], writes=[B.t_G[fc][t2]])
                for t2 in range(2):
                    tt = th * 2 + t2
                    sl = slice(tt * 512, (tt + 1) * 512)
                    tsl = slice(t2 * 512, (t2 + 1) * 512)
                    for dp in range(8):
                        b = 4 + (dp % 2)
                        kb.op("pe", _mm_group(ps[:, b, :], [(B.w2b[:, fc, dp * 128:(dp + 1) * 128], B.G[:, fc, tsl]) for fc in range(22)]),
                              reads=B.t_w2 + [B.t_G[fc][t2] for fc in range(22)], writes=[PB[b]])
                        kb.op("dve", lambda e, b=b, dp=dp, sl=sl: e.scalar_tensor_tensor(
                            out=hT[:, dp, sl], in0=ps[:, b, :], scalar=0.5, in1=hT[:, dp, sl], op0=ALU.mult, op1=ALU.add),
                            reads=[PB[b], t_h[dp][tt]], writes=[t_h[dp][tt]])
                    if after_tile is not None:
                        late += after_tile(tt)
            for f_ in late:
                f_()

        if do1:
            with ExitStack() as st:
                xin = [sb(st, "xin%d" % i, [128, 1024], F32) for i in range(2)]
                t_xin = [Tok(), Tok()]
                for tb in range(16):
                    s = tb % 2
                    kb.dma("sp", lambda e, sem, s=s, tb=tb: e.dma_start(out=xin[s][:], in_=x[tb * 128:(tb + 1) * 128, :]).then_inc(sem, 16),
                           1, writes=[t_xin[s]])
                    tt = tb // 4
                    for dcg in range(2):
                        b = 6 + dcg

                        def tr(e, s=s, dcg=dcg, b=b):
                            for k in range(4):
                                dc = dcg * 4 + k
                                ins = e.transpose(ps[:, b, k * 128:(k + 1) * 128], xin[s][:, dc * 128:(dc + 1) * 128], ident)
                            return ins

                        kb.op("pe", tr, reads=[t_xin[s], t_c], writes=[PB[b]])
                        dst = hT[:, dcg * 4:(dcg + 1) * 4, tb * 128:(tb + 1) * 128]
                        src = ps[:, b, :].rearrange("p (k t) -> p k t", k=4)
                        wr = [t_h[dc][tt] for dc in range(dcg * 4, dcg * 4 + 4)]
                        if dcg == 0:
                            kb.op("act", lambda e, dst=dst, src=src: e.copy(out=dst, in_=src), reads=[PB[b]], writes=wr)
                        else:
                            kb.op("dve", lambda e, dst=dst, src=src: e.tensor_copy(out=dst, in_=src), reads=[PB[b]], writes=wr)
                kb.flush()
            with ExitStack() as st:
                B = make_ffn_bufs(st)
                def emit_u(tt):
                    late = []
                    t2 = tt % 2
                    norm_tile(B, tt, 1, B.xn, t2 * 512, B.t_xn[t2])
                    uv = uT_loc[tt].rearrange("(c p) t -> p c t", p=128)
                    kb.dma("sp", lambda e, sem, uv=uv, t2=t2: e.dma_start(out=uv, in_=B.xn[:, :, t2 * 512:(t2 + 1) * 512]).then_inc(sem, 16),
                           1, reads=[B.t_xn[t2]], writes=[t_uloc[tt]])
                    if mode == "full":
                        late.append(lambda tt=tt: kb.collective(
                            lambda e, tt=tt: e.collective_compute("AllGather", ALU.bypass, replica_groups=GROUPS, dma_qos="P3",
                                                                  ins=[uT_loc[tt].opt()], outs=[uT_all[tt].opt()]),
                            reads=[t_uloc[tt]], writes=[t_uall[tt]]))
                        hpv = hT_park.rearrange("(c p) t -> p c t", p=128)
                        kb.dma("sp", lambda e, sem, tt=tt: e.dma_start(out=hpv[:, :, tt * 512:(tt + 1) * 512], in_=hT[:, :, tt * 512:(tt + 1) * 512]).then_inc(sem, 16), 1,
                               reads=th_all(tt), writes=[t_park])
                    return late

                ffn(B, w1a, w3a, w2a, 0, after_tile=emit_u)
                if mode == "s1":
                    hv = dbg_h.rearrange("(c p) t -> p c t", p=128)
                    for dc in range(8):
                        kb.dma("sp", lambda e, sem, dc=dc: e.dma_start(out=hv[:, dc, :], in_=hT[:, dc, :]).then_inc(sem, 16), 1,
                               reads=[t_h[dc][tt] for tt in range(4)])
                kb.flush()

        if do2:
            hstack.close()
            uav = [u_.rearrange("(r c p) t -> p r c t", r=4, p=128) for u_ in uT_all]
            winv = win_d.rearrange("(c p) f -> p c f", p=128)
            mixst = ExitStack()
            BRb = sb(mixst, "BRb", [128, 512], BF16); BIb = sb(mixst, "BIb", [128, 512], BF16)
            CRb = sb(mixst, "CRb", [128, 512], BF16); CRnb = sb(mixst, "CRnb", [128, 512], BF16); CInb = sb(mixst, "CInb", [128, 512], BF16)
            COS = sb(mixst, "COS", [128, 4, 512], F32); SIN = sb(mixst, "SIN", [128, 4, 512], F32)
            Cw = sb(mixst, "Cw", [128, 16, 4], F32)
            dcol = sb(mixst, "dcol", [128, 1], F32)
            ws = sb(mixst, "ws", [128, 8, 128], BF16)
            t_R = Tok(); t_p = Tok(); t_ws = Tok()
            mk = sb(mixst, "mk", [128, 640], BF16); t_mk = Tok()
            kb.dma("pool", lambda e, sem: e.dma_start(out=mk[:], in_=mask_d).then_inc(sem, 16), 1, writes=[t_mk])
            t_zi = [[Tok() for _ in range(4)] for _ in range(2)]
            mix_w = [[] for _ in range(8)]

            def mix_written(k, tok):
                mix_w[k].append(tok)
                if len(mix_w[k]) == 6 and mode == "full":
                    kb.collective(lambda e, k=k: e.collective_compute("AllGather", ALU.bypass, replica_groups=GROUPS, dma_qos="P3",
                                                                      ins=[mix_loc[k].opt()], outs=[mix_all[k].opt()]),
                                  reads=mix_w[k], writes=[t_mall[k]])

            kb.dma("pool", lambda e, sem: e.dma_start(out=ws[:], in_=winv[:, :, 388:516]).then_inc(sem, 16), 1, writes=[t_ws])

            def emit_prep(K, st):
                rowp = sb(st, "rowp", [128, 3, 512], F32)
                colp = sb(st, "colp", [128, 3, 4], F32)
                braw = sb(st, "braw", [128, 2, 512], F32)
                craw = sb(st, "craw", [128, 2, 512], F32)

                def ldp(e, sem):
                    e.dma_start(out=rowp[:], in_=rowp_d.partition_broadcast(128)).then_inc(sem, 16)
                    e.dma_start(out=colp[:], in_=colp_d).then_inc(sem, 16)
                    e.dma_start(out=braw[:], in_=braw_d).then_inc(sem, 16)
                    e.dma_start(out=craw[:], in_=craw_d).then_inc(sem, 16)
                    e.dma_start(out=dcol[:], in_=dcol_d).then_inc(sem, 16)

                K.dma("sp", ldp, 5, writes=[t_p])
                R = sb(st, "Rw", [128, 12, 512], F32)
                TT = sb(st, "TT", [128, 4, 4, 256], F32)
                RD = [t_p, t_R]

                def dv(fn):
                    K.op("dve", fn, reads=RD, writes=[t_R])

                def ac(fn):
                    K.op("act", fn, reads=RD, writes=[t_R])

                def tt_(o, a, b, op):
                    dv(lambda e: e.tensor_tensor(out=o, in0=a, in1=b, op=op))

                def csq(c, s, t0, t1, t2):
                    tt_(t0, c, c, ALU.mult)
                    tt_(t1, s, s, ALU.mult)
                    tt_(t2, c, s, ALU.mult)
                    tt_(c, t0, t1, ALU.subtract)
                    dv(lambda e: e.tensor_scalar(out=s, in0=t2, scalar1=2.0, scalar2=None, op0=ALU.mult))

                hp_t = sb(st, "hp_t", [128, 1], F32)
                K.op("pool", lambda e: e.memset(hp_t[:], float(np.pi / 2)), writes=[t_R])
                halfpi = hp_t[:, 0:1]

                def cossin(th, c, s, t0, t1, t2):
                    ac(lambda e: e.activation(out=s, in_=th, func=AF.Sin, scale=1.0 / 16))
                    ac(lambda e: e.activation(out=c, in_=th, func=AF.Sin, scale=1.0 / 16, bias=halfpi))
                    for _ in range(4):
                        csq(c, s, t0, t1, t2)

                arR, aiR, ldR = rowp[:, 0, :], rowp[:, 1, :], rowp[:, 2, :]
                ac(lambda e: e.activation(out=R[:, 0, :], in_=ldR, func=AF.Exp))
                tt_(R[:, 1, :], R[:, 0, :], aiR, ALU.mult)
                tt_(R[:, 2, :], R[:, 0, :], arR, ALU.mult)
                ac(lambda e: e.activation(out=R[:, 2, :], in_=R[:, 2, :], func=AF.Exp))
                cossin(R[:, 1, :], R[:, 3, :], R[:, 4, :], R[:, 5, :], R[:, 6, :], R[:, 7, :])
                tt_(R[:, 3, :], R[:, 3, :], R[:, 2, :], ALU.mult)
                tt_(R[:, 4, :], R[:, 4, :], R[:, 2, :], ALU.mult)
                dv(lambda e: e.tensor_scalar(out=R[:, 3, :], in0=R[:, 3, :], scalar1=-1.0, scalar2=None, op0=ALU.add))
                tt_(R[:, 5, :], arR, arR, ALU.mult)
                tt_(R[:, 6, :], aiR, aiR, ALU.mult)
                tt_(R[:, 10, :], R[:, 5, :], R[:, 6, :], ALU.add)
                dv(lambda e: e.reciprocal(out=R[:, 10, :], in_=R[:, 10, :]))
                tt_(R[:, 5, :], R[:, 3, :], arR, ALU.mult)
                tt_(R[:, 6, :], R[:, 4, :], aiR, ALU.mult)
                tt_(R[:, 8, :], R[:, 5, :], R[:, 6, :], ALU.add)
                tt_(R[:, 8, :], R[:, 8, :], R[:, 10, :], ALU.mult)
                tt_(R[:, 5, :], R[:, 4, :], arR, ALU.mult)
                tt_(R[:, 6, :], R[:, 3, :], aiR, ALU.mult)
                tt_(R[:, 9, :], R[:, 5, :], R[:, 6, :], ALU.subtract)
                tt_(R[:, 9, :], R[:, 9, :], R[:, 10, :], ALU.mult)
                tt_(R[:, 5, :], R[:, 8, :], braw[:, 0, :], ALU.mult)
                tt_(R[:, 6, :], R[:, 9, :], braw[:, 1, :], ALU.mult)
                tt_(BRb[:], R[:, 5, :], R[:, 6, :], ALU.subtract)
                tt_(R[:, 5, :], R[:, 8, :], braw[:, 1, :], ALU.mult)
                tt_(R[:, 6, :], R[:, 9, :], braw[:, 0, :], ALU.mult)
                tt_(BIb[:], R[:, 5, :], R[:, 6, :], ALU.add)
                dv(lambda e: e.tensor_copy(out=CRb[:], in_=craw[:, 0, :]))
                dv(lambda e: e.tensor_scalar(out=CRnb[:], in0=craw[:, 0, :], scalar1=-1.0, scalar2=None, op0=ALU.mult))
                dv(lambda e: e.tensor_scalar(out=CInb[:], in0=craw[:, 1, :], scalar1=-1.0, scalar2=None, op0=ALU.mult))
                arC, aiC, ldC = colp[:, 0, :], colp[:, 1, :], colp[:, 2, :]
                ac(lambda e: e.activation(out=Cw[:, 0, :], in_=ldC, func=AF.Exp))
                tt_(Cw[:, 1, :], Cw[:, 0, :], aiC, ALU.mult)
                tt_(Cw[:, 2, :], Cw[:, 0, :], arC, ALU.mult)
                ac(lambda e: e.activation(out=Cw[:, 2, :], in_=Cw[:, 2, :], func=AF.Exp))
                cossin(Cw[:, 1, :], Cw[:, 3, :], Cw[:, 4, :], Cw[:, 5, :], Cw[:, 6, :], Cw[:, 7, :])
                K.op("pool", lambda e: e.memset(COS[:, :, 0:1], 1.0), reads=RD, writes=[t_R])
                K.op("pool", lambda e: e.memset(SIN[:, :, 0:1], 0.0), reads=RD, writes=[t_R])
                for m in range(9):
                    n = 1 << m
                    cm = Cw[:, 3, :].unsqueeze(2).broadcast_to([128, 4, n])
                    sm_ = Cw[:, 4, :].unsqueeze(2).broadcast_to([128, 4, n])
                    a0, a1, a2, a3 = (TT[:, k, :, 0:n] for k in range(4))
                    tt_(a0, COS[:, :, 0:n], cm, ALU.mult)
                    tt_(a1, SIN[:, :, 0:n], sm_, ALU.mult)
                    tt_(a2, COS[:, :, 0:n], sm_, ALU.mult)
                    tt_(a3, SIN[:, :, 0:n], cm, ALU.mult)
                    tt_(COS[:, :, n:2 * n], a0, a1, ALU.subtract)
                    tt_(SIN[:, :, n:2 * n], a2, a3, ALU.add)
                    csq(Cw[:, 3, :], Cw[:, 4, :], Cw[:, 5, :], Cw[:, 6, :], Cw[:, 7, :])
                K.op("pool", lambda e: e.memset(Cw[:, 8:12, :], 0.0), reads=RD, writes=[t_R])

            def load_ut(ut, t_ut, n, gt):
                s = n % 2
                r, lt = gt // 4, gt % 4
                kb.dma("sp", lambda e, sem: e.dma_start(out=ut[s][:], in_=uav[lt][:, r, :, :]).then_inc(sem, 16),
                       1, reads=[t_uall[lt]], writes=[t_ut[s]])
                return s

            def ssm_emit(K, S, tiles, t0):
                pending = None

                def back(gt, q):
                    g2 = q % 2
                    P16 = S.pr[g2]; TP = S.t_pr[g2]

                    def cproj(e, P16=P16):
                        n = 0
                        for gp in range(4):
                            gsl = slice(gp * 128, (gp + 1) * 128)
                            for (w, idx) in ((CRb, 0), (CRnb, 1), (CInb, 2), (CInb, 3)):
                                ins = e.matmul(ps[:, 5, :], lhsT=w[:, gsl], rhs=P16[gp * 4 + idx][:], start=(n == 0), stop=(n == 15))
                                n += 1
                        return ins

                    ssl = slice((gt - t0) * 512, (gt - t0 + 1) * 512)

                    y4 = q % 4

                    def y_unit(cproj=cproj, TP=TP, y4=y4, ssl=ssl):
                        kb.op("pe", cproj, reads=TP + [t_R], writes=[PB[5]])
                        kb.op("dve", lambda e, y4=y4, ssl=ssl: e.scalar_tensor_tensor(out=S.yp[y4][:], in0=S.sT[:, ssl], scalar=dcol[:, 0:1], in1=ps[:, 5, :],
                                                                                    op0=ALU.mult, op1=ALU.add),
                              reads=[PB[5], S.t_sT, t_p], writes=[S.t_yp[y4]])

                    K.call(y_unit)
                    gelu_wait.append((gt, y4))
                    if q % 2 == 1:
                        for (gt_, y_) in gelu_wait:
                            K.op("act", lambda e, y_=y_: e.activation(out=S.yp[y_][:], in_=S.yp[y_][:], func=AF.Gelu_apprx_tanh),
                                 reads=[S.t_yp[y_]], writes=[S.t_yp[y_]])
                        for (gt_, y_) in gelu_wait:
                            t_w = Tok()
                            K.dma("sp", lambda e, sem, y_=y_, gt_=gt_: e.dma_start(out=mix_loc[gt_ // 2][128:256, (gt_ % 2) * 512:(gt_ % 2) * 512 + 512], in_=S.yp[y_][:]).then_inc(sem, 16),
                                  1, reads=[S.t_yp[y_]], writes=[t_w])
                            K.call(lambda gt_=gt_, t_w=t_w: mix_written(gt_ // 2, t_w))
                        del gelu_wait[:]

                gelu_wait = []
                for q, gt in enumerate(tiles):
                    ssl = slice((gt - t0) * 512, (gt - t0 + 1) * 512)
                    P16 = S.pr[q % 2]; TP = S.t_pr[q % 2]
                    for gp in range(4):
                        a = (q * 4 + gp) % 2
                        W = S.wk[a]; TW = S.t_wk[a]
                        gsl = slice(gp * 128, (gp + 1) * 128)
                        br, bi = 6, 7
                        K.op("pe", lambda e, gsl=gsl, ssl=ssl: e.matmul(ps[:, 6, :], lhsT=BRb[:, gsl], rhs=S.sT[:, ssl], start=True, stop=True),
                             reads=[t_R, S.t_sT], writes=[PB[6]])
                        K.op("pe", lambda e, gsl=gsl, ssl=ssl: e.matmul(ps[:, 7, :], lhsT=BIb[:, gsl], rhs=S.sT[:, ssl], start=True, stop=True),
                             reads=[t_R, S.t_sT], writes=[PB[7]])
                        cosg, sing = COS[:, gp, :], SIN[:, gp, :]
                        K.op("dve", lambda e, W=W, cosg=cosg: e.tensor_tensor(out=W[0][:], in0=ps[:, 6, :], in1=cosg, op=ALU.mult),
                             reads=[PB[6], t_R], writes=[TW[0]])
                        K.op("dve", lambda e, W=W, sing=sing: e.tensor_tensor(out=W[1][:], in0=ps[:, 7, :], in1=sing, op=ALU.mult),
                             reads=[PB[7], t_R], writes=[TW[1]])
                        K.op("dve", lambda e, W=W, cosg=cosg: e.tensor_tensor(out=W[2][:], in0=ps[:, 7, :], in1=cosg, op=ALU.mult),
                             reads=[PB[7], t_R], writes=[TW[2]])
                        K.op("dve", lambda e, W=W, sing=sing: e.tensor_tensor(out=W[3][:], in0=ps[:, 6, :], in1=sing, op=ALU.mult),
                             reads=[PB[6], t_R], writes=[TW[3]])
                        K.op("pool", lambda e, W=W: e.tensor_tensor(out=W[4][:], in0=W[0][:], in1=W[1][:], op=ALU.add),
                             reads=[TW[0], TW[1]], writes=[TW[4]])
                        K.op("pool", lambda e, W=W: e.tensor_tensor(out=W[5][:], in0=W[2][:], in1=W[3][:], op=ALU.subtract),
                             reads=[TW[2], TW[3]], writes=[TW[5]])
                        zin = gt % 2
                        zout = (gt + 1) % 2
                        rho_b = Cw[:, 2, gp:gp + 1].broadcast_to([128, 512])
                        K.op("dve", lambda e, W=W, rho_b=rho_b, zin=zin, gp=gp: e.tensor_tensor_scan(
                            out=W[6][:], data0=rho_b, data1=W[4][:], initial=Cw[:, 8 + zin, gp:gp + 1], op0=ALU.mult, op1=ALU.add),
                            reads=[TW[4], t_R, t_zi[zin][gp]], writes=[TW[6]])
                        K.op("dve", lambda e, W=W, rho_b=rho_b, zin=zin, gp=gp: e.tensor_tensor_scan(
                            out=W[7][:], data0=rho_b, data1=W[5][:], initial=Cw[:, 10 + zin, gp:gp + 1], op0=ALU.mult, op1=ALU.add),
                            reads=[TW[5], t_R, t_zi[zin][gp]], writes=[TW[7]])
                        c5 = Cw[:, 3, gp:gp + 1]; s5 = Cw[:, 4, gp:gp + 1]
                        tmpa = Cw[:, 12, gp:gp + 1]; tmpb = Cw[:, 13, gp:gp + 1]
                        t_tmp = Tok()
                        K.op("dve", lambda e, W=W, s5=s5, tmpa=tmpa: e.tensor_tensor(out=tmpa, in0=W[7][:, 511:512], in1=s5, op=ALU.mult),
                             reads=[TW[7], t_R], writes=[t_tmp])
                        K.op("dve", lambda e, W=W, c5=c5, tmpa=tmpa, zout=zout, gp=gp: e.scalar_tensor_tensor(
                            out=Cw[:, 8 + zout, gp:gp + 1], in0=W[6][:, 511:512], scalar=c5, in1=tmpa, op0=ALU.mult, op1=ALU.subtract),
                            reads=[TW[6], t_tmp, t_R], writes=[t_zi[zout][gp]])
                        K.op("dve", lambda e, W=W, c5=c5, tmpb=tmpb: e.tensor_tensor(out=tmpb, in0=W[7][:, 511:512], in1=c5, op=ALU.mult),
                             reads=[TW[7], t_R], writes=[t_tmp])
                        K.op("dve", lambda e, W=W, s5=s5, tmpb=tmpb, zout=zout, gp=gp: e.scalar_tensor_tensor(
                            out=Cw[:, 10 + zout, gp:gp + 1], in0=W[6][:, 511:512], scalar=s5, in1=tmpb, op0=ALU.mult, op1=ALU.add),
                            reads=[TW[6], t_tmp, t_R], writes=[t_zi[zout][gp]])
                        for idx, (src, tab) in enumerate(((6, cosg), (7, sing), (6, sing), (7, cosg))):
                            K.op("pool", lambda e, W=W, P16=P16, src=src, tab=tab, j=gp * 4 + idx: e.tensor_tensor(
                                out=P16[j][:], in0=W[src][:], in1=tab, op=ALU.mult),
                                reads=[TW[src], t_R], writes=[TP[gp * 4 + idx]])
                        if gp == 2 and pending is not None:
                            back(*pending)
                            pending = None
                    pending = (gt, q)
                back(*pending)

            import os
            PREP_INLINE = os.environ.get("PREP_INLINE", "1") == "1"
            CLAMP = False
            if not PREP_INLINE:
                with ExitStack() as pst:
                    emit_prep(kb, pst)
                    kb.flush()
            for hh in range(2):
                with ExitStack() as st:
                    QA = sb(st, "QA", [128, 8192], BF16)
                    KA = sb(st, "KA", [128, 8192], BF16)
                    V = sb(st, "V", [128, 64, 128], BF16)
                    t_Q = [Tok() for _ in range(16)]; t_K = [Tok() for _ in range(16)]; t_V = [Tok() for _ in range(16)]
                    t_qrow = Tok()
                    wh = sb(st, "wh", [128, 8, 194], BF16)
                    wq = wh[:, :, 0:64]; wk = wh[:, :, 64:128]; wvf = wh[:, :, 128:194]
                    t_w = Tok()
                    ut = [sb(st, "ut%d" % i, [128, 8, 512], BF16) for i in range(2)]; t_ut = [Tok(), Tok()]
                    zf = sb(st, "zf", [128, 64], F32); t_zf = Tok()
                    bfbc = sb(st, "bfbc_s", [128, 2], F32)
                    sm = sb(st, "sm", [128, 8, 64], F32)
                    t_sm = Tok()
                    b8T = sb(st, "b8T", [64, 128], BF16); t_b8T = Tok()
                    biasT = sb(st, "biasT", [128, 16, 64], F32); t_bias = Tok()
                    clampT = sb(st, "clampT", [128, 16, 4], F32)
                    Pt = [sb(st, "Pt%d" % i, [128, 512], BF16) for i in range(3)]; t_P = [Tok() for _ in range(3)]
                    Rf = sb(st, "Rf", [128, 512], F32); t_Rf = Tok()
                    Osb = sb(st, "Osb", [64, 512], F32); t_Osb = Tok()
                    ot = [sb(st, "ot%d" % i, [64, 512], F32) for i in range(2)]; t_ot = [Tok(), Tok()]
                    S = Ctx()
                    S.sT = sb(st, "sT", [128, 4096], BF16); S.t_sT = Tok()
                    Dp = Deferred()
                    if hh == 0 and PREP_INLINE:
                        with ExitStack() as pst:
                            emit_prep(Dp, pst)
                    S.wk = [[sb(st, "wk%d_%d" % (a, k), [128, 512], F32) for k in range(8)] for a in range(2)]
                    S.t_wk = [[Tok() for _ in range(8)] for _ in range(2)]
                    S.pr = [[sb(st, "pr%d_%d" % (a, k), [128, 512], BF16) for k in range(16)] for a in range(2)]
                    S.t_pr = [[Tok() for _ in range(16)] for _ in range(2)]
                    S.yp = [sb(st, "yp%d" % i, [128, 512], F32) for i in range(4)]; S.t_yp = [Tok() for _ in range(4)]
                    t0s = 8 * hh

                    def ldw(e, sem, hh=hh):
                        e.dma_start(out=wh[:], in_=winv[:, :, 194 * hh:194 * (hh + 1)]).then_inc(sem, 16)

                    kb.dma("pool", ldw, 1, writes=[t_w])
                    t_bf = Tok()
                    kb.dma("sp", lambda e, sem: e.dma_start(out=bfbc[:], in_=bfbc_d).then_inc(sem, 16), 1, writes=[t_bf])
                    kb.op("pool", lambda e: e.memset(V[:, :, 64:128], 1.0), writes=t_V)
                    kb.op("pool", lambda e: e.memset(KA[64:65, :], 1.0), writes=t_K)
                    kb.op("pool", lambda e: e.memset(Rf[:], 0.0), writes=[t_Rf])
                    kb.op("pool", lambda e: e.memset(sm[:, 7, :], 1.0), writes=[t_sm])
                    order = [r * 4 + lt for lt in range(4) for r in range(4)]
                    n_prep = len(Dp.q)
                    for n, gt in enumerate(order):
                        s = load_ut(ut, t_ut, n, gt)
                        par = n % 2
                        bq, bk, bv, bf_, bs = par, 2 + par, 4 + par, 6, 7
                        gsl = slice(gt * 512, (gt + 1) * 512)
                        kb.op("pe", _mm_group(ps[0:64, bq, :], [(wq[:, dc, :], ut[s][:, dc, :]) for dc in range(8)]),
                              reads=[t_w, t_ut[s]], writes=[PB[bq]])
                        kb.op("act", lambda e, gsl=gsl, bq=bq: e.copy(out=QA[0:64, gsl], in_=ps[0:64, bq, :]), reads=[PB[bq]], writes=[t_Q[gt]])
                        kb.op("pe", _mm_group(ps[0:64, bk, :], [(wk[:, dc, :], ut[s][:, dc, :]) for dc in range(8)]),
                              reads=[t_w, t_ut[s]], writes=[PB[bk]])
                        kb.op("dve", lambda e, gsl=gsl, bk=bk: e.tensor_copy(out=KA[0:64, gsl], in_=ps[0:64, bk, :]), reads=[PB[bk]], writes=[t_K[gt]])

                        def vproj(e, s=s, bv=bv):
                            for blk in range(4):
                                for dc in range(8):
                                    ins = e.matmul(ps[:, bv, blk * 66:(blk + 1) * 66], lhsT=ut[s][:, dc, blk * 128:(blk + 1) * 128],
                                                   rhs=wvf[:, dc, :], start=(dc == 0), stop=(dc == 7))
                            return ins

                        kb.op("pe", vproj, reads=[t_w, t_ut[s]], writes=[PB[bv]])
                        pv3 = ps[:, bv, 0:264].rearrange("p (b d) -> p b d", b=4)
                        kb.op("act", lambda e, gt=gt, pv3=pv3: e.copy(out=V[:, gt * 4:(gt + 1) * 4, 0:64], in_=pv3[:, :, 0:64]),
                              reads=[PB[bv]], writes=[t_V[gt]])
                        kb.op("act", lambda e, gt=gt, hh=hh, pv3=pv3: e.copy(out=zf[:, gt * 4:(gt + 1) * 4], in_=pv3[:, :, 64 + hh]),
                              reads=[PB[bv]], writes=[t_zf])
                        if t0s <= gt < t0s + 8:
                            kb.op("pe", _mm_group(ps[:, bs, :], [(ws[:, dc, :], ut[s][:, dc, :]) for dc in range(8)]),
                                  reads=[t_ws, t_ut[s]], writes=[PB[bs]])
                            kb.op("act", lambda e, gt=gt, t0s=t0s, bs=bs: e.copy(out=S.sT[:, (gt - t0s) * 512:(gt - t0s + 1) * 512], in_=ps[:, bs, :]),
                                  reads=[PB[bs]], writes=[S.t_sT])
                        Dp.replay(kb, (n_prep * (n + 1)) // 12)
                    Dp.replay(kb, n_prep)
                    kb.op("act", lambda e, hh=hh: e.activation(out=sm[:, 0, :], in_=zf[:], func=AF.Sigmoid, bias=bfbc[:, hh:hh + 1], scale=1.0),
                          reads=[t_zf, t_bf], writes=[t_sm])
                    kb.op("act", lambda e: e.activation(out=sm[:, 0, :], in_=sm[:, 0, :], func=AF.Ln), reads=[t_sm], writes=[t_sm])

                    def cums(e):
                        e.matmul(ps[:, 4, 0:64], lhsT=triF, rhs=sm[:, 0, :], start=True, stop=True)
                        return e.matmul(ps[:, 4, 64:128], lhsT=onesF, rhs=sm[:, 0, :], start=True, stop=True)

                    kb.op("pe", cums, reads=[t_sm, t_c], writes=[PB[4]])
                    kb.op("dve", lambda e: e.tensor_copy(out=sm[:, 1:3, :], in_=ps[:, 4, 0:128].rearrange("p (a b) -> p a b", a=2)),
                          reads=[PB[4]], writes=[t_sm])
                    kb.op("dve", lambda e: e.tensor_tensor_scan(out=sm[:, 3, :], data0=sm[:, 7, :], data1=sm[:, 2, :], initial=0.0,
                                                                op0=ALU.mult, op1=ALU.add), reads=[t_sm], writes=[t_sm])
                    kb.op("dve", lambda e: e.tensor_tensor(out=sm[:, 4, :], in0=sm[:, 3, :], in1=sm[:, 2, :], op=ALU.subtract),
                          reads=[t_sm], writes=[t_sm])
                    kb.op("dve", lambda e: e.tensor_tensor(out=sm[:, 5, :], in0=sm[:, 1, :], in1=sm[:, 4, :], op=ALU.add),
                          reads=[t_sm], writes=[t_sm])
                    offs4 = sm[:, 4, :].rearrange("p (i f) -> p i f", f=4)
                    kb.op("dve", lambda e: e.tensor_tensor(out=sm[:, 6, :].rearrange("p (i f) -> p i f", f=4),
                                                           in0=sm[:, 5, :].rearrange("p (i f) -> p i f", f=4),
                                                           in1=offs4[:, :, 0:1].broadcast_to([128, 16, 4]), op=ALU.subtract),
                          reads=[t_sm], writes=[t_sm])
                    kb.op("dve", lambda e: e.tensor_scalar(out=sm[:, 6, :], in0=sm[:, 6, :], scalar1=8.0, scalar2=None, op0=ALU.mult),
                          reads=[t_sm], writes=[t_sm])
                    kb.op("pe", lambda e: e.transpose(ps[0:64, 5, 0:128], sm[:, 6, :], ident), reads=[t_sm, t_c], writes=[PB[5]])
                    kb.op("act", lambda e: e.copy(out=b8T[:], in_=ps[0:64, 5, 0:128]), reads=[PB[5]], writes=[t_b8T])
                    kb.dma("sp", lambda e, sem: e.dma_start(out=QA[64:65, :].rearrange("o (b t) -> o b t", t=128), in_=b8T[:]).then_inc(sem, 16),
                           1, reads=[t_b8T], writes=[t_qrow])
                    for i in range(16):
                        nk = 4 * i + 4
                        kb.op("dve", lambda e, i=i, nk=nk: e.tensor_scalar(out=biasT[:, i, 0:nk], in0=sm[:, 5, 0:nk], scalar1=-1.0,
                                                                           scalar2=sm[:, 4, 4 * i:4 * i + 1], op0=ALU.mult, op1=ALU.add),
                              reads=[t_sm], writes=[t_bias])
                        if CLAMP:
                            kb.op("act", lambda e, i=i: e.activation(out=clampT[:, i, :], in_=biasT[:, i, 4 * i:4 * i + 4], func=AF.Identity,
                                                                     scale=-8.0, bias=240.0),
                                  reads=[t_bias], writes=[t_bias])
                    D = Deferred()
                    ssm_emit(D, S, list(range(t0s, t0s + 8)), t0s)
                    n_ssm = len(D.q)
                    steps_total = 544
                    steps = 0
                    mv = [m_.rearrange("(a p) t -> p a t", p=64) for m_ in mix_loc]
                    norm_late = []
                    for i in range(16):
                        nk = 4 * i + 4
                        qsl = slice(i * 512, (i + 1) * 512)
                        ob = 3 + (i % 2)

                        def s_op(kk, i=i, qsl=qsl):
                            c0 = max(0, kk - 4 * i) * 128
                            diag = kk >= 4 * i

                            def f(e, kk=kk, qsl=qsl, c0=c0, diag=diag):
                                ins = e.matmul(ps[:, kk % 3, c0:512], lhsT=KA[0:65, kk * 128:(kk + 1) * 128],
                                               rhs=QA[0:65, qsl.start + c0:qsl.stop], start=True, stop=not diag)
                                if diag:
                                    ins = e.matmul(ps[:, kk % 3, c0:512], lhsT=mk[:, 0:128], rhs=mk[:, 128:128 + 512 - c0], start=False, stop=True)
                                return ins

                            kb.op("pe", f, reads=[t_K[kk // 4], t_Q[i], t_qrow, t_mk], writes=[PB[kk % 3]])

                        def p_op(kk, i=i):
                            c0 = max(0, kk - 4 * i) * 128
                            kb.op("act", lambda e, kk=kk, i=i, c0=c0: e.activation(out=Pt[kk % 3][:, c0:512], in_=ps[:, kk % 3, c0:512], func=AF.Exp,
                                                                                    bias=biasT[:, i, kk:kk + 1], scale=0.125),
                                  reads=[PB[kk % 3], t_bias], writes=[t_P[kk % 3]])

                        def pv_op(kk, ob=ob, nk=nk, i=i):
                            c0 = max(0, kk - 4 * i) * 128
                            kb.op("pe", lambda e, kk=kk, ob=ob, nk=nk, c0=c0: e.matmul(ps[:, ob, c0:512], lhsT=V[:, kk, :], rhs=Pt[kk % 3][:, c0:512],
                                                                                        start=(kk == 0), stop=(kk == nk - 1)),
                                  reads=[t_V[kk // 4], t_P[kk % 3]], writes=[PB[ob]])

                        s_op(0)
                        if nk > 1:
                            s_op(1)
                        for kk in range(nk):
                            p_op(kk)
                            if kk + 2 < nk:
                                s_op(kk + 2)
                            pv_op(kk)
                            steps += 1
                            D.replay(kb, (n_ssm * steps) // steps_total)
                            if kk == 3:
                                for f_ in norm_late:
                                    f_()
                                norm_late = []
                        kb.op("act", lambda e, ob=ob: e.activation(out=Rf[64:128, :], in_=ps[64:128, ob, :], func=AF.Ln), reads=[PB[ob]], writes=[t_Rf])
                        kb.op("act", lambda e: e.activation(out=Rf[64:128, :], in_=Rf[64:128, :], func=AF.Exp, scale=-1.0), reads=[t_Rf], writes=[t_Rf])
                        kb.op("dve", lambda e, ob=ob: e.tensor_copy(out=Osb[:], in_=ps[0:64, ob, :]), reads=[PB[ob]], writes=[t_Osb])

                        def norm_b(i=i, hh=hh):
                            o = i % 2
                            kb.op("pe", lambda e: e.matmul(ps[0:64, 5, :], lhsT=selF, rhs=Rf[:], start=True, stop=True),
                                  reads=[t_Rf, t_c], writes=[PB[5]])
                            kb.op("dve", lambda e, o=o: e.tensor_tensor(out=ot[o][:], in0=Osb[:], in1=ps[0:64, 5, :], op=ALU.mult),
                                  reads=[t_Osb, PB[5]], writes=[t_ot[o]])
                            t_wr = Tok()
                            kb.dma("sp", lambda e, sem, o=o, i=i, hh=hh: e.dma_start(out=mv[i // 2][:, hh, (i % 2) * 512:(i % 2) * 512 + 512], in_=ot[o][:]).then_inc(sem, 16),
                                   1, reads=[t_ot[o]], writes=[t_wr])
                            mix_written(i // 2, t_wr)

                        norm_late.append(norm_b)
                    for f_ in norm_late:
                        f_()
                    D.replay(kb, n_ssm)
                    kb.flush()
            mixst.close()

        if do3:
            if mode == "full":
                hstack = ExitStack()
                hT = sb(hstack, "hT3", [128, 8, 2048], F32)
                h1T_d = hT_park
            def reload_h(tts, after):
                if mode in ("s3", "full"):
                    hv = h1T_d.rearrange("(c p) t -> p c t", p=128)
                    for tt in tts:
                        for hf in range(2):
                            kb.dma("sp", lambda e, sem, tt=tt, hf=hf: e.dma_start(out=hT[:, hf * 4:hf * 4 + 4, tt * 512:(tt + 1) * 512],
                                                                                  in_=hv[:, hf * 4:hf * 4 + 4, tt * 512:(tt + 1) * 512]).then_inc(sem, 16), 1,
                                   reads=[t_park] + after, writes=[t_h[dc][tt] for dc in range(hf * 4, hf * 4 + 4)])

            with ExitStack() as st:
                Bs3 = [make_norm_bufs(st), make_norm_bufs(st)]
                idxs = sb(st, "idxs", [128, 16], mybir.dt.uint32)
                bglu = sb(st, "bglu_s", [128, 4], F32)
                wglub = sb(st, "wglub", [128, 4, 512], BF16)
                woutb = sb(st, "woutb", [128, 8, 1024], BF16)
                t_w = Tok()
                selp = [sb(st, "selp%d" % i, [128, 8, 1024], F32) for i in range(2)]
                t_selp = [[Tok(), Tok()] for _ in range(2)]
                ygb2 = [sb(st, "ygb%d" % i, [128, 4, 512], BF16) for i in range(2)]; t_ygb2 = [Tok(), Tok()]
                gt2 = [sb(st, "gate%d" % i, [128, 512], F32) for i in range(2)]; t_gt2 = [Tok(), Tok()]
                mxb2 = [sb(st, "mxb%d" % i, [128, 8, 512], BF16) for i in range(2)]; t_mx2 = [Tok(), Tok()]

                def ldw3(e, sem):
                    e.dma_start(out=idxs[:], in_=idx_d).then_inc(sem, 16)
                    e.dma_start(out=bglu[:], in_=bglu_d).then_inc(sem, 16)

                t_w0 = Tok()
                kb.dma("sp", ldw3, 2, writes=[t_w0])

                def ldw3b(e, sem):
                    e.dma_start(out=wglub[:], in_=wglu_d.rearrange("(c p) f -> p c f", p=128)).then_inc(sem, 16)
                    e.dma_start(out=woutb[:, 0:4, :], in_=wout_d.rearrange("(c p) f -> p c f", p=128)[:, 0:4, :]).then_inc(sem, 16)
                    e.dma_start(out=woutb[:, 4:8, :], in_=wout_d.rearrange("(c p) f -> p c f", p=128)[:, 4:8, :]).then_inc(sem, 16)

                kb.dma("pool", ldw3b, 3, writes=[t_w])
                for pp in range(2):
                    def gat(e, sem, pp=pp):
                        for r in range(4):
                            for two in range(2):
                                col = (pp * 4 + r) * 2 + two
                                e.indirect_dma_start(out=selp[pp][:, two * 4 + r, :], out_offset=None, in_=mix_big,
                                                     in_offset=bass.IndirectOffsetOnAxis(ap=idxs[:, col:col + 1], axis=0)).then_inc(sem, 16)

                    kb.dma("pool", gat, 8, reads=[t_w0] + ([t_mall[k] for k in range(pp, 8, 2)] if mode == "full" else []), writes=t_selp[pp])
                    reload_h([2 * pp, 2 * pp + 1], [])
                def chain(K, tt):
                    pp, hq = tt // 2, tt % 2
                    sel = selp[pp][:, :, hq * 512:(hq + 1) * 512]
                    t_sel = t_selp[pp][hq]
                    ygb, t_ygb = ygb2[tt % 2], t_ygb2[tt % 2]
                    mxb, t_mx = mxb2[tt % 2], t_mx2[tt % 2]
                    B = Bs3[tt % 2]
                    K.op("act", lambda e, ygb=ygb, sel=sel: e.copy(out=ygb[:], in_=sel[:, 4:8, :]), reads=[t_sel], writes=[t_ygb])
                    rstd_from(B, sel[:, 0:4, :], [t_sel], 512.0, K=K)
                    for k in range(4):
                        K.op("dve", lambda e, k=k, mxb=mxb, sel=sel, B=B: e.scalar_tensor_tensor(out=mxb[:, k, :], in0=sel[:, k, :], scalar=gcols[:, 5, k:k + 1],
                                                                                                 in1=B.rt[:], op0=ALU.mult, op1=ALU.mult),
                             reads=[t_sel, B.t_rt, t_c], writes=[t_mx])
                    for cp in range(4):
                        b = cp % 2
                        gt_, t_gt = gt2[cp % 2], t_gt2[cp % 2]
                        K.op("pe", _mm_group(ps[:, b, :], [(wglub[:, c, cp * 128:(cp + 1) * 128], ygb[:, c, :]) for c in range(4)]),
                             reads=[t_w, t_ygb], writes=[PB[b]])
                        K.op("act", lambda e, b=b, cp=cp, gt_=gt_: e.activation(out=gt_[:], in_=ps[:, b, :], func=AF.Sigmoid, bias=bglu[:, cp:cp + 1], scale=1.0),
                             reads=[PB[b], t_w0], writes=[t_gt])
                        K.op("dve", lambda e, cp=cp, sel=sel, gt_=gt_: e.tensor_tensor(out=sel[:, 4 + cp, :], in0=sel[:, 4 + cp, :], in1=gt_[:], op=ALU.mult),
                             reads=[t_gt, t_sel], writes=[t_sel])
                    rstd_from(B, sel[:, 4:8, :], [t_sel], 512.0, K=K)
                    for k in range(4, 8):
                        K.op("dve", lambda e, k=k, mxb=mxb, sel=sel, B=B: e.scalar_tensor_tensor(out=mxb[:, k, :], in0=sel[:, k, :], scalar=gcols[:, 5, k:k + 1],
                                                                                                 in1=B.rt[:], op0=ALU.mult, op1=ALU.mult),
                             reads=[t_sel, B.t_rt, t_c], writes=[t_mx])

                chain(kb, 0)
                for tt in range(4):
                    sl = slice(tt * 512, (tt + 1) * 512)
                    mxb, t_mx = mxb2[tt % 2], t_mx2[tt % 2]
                    Dn = Deferred()
                    if tt < 3:
                        chain(Dn, tt + 1)
                    nq = len(Dn.q)
                    for dp in range(8):
                        b = 2 + (dp % 2)
                        kb.op("pe", _mm_group(ps[:, b, :], [(woutb[:, k, dp * 128:(dp + 1) * 128], mxb[:, k, :]) for k in range(8)]),
                              reads=[t_w, t_mx], writes=[PB[b]])
                        kb.op("dve", lambda e, b=b, dp=dp, sl=sl: e.tensor_tensor(out=hT[:, dp, sl], in0=ps[:, b, :], in1=hT[:, dp, sl], op=ALU.add),
                              reads=[PB[b], t_h[dp][tt]], writes=[t_h[dp][tt]])
                        Dn.replay(kb, (nq * (dp + 1)) // 7)
                    Dn.replay(kb, nq)
                kb.flush()
            with ExitStack() as st:
                B = make_ffn_bufs(st)
                ffn(B, w1b, w3b, w2b_d, 2)
                kb.flush()
            with ExitStack() as st:
                Bs = [make_norm_bufs(st), make_norm_bufs(st)]
                xn4 = [sb(st, "xn3_%d" % i, [128, 8, 512], BF16) for i in range(4)]; t_xn4 = [Tok() for _ in range(4)]
                wgb = sb(st, "wgb", [128, 8, 1024], BF16)
                wpb = sb(st, "wpb", [128, 2, 1024], BF16)
                t_w = Tok()
                pin = [sb(st, "pin%d" % i, [128, 256], F32) for i in range(2)]; t_pin = [Tok(), Tok()]
                pT = sb(st, "pT", [128, 2, 2048], BF16); t_pT = [Tok() for _ in range(4)]
                sg = [sb(st, "sg%d" % i, [128, 512], F32) for i in range(2)]; t_sg = [Tok(), Tok()]
                yT2 = [sb(st, "yT%d" % i, [128, 8, 512], F32) for i in range(2)]; t_yT2 = [Tok(), Tok()]
                otl = [sb(st, "otl%d" % i, [128, 1024], F32) for i in range(2)]; t_otl = [Tok(), Tok()]

                def ldw4(e, sem):
                    wgv = wgate_d.rearrange("(c p) f -> p c f", p=128)
                    e.dma_start(out=wgb[:, 0:4, :], in_=wgv[:, 0:4, :]).then_inc(sem, 16)
                    e.dma_start(out=wgb[:, 4:8, :], in_=wgv[:, 4:8, :]).then_inc(sem, 16)
                    e.dma_start(out=wpb[:], in_=wproj_d.rearrange("(c p) f -> p c f", p=128)).then_inc(sem, 16)

                kb.dma("pool", ldw4, 3, writes=[t_w])
                for tt in range(4):
                    norm_tile(Bs[tt % 2], tt, 3, xn4[tt], 0, t_xn4[tt])
                for tb in range(16):
                    s = tb % 2
                    kb.dma("sp", lambda e, sem, s=s, tb=tb: e.dma_start(out=pin[s][:], in_=p_d[tb * 128:(tb + 1) * 128, :]).then_inc(sem, 16),
                           1, writes=[t_pin[s]])
                    b = 6 + (tb % 2)

                    def trp(e, s=s, b=b):
                        e.transpose(ps[:, b, 0:128], pin[s][:, 0:128], ident)
                        return e.transpose(ps[:, b, 128:256], pin[s][:, 128:256], ident)

                    kb.op("pe", trp, reads=[t_pin[s], t_c], writes=[PB[b]])
                    kb.op("act", lambda e, b=b, tb=tb: e.copy(out=pT[:, :, tb * 128:(tb + 1) * 128],
                                                              in_=ps[:, b, 0:256].rearrange("p (k t) -> p k t", k=2)),
                          reads=[PB[b]], writes=[t_pT[tb // 4]])
                for tt in range(4):
                    sl = slice(tt * 512, (tt + 1) * 512)
                    xn = xn4[tt]
                    for dp in range(8):
                        par = dp % 2
                        bg, bp = par, 2 + par
                        dsl = slice(dp * 128, (dp + 1) * 128)
                        kb.op("pe", _mm_group(ps[:, bg, :], [(wgb[:, k, dsl], xn[:, k, :]) for k in range(8)]),
                              reads=[t_w, t_xn4[tt]], writes=[PB[bg]])
                        kb.op("pe", _mm_group(ps[:, bp, :], [(wpb[:, k, dsl], pT[:, k, sl]) for k in range(2)]),
                              reads=[t_w, t_pT[tt]], writes=[PB[bp]])
                        kb.op("act", lambda e, par=par, bg=bg: e.activation(out=sg[par][:], in_=ps[:, bg, :], func=AF.Sigmoid),
                              reads=[PB[bg]], writes=[t_sg[par]])
                        kb.op("dve", lambda e, par=par, bp=bp: e.tensor_tensor(out=sg[par][:], in0=sg[par][:], in1=ps[:, bp, :], op=ALU.mult),
                              reads=[t_sg[par], PB[bp]], writes=[t_sg[par]])
                        kb.op("pool", lambda e, par=par, dp=dp, sl=sl: e.tensor_tensor(out=hT[:, dp, sl], in0=hT[:, dp, sl], in1=sg[par][:], op=ALU.add),
                              reads=[t_sg[par], t_h[dp][tt]], writes=[t_h[dp][tt]])
                for tt in range(4):
                    yT, t_yT = yT2[tt % 2], t_yT2[tt % 2]
                    norm_tile(Bs[tt % 2], tt, 4, yT, 0, t_yT)
                    for tb in range(4):
                        o = tb % 2
                        for dcg in range(2):
                            b = 4 + dcg

                            def trb(e, tb=tb, dcg=dcg, b=b, yT=yT):
                                for k in range(4):
                                    ins = e.transpose(ps[:, b, k * 128:(k + 1) * 128], yT[:, dcg * 4 + k, tb * 128:(tb + 1) * 128], ident)
                                return ins

                            kb.op("pe", trb, reads=[t_yT, t_c], writes=[PB[b]])
                            if dcg == 0:
                                kb.op("act", lambda e, o=o, b=b: e.copy(out=otl[o][:, 0:512], in_=ps[:, b, :]), reads=[PB[b]], writes=[t_otl[o]])
                            else:
                                kb.op("dve", lambda e, o=o, b=b: e.tensor_copy(out=otl[o][:, 512:1024], in_=ps[:, b, :]), reads=[PB[b]], writes=[t_otl[o]])
                        r0 = (tt * 4 + tb) * 128
                        kb.dma("sp", lambda e, sem, o=o, r0=r0: e.dma_start(out=out_d[r0:r0 + 128, :], in_=otl[o][:]).then_inc(sem, 16),
                               1, reads=[t_otl[o]])
                kb.flush()
        hstack.close()
    return nc


def _gcols(inputs):
    gs = [inputs["g_ffn1"][0], inputs["g_mix"][0], inputs["g_ffn2"][0], inputs["g_ple"][0], inputs["g_final"],
          np.concatenate([inputs["g_attn_out"][0], inputs["g_ssm_out"][0]])]
    out = np.zeros((128, 6, 8), np.float32)
    for i, g in enumerate(gs):
        out[:, i, :] = np.asarray(g, np.float32).reshape(8, 128).T
    return out


def _consts():
    cst = np.zeros((128, 4, 128), np.float32)
    cst[:, 0, :] = np.eye(128, dtype=np.float32)
    cst[:, 1, :] = np.triu(np.ones((128, 128), np.float32))
    cst[:, 2, :] = 1.0
    for m in range(64):
        cst[64 + m, 3, m] = 1.0
    return cst


def _mask_const():
    m = np.zeros((128, 640), np.float32)
    r = np.arange(128)
    m[:, 0:128] = np.where(r[None, :] > r[:, None], -30000.0, 0.0)
    m[:, 128:256] = np.eye(128, dtype=np.float32)
    return m


def _mixer_params(inputs, j):
    w_in = inputs["w_in"][0]
    win = np.zeros((1024, 520), np.float32)
    for hh in range(2):
        c0 = 194 * hh
        hd = 128 * j + 64 * hh
        win[:, c0:c0 + 64] = w_in[:, hd:hd + 64]
        win[:, c0 + 64:c0 + 128] = w_in[:, 512 + hd:512 + hd + 64]
        win[:, c0 + 128:c0 + 192] = w_in[:, 1024 + hd:1024 + hd + 64]
        win[:, c0 + 192:c0 + 194] = w_in[:, 1536 + 2 * j:1536 + 2 * j + 2]
    win[:, 388:516] = w_in[:, 1544 + 128 * j:1544 + 128 * (j + 1)]
    bfbc = np.broadcast_to(inputs["b_f"][0][2 * j:2 * j + 2][None, :], (128, 2)).astype(np.float32).copy()
    a_re, a_im, log_dt = inputs["a_re"][0], inputs["a_im"][0], inputs["log_dt"][0]
    b_re, b_im, c_re, c_im = inputs["b_re"][0], inputs["b_im"][0], inputs["c_re"][0], inputs["c_im"][0]
    colp = np.zeros((128, 3, 4), np.float32)
    rowp = np.zeros((3, 512), np.float32)
    braw = np.zeros((128, 2, 512), np.float32)
    craw = np.zeros((128, 2, 512), np.float32)
    dcol = np.zeros((128, 1), np.float32)
    for gl in range(8):
        g = 8 * j + gl
        gp, half = gl // 2, gl % 2
        st = slice(half * 64, half * 64 + 64)
        colp[st, 0, gp] = a_re[g]; colp[st, 1, gp] = a_im[g]; colp[st, 2, gp] = log_dt[g]
        rs = slice(gp * 128 + half * 64, gp * 128 + half * 64 + 64)
        rowp[0, rs] = a_re[g]; rowp[1, rs] = a_im[g]; rowp[2, rs] = log_dt[g]
        ch = slice(16 * gl, 16 * gl + 16)
        braw[ch, 0, rs] = b_re[g].T
        braw[ch, 1, rs] = b_im[g].T
        cs = slice(gp * 128 + 16 * gl, gp * 128 + 16 * gl + 16)
        craw[st, 0, cs] = c_re[g].T
        craw[st, 1, cs] = c_im[g].T
        dcol[ch, 0] = inputs["d_skip"][0][g]
    return dict(win=win, bfbc=bfbc, colp=colp, rowp=rowp, braw=braw, craw=craw, dcol=dcol, maskc=_mask_const())


def make_in_maps(inputs, mode="full"):
    cst = _consts()
    gc = _gcols(inputs)
    f = lambda k: np.ascontiguousarray(inputs[k][0], dtype=np.float32)
    shared = {"cst": cst, "gcols": gc}
    if mode in ("full", "s1"):
        shared.update({"w1_a": f("w1_a"), "w3_a": f("w3_a"), "w2_a": f("w2_a")})
    if mode in ("full", "s3"):
        shared.update({"w_glu": f("w_glu"), "w_out": f("w_out"), "w1_b": f("w1_b"), "w3_b": f("w3_b"), "w2_b": f("w2_b"),
                       "w_gate": f("w_ple_gate"), "w_proj": f("w_ple_proj"),
                       "bglu": np.ascontiguousarray(inputs["b_glu"][0].reshape(4, 128).T, dtype=np.float32)})
    maps = []
    for c in range(8):
        b, j = c // 4, c % 4
        m = dict(shared)
        if mode in ("full", "s1"):
            m["x"] = np.ascontiguousarray(inputs["x"][b, j * 2048:(j + 1) * 2048, :], dtype=np.float32)
        if mode in ("full", "s2"):
            m.update(_mixer_params(inputs, j))
        if mode in ("full", "s3"):
            idx = np.zeros((128, 16), np.uint32)
            for kk in range(2):
                for r in range(4):
                    for two in range(2):
                        idx[:, (kk * 4 + r) * 2 + two] = (((2 * j + kk) * 4 + r) * 2 + two) * 128 + np.arange(128)
            m["idxT"] = idx
            m["p"] = np.ascontiguousarray(inputs["p"][0, b, j * 2048:(j + 1) * 2048, :], dtype=np.float32)
        maps.append(m)
    return maps


def kernel(**inputs):
    inputs = {k: np.asarray(v) for k, v in inputs.items()}
    nc = build("full")
    maps = make_in_maps(inputs, "full")
    res = run_bass_kernel_spmd(nc, maps, core_ids=list(range(8)))
    out = np.zeros((2, 8192, 1024), np.float32)
    for c in range(8):
        b, j = c // 4, c % 4
        out[b, j * 2048:(j + 1) * 2048, :] = res.results[c]["out"]
    return out
```
